# Optimizing a Trainium2 kernel written in Bass

```python
import jax, jax.numpy as jnp
from jax import lax
import numpy as np

D_MODEL = 1024
BATCH = 32
SEQ = 2048
DEPTH = 2
DEC_BATCH = 16
DEC_SEQ = 16
PAST_LEN = 4096

CHUNK = 64
N_A_LAYERS = DEPTH // 2
N_B_LAYERS = DEPTH - N_A_LAYERS
LRU_WIDTH = D_MODEL
N_LRU_BLOCKS = 16
LRU_BLOCK = LRU_WIDTH // N_LRU_BLOCKS
LRU_CONV = 4
LRU_C = 8.0
N_HEADS = 8
HEAD_DIM = D_MODEL // N_HEADS
D_FF = 3 * D_MODEL
FFN_CONV = 3
Q_BLOCK = 128
EPS = 1e-6

kernel_name = 'hawk_stickbreak_yoco_stream_step'


def rmsnorm(x, g):
    xf = x.astype(jnp.float32)
    y = xf * lax.rsqrt(jnp.mean(xf * xf, axis=-1, keepdims=True) + EPS) * g.astype(jnp.float32)
    return y.astype(x.dtype)


def causal_dwconv(x, prev, w, b):
    width = w.shape[0]
    t_len = x.shape[1]
    xp = jnp.concatenate([prev.astype(x.dtype), x], axis=1)
    y = b
    for k in range(width):
        y = y + w[k] * xp[:, k:k + t_len]
    return y.astype(x.dtype), xp[:, xp.shape[1] - (width - 1):]


def linear_scan(a, b, h0):
    b = b.at[:, 0].add(a[:, 0] * h0)
    def comb(l, r):
        return (l[0] * r[0], r[0] * l[1] + r[1])
    _, h = lax.associative_scan(comb, (a, b), axis=1)
    return h


def rglru_mixer(x, h0, conv_prev, norm, w_in, conv_w, conv_b, wr, br, wi, bi, lam, w_out):
    bsz, t_len, _ = x.shape
    xn = rmsnorm(x, norm)
    gate_in, rec_in = jnp.split(xn @ w_in, 2, axis=-1)
    c, conv_new = causal_dwconv(rec_in, conv_prev, conv_w, conv_b)
    cb = c.reshape(bsz, t_len, N_LRU_BLOCKS, LRU_BLOCK)
    r = jax.nn.sigmoid(jnp.einsum('btnd,nde->btne', cb, wr).reshape(bsz, t_len, LRU_WIDTH) + br).astype(jnp.float32)
    i = jax.nn.sigmoid(jnp.einsum('btnd,nde->btne', cb, wi).reshape(bsz, t_len, LRU_WIDTH) + bi).astype(jnp.float32)
    log_a = LRU_C * r * jax.nn.log_sigmoid(lam.astype(jnp.float32))
    a = jnp.exp(log_a)
    bterm = jnp.sqrt(-jnp.expm1(2.0 * log_a)) * i * c.astype(jnp.float32)
    h = linear_scan(a, bterm, h0.astype(jnp.float32))
    y = (h.astype(x.dtype) * jax.nn.gelu(gate_in)) @ w_out
    return y, h[:, -1].astype(x.dtype), conv_new


def shared_kv(x, kv_norm, w_kv, k_norm):
    bsz, t_len, _ = x.shape
    k, v = jnp.split(rmsnorm(x, kv_norm) @ w_kv, 2, axis=-1)
    k = rmsnorm(k.reshape(bsz, t_len, N_HEADS, HEAD_DIM), k_norm)
    return k, v.reshape(bsz, t_len, N_HEADS, HEAD_DIM)


def sb_attention(q, k, v):
    t_q = q.shape[1]
    off = k.shape[1] - t_q
    scale = HEAD_DIM ** -0.5
    outs = []
    for qs in range(0, t_q, Q_BLOCK):
        qe = min(qs + Q_BLOCK, t_q)
        kl = off + qe
        z = jnp.einsum('bqhd,bshd->bhqs', q[:, qs:qe], k[:, :kl], preferred_element_type=jnp.float32) * scale
        qpos = off + qs + jnp.arange(qe - qs)
        mask = jnp.arange(kl)[None, :] < qpos[:, None]
        log_beta = jax.nn.log_sigmoid(z)
        log_stay = jnp.where(mask, jax.nn.log_sigmoid(-z), 0.0)
        log_after = lax.cumsum(log_stay, axis=3, reverse=True) - log_stay
        w = jnp.where(mask, jnp.exp(log_beta + log_after), 0.0)
        outs.append(jnp.einsum('bhqs,bshd->bqhd', w.astype(v.dtype), v[:, :kl]))
    return jnp.concatenate(outs, axis=1)


def sb_mixer(x, k_all, v_all, norm, wq, q_norm, wo):
    bsz, t_len, _ = x.shape
    q = rmsnorm((rmsnorm(x, norm) @ wq).reshape(bsz, t_len, N_HEADS, HEAD_DIM), q_norm)
    o = sb_attention(q, k_all, v_all)
    return o.reshape(bsz, t_len, N_HEADS * HEAD_DIM) @ wo


def conv_ffn(x, prev, norm, w_up, conv_w, conv_b, w_down):
    g, u = jnp.split(rmsnorm(x, norm) @ w_up, 2, axis=-1)
    gc, new = causal_dwconv(g, prev, conv_w, conv_b)
    return (jax.nn.gelu(gc) * u) @ w_down, new


def trunk(x, lru_h, lru_conv, ffn_conv, cache_k, cache_v,
          a_norm, a_w_in, a_conv_w, a_conv_b, a_wr, a_br, a_wi, a_bi, a_lambda, a_w_out,
          kv_norm, w_kv, k_norm, b_norm, b_wq, q_norm, b_wo,
          f_norm, f_w_up, f_conv_w, f_conv_b, f_w_down, out_norm):
    h_out, c_out, f_out = [], [], []
    for l in range(DEPTH):
        if l < N_A_LAYERS:
            y, h_l, c_l = rglru_mixer(x, lru_h[l], lru_conv[l], a_norm[l], a_w_in[l], a_conv_w[l], a_conv_b[l],
                                      a_wr[l], a_br[l], a_wi[l], a_bi[l], a_lambda[l], a_w_out[l])
            h_out.append(h_l)
            c_out.append(c_l)
        else:
            if l == N_A_LAYERS:
                k_new, v_new = shared_kv(x, kv_norm, w_kv, k_norm)
                k_all = jnp.concatenate([cache_k.astype(x.dtype), k_new], axis=1)
                v_all = jnp.concatenate([cache_v.astype(x.dtype), v_new], axis=1)
            j = l - N_A_LAYERS
            y = sb_mixer(x, k_all, v_all, b_norm[j], b_wq[j], q_norm[j], b_wo[j])
        x = x + y
        y, f_l = conv_ffn(x, ffn_conv[l], f_norm[l], f_w_up[l], f_conv_w[l], f_conv_b[l], f_w_down[l])
        f_out.append(f_l)
        x = x + y
    return rmsnorm(x, out_norm), jnp.stack(h_out), jnp.stack(c_out), jnp.stack(f_out), k_new, v_new


def setup_inputs(seed: int = 0) -> dict:
    key = jax.random.key(seed)
    ks = jax.random.split(key, 40)
    f32 = jnp.float32
    def nrm(k, shape, s):
        return jax.random.normal(k, shape, f32) * s
    na, nb = N_A_LAYERS, N_B_LAYERS
    a0 = jax.random.uniform(ks[14], (na, LRU_WIDTH), f32, 0.9, 0.999)
    root = a0 ** (1.0 / LRU_C)
    a_lambda = jnp.log(root) - jnp.log1p(-root)
    return {
        'x_prompt': nrm(ks[0], (BATCH, SEQ, D_MODEL), 1.0),
        'x_sample': nrm(ks[1], (DEC_BATCH, DEC_SEQ, D_MODEL), 1.0),
        'state_lru_h': nrm(ks[2], (na, DEC_BATCH, LRU_WIDTH), 0.5),
        'state_lru_conv': nrm(ks[3], (na, DEC_BATCH, LRU_CONV - 1, LRU_WIDTH), 1.0),
        'state_ffn_conv': nrm(ks[4], (DEPTH, DEC_BATCH, FFN_CONV - 1, D_FF), 1.0),
        'cache_k': nrm(ks[5], (DEC_BATCH, PAST_LEN, N_HEADS, HEAD_DIM), 1.0),
        'cache_v': nrm(ks[6], (DEC_BATCH, PAST_LEN, N_HEADS, HEAD_DIM), 1.0),
        'a_norm': 1.0 + nrm(ks[7], (na, D_MODEL), 0.01),
        'a_w_in': nrm(ks[8], (na, D_MODEL, 2 * LRU_WIDTH), D_MODEL ** -0.5),
        'a_conv_w': nrm(ks[9], (na, LRU_CONV, LRU_WIDTH), LRU_CONV ** -0.5),
        'a_conv_b': nrm(ks[10], (na, LRU_WIDTH), 0.01),
        'a_wr': nrm(ks[11], (na, N_LRU_BLOCKS, LRU_BLOCK, LRU_BLOCK), LRU_BLOCK ** -0.5),
        'a_br': nrm(ks[12], (na, LRU_WIDTH), 0.01),
        'a_wi': nrm(ks[13], (na, N_LRU_BLOCKS, LRU_BLOCK, LRU_BLOCK), LRU_BLOCK ** -0.5),
        'a_bi': nrm(ks[15], (na, LRU_WIDTH), 0.01),
        'a_lambda': a_lambda,
        'a_w_out': nrm(ks[16], (na, LRU_WIDTH, D_MODEL), LRU_WIDTH ** -0.5),
        'kv_norm': 1.0 + nrm(ks[17], (D_MODEL,), 0.01),
        'w_kv': nrm(ks[18], (D_MODEL, 2 * N_HEADS * HEAD_DIM), D_MODEL ** -0.5),
        'k_norm': 1.0 + nrm(ks[19], (HEAD_DIM,), 0.01),
        'b_norm': 1.0 + nrm(ks[20], (nb, D_MODEL), 0.01),
        'b_wq': nrm(ks[21], (nb, D_MODEL, N_HEADS * HEAD_DIM), D_MODEL ** -0.5),
        'q_norm': 1.0 + nrm(ks[22], (nb, HEAD_DIM), 0.01),
        'b_wo': nrm(ks[23], (nb, N_HEADS * HEAD_DIM, D_MODEL), (N_HEADS * HEAD_DIM) ** -0.5),
        'f_norm': 1.0 + nrm(ks[24], (DEPTH, D_MODEL), 0.01),
        'f_w_up': nrm(ks[25], (DEPTH, D_MODEL, 2 * D_FF), D_MODEL ** -0.5),
        'f_conv_w': nrm(ks[26], (DEPTH, FFN_CONV, D_FF), FFN_CONV ** -0.5),
        'f_conv_b': nrm(ks[27], (DEPTH, D_FF), 0.01),
        'f_w_down': nrm(ks[28], (DEPTH, D_FF, D_MODEL), D_FF ** -0.5),
        'out_norm': 1.0 + nrm(ks[29], (D_MODEL,), 0.01),
    }


def reference(x_prompt, x_sample, state_lru_h, state_lru_conv, state_ffn_conv, cache_k, cache_v,
              a_norm, a_w_in, a_conv_w, a_conv_b, a_wr, a_br, a_wi, a_bi, a_lambda, a_w_out,
              kv_norm, w_kv, k_norm, b_norm, b_wq, q_norm, b_wo,
              f_norm, f_w_up, f_conv_w, f_conv_b, f_w_down, out_norm):
    assert x_sample.shape[1] <= CHUNK
    weights = (a_norm, a_w_in, a_conv_w, a_conv_b, a_wr, a_br, a_wi, a_bi, a_lambda, a_w_out,
               kv_norm, w_kv, k_norm, b_norm, b_wq, q_norm, b_wo,
               f_norm, f_w_up, f_conv_w, f_conv_b, f_w_down, out_norm)
    bp = x_prompt.shape[0]
    dt = x_prompt.dtype
    y_prompt, p_lru_h, p_lru_conv, p_ffn_conv, p_k, p_v = trunk(
        x_prompt,
        jnp.zeros((N_A_LAYERS, bp, LRU_WIDTH), dt),
        jnp.zeros((N_A_LAYERS, bp, LRU_CONV - 1, LRU_WIDTH), dt),
        jnp.zeros((DEPTH, bp, FFN_CONV - 1, D_FF), dt),
        jnp.zeros((bp, 0, N_HEADS, HEAD_DIM), dt),
        jnp.zeros((bp, 0, N_HEADS, HEAD_DIM), dt),
        *weights)
    y_sample, s_lru_h, s_lru_conv, s_ffn_conv, s_k, s_v = trunk(
        x_sample, state_lru_h, state_lru_conv, state_ffn_conv, cache_k, cache_v, *weights)
    return (y_prompt, y_sample, p_lru_h, p_lru_conv, p_ffn_conv, p_k, p_v,
            s_lru_h, s_lru_conv, s_ffn_conv, s_k, s_v)
```

```python
import contextlib
import numpy as np
import concourse.bass as bass
import concourse.mybir as mybir
from concourse.bass_utils import run_bass_kernel_spmd

F32 = mybir.dt.float32
BF16 = mybir.dt.bfloat16
F32R = mybir.dt.float32r
AF = mybir.ActivationFunctionType
ALU = mybir.AluOpType

NCORES = 8
D = 1024
DFF = 3072
SEQ = 2048
PB = 4
SB_ = 2
DSEQ = 16
PAST = 4096
NH = 8
HD = 128
TN = 512
EPS = 1e-6
NSLOT = 4
NEG = -30000.0
DEBUG = False


class Buf:
    __slots__ = ("name", "last_w", "readers", "excl")

    def __init__(self, name, excl=False):
        self.name = name
        self.last_w = None
        self.readers = {}
        self.excl = excl


class Op:
    __slots__ = ("eng", "fn", "deps", "signal", "ev", "idx", "is_dma", "dsem")


class Prog:
    ENGS = ("pe", "act", "dve", "pool", "sp")

    def __init__(self, nc, n_generic_dma_sems=24):
        self.nc = nc
        self.ops = []
        self.dma_sem_state = {}
        self.n_generic = n_generic_dma_sems
        self.generic_i = 0
        self.n_unique = 0
        self.chains = {}

    def buf(self, name, excl=False):
        return Buf(name, excl)

    def _add(self, eng, fn, reads, writes, is_dma=False, dsem=None, after=None):
        op = Op()
        op.eng = eng
        op.fn = fn
        op.signal = False
        op.ev = None
        op.idx = len(self.ops)
        op.is_dma = is_dma
        op.dsem = dsem
        rd = [b for b in reads if not b.excl]
        wr = list(writes) + [b for b in reads if b.excl]
        deps = {}
        for b in rd:
            if b.last_w is not None:
                deps[b.last_w.idx] = b.last_w
        for b in wr:
            if b.last_w is not None:
                deps[b.last_w.idx] = b.last_w
            for r in b.readers.values():
                deps[r.idx] = r
        if after is not None:
            deps[after.idx] = after
        if is_dma:
            st = self.dma_sem_state.setdefault(dsem, [0, None])
            if st[1] is not None:
                deps[st[1].idx] = st[1]
            st[0] += 1
            st[1] = op
            op.ev = (dsem, 16 * st[0])
        dl = []
        for d in deps.values():
            if d is op:
                continue
            if (not d.is_dma) and (not is_dma) and d.eng == "pe" and eng == "pe":
                continue
            if not d.is_dma:
                d.signal = True
            dl.append(d)
        op.deps = dl
        key = ("dma", op.idx) if is_dma else eng
        for b in rd:
            b.readers[key] = op
        for b in wr:
            b.last_w = op
            b.readers = {}
        self.ops.append(op)
        return op

    def op(self, eng, fn, reads=(), writes=()):
        return self._add(eng, fn, reads, writes)

    def dma(self, queue, fn, reads=(), writes=(), sem=None, chain=None):
        after = None
        if chain is not None:
            after = self.chains.get(chain)
        if queue == "pool":
            sem = "u%d" % self.n_unique
            self.n_unique += 1
        elif sem is None:
            sem = "g%d" % self.generic_i
            self.generic_i = (self.generic_i + 1) % self.n_generic
        op = self._add(queue, fn, reads, writes, is_dma=True, dsem=sem, after=after)
        if chain is not None:
            self.chains[chain] = op
        return op

    def emit(self):
        nc = self.nc
        cnt = {e: 0 for e in self.ENGS}
        for op in self.ops:
            if op.is_dma:
                continue
            if op.signal:
                cnt[op.eng] += 1
                op.ev = ("eng_" + op.eng, cnt[op.eng])
        sem_names = ["eng_" + e for e in self.ENGS] + list(self.dma_sem_state.keys())
        with contextlib.ExitStack() as st:
            sems = {}
            for n in sem_names:
                sems[n] = st.enter_context(nc.semaphore("s_" + n))
            block = st.enter_context(nc.Block())
            per_eng = {e: [] for e in self.ENGS}
            for op in self.ops:
                per_eng[op.eng].append(op)
            finals = [(n, 16 * s[0]) for n, s in self.dma_sem_state.items()]

            def run(e, ename):
                known = {}
                for op in per_eng[ename]:
                    for d in op.deps:
                        sn, val = d.ev
                        if known.get(sn, 0) >= val:
                            continue
                        known[sn] = val
                        e.wait_ge(sems[sn], val)
                    ins = op.fn(e)
                    if op.is_dma:
                        ins.then_inc(sems[op.dsem], 16)
                    elif op.signal:
                        ins.then_inc(sems["eng_" + ename], 1)
                if ename == "sp":
                    for n, v in finals:
                        if v > 0:
                            e.wait_ge(sems[n], v)

            @block.tensor
            def _(e):
                run(e, "pe")

            @block.scalar
            def _(e):
                run(e, "act")

            @block.vector
            def _(e):
                run(e, "dve")

            @block.gpsimd
            def _(e):
                run(e, "pool")

            @block.sync
            def _(e):
                run(e, "sp")


class Ring:
    def __init__(self, items):
        self.items = items
        self.i = 0

    def next(self):
        it = self.items[self.i]
        self.i = (self.i + 1) % len(self.items)
        return it


class TileCtx:
    pass


class Builder:
    def __init__(self, nc):
        self.nc = nc
        self.P = Prog(nc)
        self.st = contextlib.ExitStack()
        self.wq = []
        self.w_issued = 0
        self.w_used = 0

    def sb(self, name, shape, dt):
        return self.st.enter_context(self.nc.sbuf_tensor(name, shape, dt))

    def dram_in(self, name, shape):
        return self.nc.dram_tensor(name, list(shape), F32, kind="ExternalInput").ap()

    def dram_out(self, name, shape):
        return self.nc.dram_tensor(name, list(shape), F32, kind="ExternalOutput").ap()

    def tile(self, name, shape, dt, nbuf=None):
        t = self.sb(name, shape, dt)
        if nbuf is None:
            return t, self.P.buf(name)
        return t, [self.P.buf("%s_%d" % (name, i)) for i in range(nbuf)]

    def ring(self, name, shape, dt, n):
        return Ring([(self.sb("%s%d" % (name, i), shape, dt), self.P.buf("%s%d" % (name, i))) for i in range(n)])

    def declare(self):
        nc = self.nc
        d = {}
        d["xp"] = self.dram_in("xp", [PB, SEQ, D])
        d["xs"] = self.dram_in("xs", [SB_ * DSEQ, D])
        d["s_h"] = self.dram_in("s_h", [SB_, D])
        d["s_conv"] = self.dram_in("s_conv", [SB_, 3, D])
        d["s_fconv"] = self.dram_in("s_fconv", [2, SB_, 2, DFF])
        d["ck"] = self.dram_in("ck", [SB_, PAST, D])
        d["cv"] = self.dram_in("cv", [SB_, PAST, D])
        d["a_norm"] = self.dram_in("a_norm", [D])
        d["a_w_in"] = self.dram_in("a_w_in", [D, 2 * D])
        d["a_conv_w"] = self.dram_in("a_conv_w", [4, D])
        d["a_conv_b"] = self.dram_in("a_conv_b", [D])
        d["a_wr"] = self.dram_in("a_wr", [16, 64, 64])
        d["a_br"] = self.dram_in("a_br", [D])
        d["a_wi"] = self.dram_in("a_wi", [16, 64, 64])
        d["a_bi"] = self.dram_in("a_bi", [D])
        d["a_lambda"] = self.dram_in("a_lambda", [D])
        d["a_w_out"] = self.dram_in("a_w_out", [D, D])
        d["kv_norm"] = self.dram_in("kv_norm", [D])
        d["w_kv"] = self.dram_in("w_kv", [D, 2 * D])
        d["k_norm"] = self.dram_in("k_norm", [HD])
        d["b_norm"] = self.dram_in("b_norm", [D])
        d["b_wq"] = self.dram_in("b_wq", [D, D])
        d["q_norm"] = self.dram_in("q_norm", [HD])
        d["b_wo"] = self.dram_in("b_wo", [D, D])
        d["f_norm"] = self.dram_in("f_norm", [2, D])
        d["f_w_up"] = self.dram_in("f_w_up", [2, D, 2 * DFF])
        d["f_conv_w"] = self.dram_in("f_conv_w", [2, 3, DFF])
        d["f_conv_b"] = self.dram_in("f_conv_b", [2, DFF])
        d["f_w_down"] = self.dram_in("f_w_down", [2, DFF, D])
        d["out_norm"] = self.dram_in("out_norm", [D])
        d["y_p"] = self.dram_out("y_p", [PB, SEQ, D])
        d["y_s"] = self.dram_out("y_s", [SB_ * DSEQ, D])
        d["p_h"] = self.dram_out("p_h", [PB, D])
        d["p_conv"] = self.dram_out("p_conv", [PB, 3, D])
        d["p_fconv"] = self.dram_out("p_fconv", [2, PB, 2, DFF])
        d["p_k"] = self.dram_out("p_k", [PB, SEQ, D])
        d["p_v"] = self.dram_out("p_v", [PB, SEQ, D])
        d["o_h"] = self.dram_out("o_h", [SB_, D])
        d["o_conv"] = self.dram_out("o_conv", [SB_, 3, D])
        d["o_fconv"] = self.dram_out("o_fconv", [2, SB_, 2, DFF])
        d["o_k"] = self.dram_out("o_k", [SB_ * DSEQ, D])
        d["o_v"] = self.dram_out("o_v", [SB_ * DSEQ, D])
        def scratch(name, shape):
            return nc.dram_tensor(name, list(shape), BF16, kind="Internal").ap()
        d["w_in_b"] = scratch("w_in_b", [D, 2 * D])
        d["w_out_b"] = scratch("w_out_b", [D, D])
        d["w_kv_b"] = scratch("w_kv_b", [D, 2 * D])
        d["wq_b"] = scratch("wq_b", [D, D])
        d["wo_b"] = scratch("wo_b", [D, D])
        d["w_up_b"] = [scratch("w_up_b%d" % l, [D, 2 * DFF]) for l in range(2)]
        d["w_dn_b"] = [scratch("w_dn_b%d" % l, [DFF, D]) for l in range(2)]
        self.d = d

    def setup(self):
        P, d = self.P, self.d
        sb = self.sb
        self.pairs = [self.st.enter_context(self.nc.psum_tensor("pair%d" % i, [128, 1024], F32)) for i in range(4)]
        self.bank_bufs = [P.buf("bank%d" % i, excl=True) for i in range(8)]
        self.banks = [(self.pairs[i // 2][:, (i % 2) * 512:(i % 2 + 1) * 512], self.bank_bufs[i]) for i in range(8)]
        self.bankb = (self.pairs[3][:, 512:1024].bitcast(BF16), self.bank_bufs[7])

        self.x, self.xb = self.tile("x", [128, 8, TN], F32, 8)
        self.xn, self.xnb = self.tile("xn", [128, 8, TN], BF16, 8)
        self.bufA, self.bufAb = self.tile("bufA", [128, 8, TN], BF16, 8)
        self.hmid, self.hmidb = self.tile("hmid", [128, 24, TN], BF16, 24)
        self.KT, self.KTb = self.tile("KT", [128, NH, SEQ], BF16, SEQ // 128)
        self.Vb, self.Vbb = self.tile("Vb", [128, SEQ // 128, D], BF16, SEQ // 128)
        self.RS, self.RSb = self.tile("RS", [128, 128], F32R)
        self.wslots = [(sb("wslot%d" % i, [128, 8, 512], BF16), P.buf("wslot%d" % i)) for i in range(NSLOT)]
        self.rowbufs = self.ring("rowbuf", [128, D], F32, 3)
        self.xin = self.rowbufs
        self.tf = self.ring("tf", [128, TN + 4], F32, 6)
        self.tr = self.ring("tr", [128, 2 * TN], F32R, 3)
        self.tb = self.ring("tb", [128, TN], BF16, 3)
        self.xpre, self.xpreb = self.tile("xpre", [128, D], F32)
        self.small = self.ring("small", [128, 64], F32, 4)
        self.rhist, self.rhistb = self.tile("rhist", [128, 8, 2, 3], F32, 8)
        self.hstate, self.hstateb = self.tile("hstate", [128, 8, 2], F32, 8)
        self.fhist = []
        for l in range(2):
            self.fhist.append(self.tile("fhist%d" % l, [128, 24, 2, 2], F32, 24))

        self.ident_f, self.ident_fb = self.tile("ident_f", [128, 128], F32)
        self.ident_b, self.ident_bb = self.tile("ident_b", [128, 128], BF16)
        self.ones_dm, self.ones_dmb = self.tile("ones_dm", [128, 128], BF16)
        self.ones_hd, self.ones_hdb = self.tile("ones_hd", [128, 128], BF16)
        self.negtri, self.negtrib = self.tile("negtri", [128, 128], F32R)
        self.negones, self.negonesb = self.tile("negones", [128, 128], F32R)
        self.M01, self.M01b = self.tile("M01", [128, 128], F32)
        self.MB, self.MBb = self.tile("MB", [128, 128], BF16)
        self.M01s, self.M01sb = self.tile("M01s", [128, 128], F32)
        self.MBs, self.MBsb = self.tile("MBs", [128, 128], BF16)
        self.gcol, self.gcolb = self.tile("gcol", [128, 6, 8], F32)
        self.acw, self.acwb = self.tile("acw", [128, 8, 4], F32)
        self.acols, self.acolsb = self.tile("acols", [128, 8, 8], F32)
        self.fcw, self.fcwb = self.tile("fcw", [128, 2, 24, 3], F32)
        self.fcb, self.fcbb = self.tile("fcb", [128, 2, 24], F32)
        self.knbc, self.knbcb = self.tile("knbc", [128, 128], F32)
        self.qg, self.qgb = self.tile("qg", [128, 2], F32)
        self.WRI, self.WRIb = self.tile("WRI", [128, 16, 128], BF16)
        self.lnhalf, self.lnhalfb = self.tile("lnhalf", [128, 1], F32)
        self.mhalf, self.mhalfb = self.tile("mhalf", [128, 8], F32)

        def pool(fn, reads=(), writes=()):
            P.op("pool", fn, reads, writes)

        idf, idb = self.ident_f, self.ident_b
        pool(lambda e: e.memset(idf[:], 1.0), writes=[self.ident_fb])
        pool(lambda e: e.affine_select(out=idf[:], in_=idf[:], pattern=[[-1, 128]], compare_op=ALU.is_equal,
                                       fill=0.0, base=0, channel_multiplier=1),
             reads=[self.ident_fb], writes=[self.ident_fb])
        pool(lambda e: e.tensor_copy(out=idb[:], in_=idf[:]), reads=[self.ident_fb], writes=[self.ident_bb])
        pool(lambda e: e.memset(self.ones_dm[:], 1.0 / D), writes=[self.ones_dmb])
        pool(lambda e: e.memset(self.ones_hd[:], 1.0 / HD), writes=[self.ones_hdb])
        self.tmpc, self.tmpcb = self.tile("tmpc", [128, 128], F32)
        tmpc = self.tmpc
        nt = self.negtri
        pool(lambda e: e.memset(tmpc[:], -1.0), writes=[self.tmpcb])
        pool(lambda e: e.affine_select(out=self.negones[:], in_=tmpc[:], pattern=[[0, 128]], compare_op=ALU.is_ge,
                                       fill=0.0, base=1, channel_multiplier=0),
             reads=[self.tmpcb], writes=[self.negonesb])
        pool(lambda e: e.affine_select(out=nt[:], in_=tmpc[:], pattern=[[-1, 128]], compare_op=ALU.is_ge,
                                       fill=0.0, base=0, channel_multiplier=1),
             reads=[self.tmpcb], writes=[self.negtrib])
        m01, mb = self.M01, self.MB
        pool(lambda e: e.memset(m01[:], 1.0), writes=[self.M01b])
        pool(lambda e: e.affine_select(out=m01[:], in_=m01[:], pattern=[[1, 128]], compare_op=ALU.is_gt,
                                       fill=0.0, base=0, channel_multiplier=-1),
             reads=[self.M01b], writes=[self.M01b])
        pool(lambda e: e.memset(mb[:], 0.0), writes=[self.MBb])
        pool(lambda e: e.affine_select(out=mb[:], in_=mb[:], pattern=[[1, 128]], compare_op=ALU.is_gt,
                                       fill=NEG, base=0, channel_multiplier=-1),
             reads=[self.MBb], writes=[self.MBb])
        m01s, mbs = self.M01s, self.MBs
        pool(lambda e: e.memset(m01s[:], 1.0), writes=[self.M01sb])
        pool(lambda e: e.affine_select(out=m01s[:], in_=m01s[:], pattern=[[0, NH], [1, DSEQ]], compare_op=ALU.is_gt,
                                       fill=0.0, base=0, channel_multiplier=-1),
             reads=[self.M01sb], writes=[self.M01sb])
        pool(lambda e: e.memset(mbs[:], 0.0), writes=[self.MBsb])
        pool(lambda e: e.affine_select(out=mbs[:], in_=mbs[:], pattern=[[0, NH], [1, DSEQ]], compare_op=ALU.is_gt,
                                       fill=NEG, base=0, channel_multiplier=-1),
             reads=[self.MBsb], writes=[self.MBsb])
        pool(lambda e: e.memset(self.WRI[:], 0.0), writes=[self.WRIb])
        pool(lambda e: e.memset(self.lnhalf[:], float(np.log(0.5))), writes=[self.lnhalfb])
        pool(lambda e: e.memset(self.mhalf[:], -0.5), writes=[self.mhalfb])

        self.wsb = {}

        def cast(dst, src, name, sem):
            b = P.buf("scr_" + name)
            dv = dst.rearrange("r (a b) -> r a b", b=1024)
            sv = src.rearrange("r (a b) -> r a b", b=1024)
            P.dma("pool", lambda e: e.dma_start(out=dv, in_=sv), writes=[b], chain=sem)
            self.wsb[name] = b

        self._cast = cast
        self.cast_plan = {
            "lru": [(d["w_dn_b"][0], d["f_w_down"][0], "w_dn0", "castA"), (d["w_kv_b"], d["w_kv"], "w_kv", "castB")],
            "ffn0": [(d["wq_b"], d["b_wq"], "wq", "castA"), (d["wo_b"], d["b_wo"], "wo", "castB")],
            "kv": [(d["w_up_b"][1], d["f_w_up"][1], "w_up1", "castC")],
            "q": [(d["w_dn_b"][1], d["f_w_down"][1], "w_dn1", "castA")],
        }

        cast(d["w_in_b"], d["a_w_in"], "w_in", "castA")
        def small_load(dst_ap, src_ap, wbuf):
            P.dma("sp", lambda e: e.dma_start(out=dst_ap, in_=src_ap, allow_slow_non_contiguous=True), writes=[wbuf])

        norms = [d["a_norm"], d["f_norm"][0], d["kv_norm"], d["b_norm"], d["f_norm"][1], d["out_norm"]]
        for i, nv in enumerate(norms):
            small_load(self.gcol[:, i, :], nv.rearrange("(j p) -> p j", p=128), self.gcolb)
        for k in range(4):
            small_load(self.acw[:, :, k], d["a_conv_w"][k].rearrange("(j p) -> p j", p=128), self.acwb)
        for i, nm in enumerate(["a_conv_b", "a_br", "a_bi", "a_lambda"]):
            small_load(self.acols[:, :, i], d[nm].rearrange("(j p) -> p j", p=128), self.acolsb)
        for l in range(2):
            for k in range(3):
                small_load(self.fcw[:, l, :, k], d["f_conv_w"][l, k].rearrange("(j p) -> p j", p=128), self.fcwb)
            small_load(self.fcb[:, l, :], d["f_conv_b"][l].rearrange("(j p) -> p j", p=128), self.fcbb)
        P.dma("sp", lambda e: e.dma_start(out=self.knbc[:], in_=d["k_norm"].partition_broadcast(128)), writes=[self.knbcb])
        small_load(self.qg[:, 0:1], d["q_norm"].rearrange("(p o) -> p o", o=1), self.qgb)
        for j in range(8):
            for half in range(2):
                n = 2 * j + half
                lo = 64 * half
                P.dma("pool", lambda e, j=j, n=n, lo=lo: e.dma_start(out=self.WRI[lo:lo + 64, j, lo:lo + 64], in_=d["a_wr"][n]),
                      writes=[self.WRIb], chain="casts%d" % (n % 4))
                P.dma("pool", lambda e, j=j, n=n, lo=lo: e.dma_start(out=self.WRI[lo:lo + 64, 8 + j, lo:lo + 64], in_=d["a_wi"][n]),
                      writes=[self.WRIb], chain="casts%d" % (n % 4))
        cast(d["w_out_b"], d["a_w_out"], "w_out", "castB")
        cast(d["w_up_b"][0], d["f_w_up"][0], "w_up0", "castC")

        ac = self.acols
        act = lambda fn, reads=(), writes=(): P.op("act", fn, reads, writes)
        act(lambda e: e.mul(out=ac[:, :, 4], in_=ac[:, :, 1], mul=0.5), reads=[self.acolsb], writes=[self.acolsb])
        act(lambda e: e.mul(out=ac[:, :, 5], in_=ac[:, :, 2], mul=0.5), reads=[self.acolsb], writes=[self.acolsb])
        act(lambda e: e.activation(out=ac[:, :, 6], in_=ac[:, :, 3], func=AF.Exp, scale=-1.0), reads=[self.acolsb], writes=[self.acolsb])
        act(lambda e: e.activation(out=ac[:, :, 7], in_=ac[:, :, 6], func=AF.Ln, bias=1.0), reads=[self.acolsb], writes=[self.acolsb])
        act(lambda e: e.mul(out=ac[:, :, 6], in_=ac[:, :, 7], mul=-4.0), reads=[self.acolsb], writes=[self.acolsb])
        act(lambda e: e.mul(out=ac[:, :, 7], in_=ac[:, :, 7], mul=-8.0), reads=[self.acolsb], writes=[self.acolsb])
        act(lambda e: e.mul(out=self.qg[:, 1:2], in_=self.qg[:, 0:1], mul=float(HD) ** -0.5), reads=[self.qgb], writes=[self.qgb])

    def plan_weights(self, ntiles):
        d = self.d
        reqs = []

        def add(src, rows0, col0, key):
            reqs.append((src, rows0, col0, key))

        for _ in range(ntiles):
            for g in range(2):
                add(d["w_in_b"], 0, D + 512 * g, "w_in")
                add(d["w_in_b"], 0, 512 * g, "w_in")
            for mg in range(2):
                add(d["w_out_b"], 0, 512 * mg, "w_out")
            self._plan_ffn(add, 0)
            for g in range(2):
                add(d["w_kv_b"], 0, 512 * g, "w_kv")
            for g in range(2):
                add(d["w_kv_b"], 0, D + 512 * g, "w_kv")
            for g in range(2):
                add(d["wq_b"], 0, 512 * g, "wq")
            for g in range(2):
                add(d["wo_b"], 0, 512 * g, "wo")
            self._plan_ffn(add, 1)
        self.wq = reqs

    def _plan_ffn(self, add, l):
        d = self.d
        for g in range(6):
            add(d["w_up_b"][l], 0, 512 * g, "w_up%d" % l)
            add(d["w_up_b"][l], 0, DFF + 512 * g, "w_up%d" % l)
        for mg in range(2):
            for ks in range(3):
                add(d["w_dn_b"][l], 1024 * ks, 512 * mg, "w_dn%d" % l)

    def _issue_w(self, upto):
        P = self.P
        while self.w_issued < min(upto, len(self.wq)):
            i = self.w_issued
            src, r0, c0, key = self.wq[i]
            assert key in self.wsb, key
            slot, sbuf_ = self.wslots[i % NSLOT]
            view = src[r0:r0 + 1024, c0:c0 + 512].rearrange("(kc p) c -> p kc c", p=128)
            P.dma("sp", lambda e, slot=slot, view=view: e.dma_start(out=slot[:], in_=view),
                  reads=[self.wsb[key]], writes=[sbuf_], sem="wslot%d" % (i % NSLOT))
            self.w_issued += 1

    def next_w(self):
        i = self.w_used
        if self.w_issued == 0:
            self._issue_w(NSLOT)
        assert i < self.w_issued or i >= len(self.wq), (i, self.w_issued)
        self.w_used += 1
        return self.wslots[i % NSLOT]

    def done_w(self, n=1):
        self.w_released = getattr(self, "w_released", 0) + n
        self._issue_w(self.w_released + NSLOT)

    def hf(self, k):
        ap = self.hmid[:, 2 * k:2 * k + 2, :].rearrange("p a b -> p (a b)").bitcast(F32)
        return ap, [self.hmidb[2 * k], self.hmidb[2 * k + 1]]

    def hb(self, k):
        ap = self.hmid[:, 2 * k:2 * k + 2, :].rearrange("p a b -> p (a b)")
        return ap, [self.hmidb[2 * k], self.hmidb[2 * k + 1]]

    def bank(self, i):
        return self.banks[i]

    def hq(self, k):
        ap = self.hmid[:, 4 * k:4 * k + 4, :].rearrange("p a b -> p (a b)").bitcast(F32)
        return ap, [self.hmidb[4 * k + i] for i in range(4)]

    def rmsnorm_to_xn(self, c, gi, inplace=False):
        P = self.P
        N = c.N
        st_bank, st_b = self.bank(6)
        for fc in range(8):
            sq, sqb = self.tb.next()
            P.op("act", lambda e, fc=fc, sq=sq: e.activation(out=sq[:, 0:N], in_=self.x[:, fc, 0:N], func=AF.Square),
                 reads=[self.xb[fc]], writes=[sqb])
            P.op("pe", lambda e, fc=fc, sq=sq: e.matmul(st_bank[:, 0:N], lhsT=self.ones_dm[:], rhs=sq[:, 0:N],
                                                        start=(fc == 0), stop=(fc == 7)),
                 reads=[self.ones_dmb, sqb], writes=[st_b])
        lnv, lnvb = self.tf.next()
        rstd, rstdb = self.tf.next()
        P.op("act", lambda e: e.activation(out=lnv[:, 0:N], in_=st_bank[:, 0:N], func=AF.Ln, bias=EPS), reads=[st_b], writes=[lnvb])
        P.op("act", lambda e: e.activation(out=rstd[:, 0:N], in_=lnv[:, 0:N], func=AF.Exp, scale=-0.5), reads=[lnvb], writes=[rstdb])
        for fc in range(8):
            if not inplace:
                P.op("dve", lambda e, fc=fc: e.scalar_tensor_tensor(out=self.xn[:, fc, 0:N], in0=self.x[:, fc, 0:N],
                                                                    scalar=self.gcol[:, gi, fc:fc + 1], in1=rstd[:, 0:N],
                                                                    op0=ALU.mult, op1=ALU.mult),
                     reads=[self.xb[fc], self.gcolb, rstdb], writes=[self.xnb[fc]])
            else:
                P.op("dve", lambda e, fc=fc: e.scalar_tensor_tensor(out=self.x[:, fc, 0:N], in0=self.x[:, fc, 0:N],
                                                                    scalar=self.gcol[:, gi, fc:fc + 1], in1=rstd[:, 0:N],
                                                                    op0=ALU.mult, op1=ALU.mult),
                     reads=[self.xb[fc], self.gcolb, rstdb], writes=[self.xb[fc]])

    def x_to_fm(self, c):
        P = self.P
        for gi, (c0, nt, src) in enumerate(c.tgs):
            if gi == 0 and getattr(c, "prefetched", False):
                xi, xib = self.xpre, self.xpreb
            else:
                xi, xib = self.rowbufs.next()
                P.dma("act", lambda e, xi=xi, src=src, nt=nt: e.dma_start(out=xi[0:nt, :], in_=src), writes=[xib])
            for half in range(2):
                bt, bb = self.bank((2 * gi + half) % 4)
                for q in range(4):
                    fc = 4 * half + q
                    P.op("pe", lambda e, xi=xi, nt=nt, fc=fc, q=q, bt=bt: e.transpose(out=bt[:, q * 128:q * 128 + nt], in_=xi[0:nt, fc * 128:(fc + 1) * 128],
                                                                                        identity=self.ident_f[0:nt, 0:nt]),
                         reads=[xib, self.ident_fb], writes=[bb])
                src3 = bt[:, :].rearrange("p (q t) -> p q t", q=4)[:, :, 0:nt]
                dst3 = self.x[:, 4 * half:4 * half + 4, c0:c0 + nt]
                wb = [self.xb[4 * half + q] for q in range(4)]
                if half == 0:
                    P.op("act", lambda e, src3=src3, dst3=dst3: e.copy(out=dst3, in_=src3), reads=[bb], writes=wb)
                else:
                    P.op("dve", lambda e, src3=src3, dst3=dst3: e.tensor_copy(out=dst3, in_=src3), reads=[bb], writes=wb)

    def lru_mixer(self, c):
        P = self.P
        N, S, L = c.N, c.nseg, c.L
        ac = self.acols
        wslot = {}
        stA = {}
        stB = {}

        def phaseA(j):
            g, jj = divmod(j, 4)
            if jj == 0:
                wslot["rec"] = self.next_w()
                wslot["gate"] = self.next_w()
            (wr_t, wr_b), (wg_t, wg_b) = wslot["rec"], wslot["gate"]
            Rt, Rb = self.bank((0, 2)[j % 2])
            Gt, Gb = self.bank((1, 3)[j % 2])
            for kc in range(8):
                P.op("pe", lambda e, kc=kc, jj=jj, wr_t=wr_t, Rt=Rt: e.matmul(Rt[:, 0:N], lhsT=wr_t[:, kc, jj * 128:(jj + 1) * 128], rhs=self.xn[:, kc, 0:N],
                                                                              start=(kc == 0), stop=(kc == 7)),
                     reads=[wr_b, self.xnb[kc]], writes=[Rb])
            for kc in range(8):
                P.op("pe", lambda e, kc=kc, jj=jj, wg_t=wg_t, Gt=Gt: e.matmul(Gt[:, 0:N], lhsT=wg_t[:, kc, jj * 128:(jj + 1) * 128], rhs=self.xn[:, kc, 0:N],
                                                                              start=(kc == 0), stop=(kc == 7)),
                     reads=[wg_b, self.xnb[kc]], writes=[Gb])
            if jj == 3:
                self.done_w(2)
            rb_t, rb_b = self.rowbufs.next()
            rb3 = rb_t[:, 0:S * (L + 3)].rearrange("p (s l) -> p s l", s=S)
            P.op("pool", lambda e, j=j, rb3=rb3: e.tensor_copy(out=rb3[:, :, 0:3], in_=self.rhist[:, j, 0:S, :]),
                 reads=[self.rhistb[j]], writes=[rb_b])
            R3 = Rt[:, 0:N].rearrange("p (s l) -> p s l", s=S)
            P.op("act", lambda e, rb3=rb3, R3=R3: e.copy(out=rb3[:, :, 3:3 + L], in_=R3), reads=[Rb], writes=[rb_b])
            c0_t, c0_b = self.hf(2 * (j % 4))
            gg_t, gg_b = self.hf(2 * (j % 4) + 1)
            c03 = c0_t[:, 0:N].rearrange("p (s l) -> p s l", s=S)
            P.op("pool", lambda e, j=j, rb3=rb3, c03=c03: e.tensor_scalar(out=c03, in0=rb3[:, :, 3:3 + L], scalar1=self.acw[:, j, 3:4], scalar2=ac[:, j, 0:1],
                                                                          op0=ALU.mult, op1=ALU.add),
                 reads=[rb_b, self.acwb, self.acolsb], writes=c0_b)
            for k in (2, 1, 0):
                P.op("dve", lambda e, j=j, k=k, rb3=rb3, c03=c03: e.scalar_tensor_tensor(out=c03, in0=rb3[:, :, k:k + L], scalar=self.acw[:, j, k:k + 1],
                                                                                          in1=c03, op0=ALU.mult, op1=ALU.add),
                     reads=[rb_b, self.acwb] + c0_b, writes=c0_b)
            P.op("pool", lambda e, j=j, rb3=rb3: e.tensor_copy(out=self.rhist[:, j, 0:S, :], in_=rb3[:, :, L:L + 3]),
                 reads=[rb_b], writes=[self.rhistb[j]])
            cb_t, cb_b = self.tb.next()
            P.op("pool", lambda e, cb_t=cb_t, c0_t=c0_t: e.tensor_copy(out=cb_t[:, 0:N], in_=c0_t[:, 0:N]), reads=c0_b, writes=[cb_b])
            P.op("act", lambda e, gg_t=gg_t, Gt=Gt: e.activation(out=gg_t[:, 0:N], in_=Gt[:, 0:N], func=AF.Gelu_apprx_tanh), reads=[Gb], writes=gg_b)
            stA[j] = (c0_t, c0_b, gg_t, gg_b, cb_t, cb_b)

        def phaseB1(j):
            c0_t, c0_b, gg_t, gg_b, cb_t, cb_b = stA[j]
            rp_t, rp_b = self.bank((4, 6)[j % 2])
            ip_t, ip_b = self.bank((5, 7)[j % 2])
            P.op("pe", lambda e, j=j, cb_t=cb_t, rp_t=rp_t: e.matmul(rp_t[:, 0:N], lhsT=self.WRI[:, j, :], rhs=cb_t[:, 0:N], start=True, stop=True),
                 reads=[self.WRIb, cb_b], writes=[rp_b])
            P.op("pe", lambda e, j=j, cb_t=cb_t, ip_t=ip_t: e.matmul(ip_t[:, 0:N], lhsT=self.WRI[:, 8 + j, :], rhs=cb_t[:, 0:N], start=True, stop=True),
                 reads=[self.WRIb, cb_b], writes=[ip_b])
            r_t, r_b = self.hf(8 + 2 * (j % 2))
            i_t, i_b = self.hf(9 + 2 * (j % 2))
            P.op("act", lambda e, j=j, r_t=r_t, rp_t=rp_t: e.activation(out=r_t[:, 0:N], in_=rp_t[:, 0:N], func=AF.Tanh, scale=0.5, bias=ac[:, j, 4:5]),
                 reads=[rp_b, self.acolsb], writes=r_b)
            P.op("act", lambda e, j=j, i_t=i_t, ip_t=ip_t: e.activation(out=i_t[:, 0:N], in_=ip_t[:, 0:N], func=AF.Tanh, scale=0.5, bias=ac[:, j, 5:6]),
                 reads=[ip_b, self.acolsb], writes=i_b)
            stB[j] = (r_t, r_b, i_t, i_b)

        def phaseB2(j):
            c0_t, c0_b, gg_t, gg_b, cb_t, cb_b = stA.pop(j)
            r_t, r_b, i_t, i_b = stB.pop(j)
            a_t, a_b = self.tf.next()
            s_t, s_b = self.tf.next()
            P.op("act", lambda e, j=j, a_t=a_t, r_t=r_t: e.activation(out=a_t[:, 0:N], in_=r_t[:, 0:N], func=AF.Exp, scale=ac[:, j, 6:7], bias=ac[:, j, 6:7]),
                 reads=r_b + [self.acolsb], writes=[a_b])
            P.op("act", lambda e, j=j, s_t=s_t, r_t=r_t: e.activation(out=s_t[:, 0:N], in_=r_t[:, 0:N], func=AF.Exp, scale=ac[:, j, 7:8], bias=ac[:, j, 7:8]),
                 reads=r_b + [self.acolsb], writes=[s_b])
            P.op("act", lambda e, s_t=s_t: e.activation(out=s_t[:, 0:N], in_=s_t[:, 0:N], func=AF.Ln, scale=-1.0, bias=1.0), reads=[s_b], writes=[s_b])
            P.op("act", lambda e, s_t=s_t: e.activation(out=s_t[:, 0:N], in_=s_t[:, 0:N], func=AF.Exp, scale=0.5, bias=self.lnhalf[:, 0:1]), reads=[s_b, self.lnhalfb], writes=[s_b])
            P.op("dve", lambda e, i_t=i_t, c0_t=c0_t: e.scalar_tensor_tensor(out=i_t[:, 0:N], in0=i_t[:, 0:N], scalar=1.0, in1=c0_t[:, 0:N], op0=ALU.add, op1=ALU.mult),
                 reads=i_b + c0_b, writes=i_b)
            P.op("dve", lambda e, i_t=i_t, s_t=s_t: e.tensor_tensor(out=i_t[:, 0:N], in0=i_t[:, 0:N], in1=s_t[:, 0:N], op=ALU.mult),
                 reads=i_b + [s_b], writes=i_b)
            h_t, h_b = self.tf.next()
            for s in range(S):
                P.op("dve", lambda e, j=j, s=s, h_t=h_t, a_t=a_t, i_t=i_t: e.tensor_tensor_scan(out=h_t[:, s * L:(s + 1) * L], data0=a_t[:, s * L:(s + 1) * L],
                                                                                                data1=i_t[:, s * L:(s + 1) * L], initial=self.hstate[:, j, s:s + 1],
                                                                                                op0=ALU.mult, op1=ALU.add),
                     reads=[a_b, self.hstateb[j]] + i_b, writes=[h_b])
            h3 = h_t[:, 0:N].rearrange("p (s l) -> p s l", s=S)
            P.op("dve", lambda e, j=j, h3=h3: e.tensor_copy(out=self.hstate[:, j, 0:S], in_=h3[:, :, L - 1]),
                 reads=[h_b], writes=[self.hstateb[j]])
            P.op("dve", lambda e, j=j, h_t=h_t, gg_t=gg_t: e.tensor_tensor(out=self.bufA[:, j, 0:N], in0=h_t[:, 0:N], in1=gg_t[:, 0:N], op=ALU.mult),
                 reads=[h_b] + gg_b, writes=[self.bufAb[j]])

        phaseA(0)
        phaseA(1)
        for j in range(0, 8, 2):
            if j + 2 < 8:
                phaseA(j + 2)
            phaseB1(j)
            if j + 3 < 8:
                phaseA(j + 3)
            phaseB1(j + 1)
            phaseB2(j)
            phaseB2(j + 1)
        self.proj_residual(c, self.bufA, self.bufAb)

    def proj_residual(self, c, src, srcb):
        P = self.P
        N = c.N
        for mg in range(2):
            w_t, w_b = self.next_w()
            bks = [self.bank(mm) for mm in range(4)]
            for kc in range(8):
                for mm in range(4):
                    bt, bb = bks[mm]
                    P.op("pe", lambda e, kc=kc, mm=mm, w_t=w_t, bt=bt: e.matmul(bt[:, 0:N], lhsT=w_t[:, kc, mm * 128:(mm + 1) * 128], rhs=src[:, kc, 0:N],
                                                                                start=(kc == 0), stop=(kc == 7)),
                         reads=[w_b, srcb[kc]], writes=[bb])
            for mm in range(4):
                m = 4 * mg + mm
                bt, bb = bks[mm]
                P.op("dve", lambda e, m=m, bt=bt: e.tensor_tensor(out=self.x[:, m, 0:N], in0=bt[:, 0:N], in1=self.x[:, m, 0:N], op=ALU.add),
                     reads=[bb, self.xb[m]], writes=[self.xb[m]])
            self.done_w(1)

    def conv_ffn(self, c, l):
        P = self.P
        N, S, L = c.N, c.nseg, c.L
        self.rmsnorm_to_xn(c, 1 if l == 0 else 4)
        fh_t, fh_b = self.fhist[l]
        bi = 0
        for g in range(6):
            wg_t, wg_b = self.next_w()
            wu_t, wu_b = self.next_w()
            for jj in range(4):
                j = 4 * g + jj
                Gt, Gb = self.bank(bi % 6)
                Ut, Ub = self.bank((bi + 1) % 6)
                bi += 2
                for kc in range(8):
                    P.op("pe", lambda e, kc=kc, jj=jj, wg_t=wg_t, Gt=Gt: e.matmul(Gt[:, 0:N], lhsT=wg_t[:, kc, jj * 128:(jj + 1) * 128], rhs=self.xn[:, kc, 0:N],
                                                                                  start=(kc == 0), stop=(kc == 7)),
                         reads=[wg_b, self.xnb[kc]], writes=[Gb])
                for kc in range(8):
                    P.op("pe", lambda e, kc=kc, jj=jj, wu_t=wu_t, Ut=Ut: e.matmul(Ut[:, 0:N], lhsT=wu_t[:, kc, jj * 128:(jj + 1) * 128], rhs=self.xn[:, kc, 0:N],
                                                                                  start=(kc == 0), stop=(kc == 7)),
                         reads=[wu_b, self.xnb[kc]], writes=[Ub])
                gb_t, gb_b = self.tf.next()
                gb3 = gb_t[:, 0:S * (L + 2)].rearrange("p (s l) -> p s l", s=S)
                P.op("pool", lambda e, j=j, gb3=gb3: e.tensor_copy(out=gb3[:, :, 0:2], in_=fh_t[:, j, 0:S, :]), reads=[fh_b[j]], writes=[gb_b])
                G3 = Gt[:, 0:N].rearrange("p (s l) -> p s l", s=S)
                P.op("act", lambda e, gb3=gb3, G3=G3: e.copy(out=gb3[:, :, 2:2 + L], in_=G3), reads=[Gb], writes=[gb_b])
                c0_t, c0_b = self.tf.next()
                P.op("act", lambda e, j=j, c0_t=c0_t, Gt=Gt: e.activation(out=c0_t[:, 0:N], in_=Gt[:, 0:N], func=AF.Identity,
                                                                          scale=self.fcw[:, l, j, 2:3], bias=self.fcb[:, l, j:j + 1]),
                     reads=[Gb, self.fcwb, self.fcbb], writes=[c0_b])
                c03 = c0_t[:, 0:N].rearrange("p (s l) -> p s l", s=S)
                for k, eng in ((1, "dve"), (0, "dve")):
                    P.op(eng, lambda e, j=j, k=k, gb3=gb3, c03=c03: e.scalar_tensor_tensor(out=c03, in0=gb3[:, :, k:k + L], scalar=self.fcw[:, l, j, k:k + 1],
                                                                                              in1=c03, op0=ALU.mult, op1=ALU.add),
                         reads=[gb_b, self.fcwb, c0_b], writes=[c0_b])
                P.op("pool", lambda e, j=j, gb3=gb3: e.tensor_copy(out=fh_t[:, j, 0:S, :], in_=gb3[:, :, L:L + 2]), reads=[gb_b], writes=[fh_b[j]])
                P.op("act", lambda e, c0_t=c0_t: e.activation(out=c0_t[:, 0:N], in_=c0_t[:, 0:N], func=AF.Gelu_apprx_tanh), reads=[c0_b], writes=[c0_b])
                P.op("dve", lambda e, j=j, c0_t=c0_t, Ut=Ut: e.tensor_tensor(out=self.hmid[:, j, 0:N], in0=Ut[:, 0:N], in1=c0_t[:, 0:N], op=ALU.mult),
                     reads=[Ub, c0_b], writes=[self.hmidb[j]])
            self.done_w(2)
        for mg in range(2):
            bks = [self.bank(i) for i in range(4)]
            for ks in range(3):
                w_t, w_b = self.next_w()
                for mm in range(4):
                    bt, bb = bks[mm]
                    for kc in range(8):
                        P.op("pe", lambda e, kc=kc, mm=mm, ks=ks, w_t=w_t, bt=bt: e.matmul(bt[:, 0:N], lhsT=w_t[:, kc, mm * 128:(mm + 1) * 128],
                                                                                           rhs=self.hmid[:, ks * 8 + kc, 0:N],
                                                                                           start=(ks == 0 and kc == 0), stop=(ks == 2 and kc == 7)),
                             reads=[w_b, self.hmidb[ks * 8 + kc]], writes=[bb])
                self.done_w(1)
            for mm in range(4):
                m = 4 * mg + mm
                bt, bb = bks[mm]
                P.op("dve", lambda e, m=m, bt=bt: e.tensor_tensor(out=self.x[:, m, 0:N], in0=bt[:, 0:N], in1=self.x[:, m, 0:N], op=ALU.add),
                     reads=[bb, self.xb[m]], writes=[self.xb[m]])

    def shared_kv(self, c):
        P = self.P
        N = c.N
        self.rmsnorm_to_xn(c, 2)
        wk = [self.next_w(), self.next_w()]
        kbs = []
        for gi, (c0, nt, _src) in enumerate(c.tgs):
            k_dst, v_dst, kt_dst = c.kv_dst[gi]
            ksb, ksbb = self.rowbufs.next()
            for half in range(2):
                bt, bb = self.bank(half + 4 * (gi % 2))
                w_t, w_b = wk[half]
                for kc in range(8):
                    P.op("pe", lambda e, kc=kc, w_t=w_t, bt=bt, c0=c0, nt=nt: e.matmul(bt[0:nt, :], lhsT=self.xn[:, kc, c0:c0 + nt], rhs=w_t[:, kc, :],
                                                                                       start=(kc == 0), stop=(kc == 7)),
                         reads=[w_b, self.xnb[kc]], writes=[bb])
                P.op("act", lambda e, half=half, bt=bt, ksb=ksb, nt=nt: e.copy(out=ksb[0:nt, half * 512:(half + 1) * 512], in_=bt[0:nt, :]), reads=[bb], writes=[ksbb])
            qk_ = 4 + (gi % 2)
            sq = self.hmid[:, 4 * qk_:4 * qk_ + 4, :].rearrange("p a b -> p (a b)").bitcast(F32)
            sqb = [self.hmidb[4 * qk_ + i_] for i_ in range(4)]
            P.op("act", lambda e, sq=sq, ksb=ksb, nt=nt: e.activation(out=sq[0:nt, :], in_=ksb[0:nt, :], func=AF.Square), reads=[ksbb], writes=sqb)
            ss, ssb = self.small.next()
            P.op("dve", lambda e, ss=ss, sq=sq, nt=nt: e.tensor_reduce(out=ss[0:nt, 0:NH], in_=sq[0:nt, :].rearrange("p (h d) -> p h d", h=NH),
                                                                      axis=mybir.AxisListType.X, op=ALU.add), reads=sqb, writes=[ssb])
            P.op("pool", lambda e, ss=ss, nt=nt: e.tensor_scalar(out=ss[0:nt, 8:16], in0=ss[0:nt, 0:NH], scalar1=1.0 / HD, scalar2=EPS, op0=ALU.mult, op1=ALU.add),
                 reads=[ssb], writes=[ssb])
            P.op("pool", lambda e, ss=ss, nt=nt: e.tensor_tensor(out=ss[0:nt, 16:24], in0=ss[0:nt, 8:16], in1=self.mhalf[0:nt, :], op=ALU.pow),
                 reads=[ssb, self.mhalfb], writes=[ssb])
            for h in range(NH):
                P.op("dve", lambda e, h=h, ksb=ksb, ss=ss, nt=nt: e.scalar_tensor_tensor(out=ksb[0:nt, h * HD:(h + 1) * HD], in0=ksb[0:nt, h * HD:(h + 1) * HD],
                                                                                         scalar=ss[0:nt, 16 + h:17 + h], in1=self.knbc[0:nt, :],
                                                                                         op0=ALU.mult, op1=ALU.mult),
                     reads=[ksbb, ssb, self.knbcb], writes=[ksbb])
            P.dma("sp", lambda e, ksb=ksb, k_dst=k_dst, nt=nt: e.dma_start(out=k_dst, in_=ksb[0:nt, :]), reads=[ksbb])
            kb, kbl = self.hb(4 + gi)
            P.op("dve", lambda e, kb=kb, ksb=ksb, nt=nt: e.tensor_copy(out=kb[0:nt, :], in_=ksb[0:nt, :]), reads=[ksbb], writes=kbl)
            kbs.append((kb, kbl))
        self.done_w(2)
        wv = [self.next_w(), self.next_w()]
        for gi, (c0, nt, _src) in enumerate(c.tgs):
            k_dst, v_dst, kt_dst = c.kv_dst[gi]
            vsb, vsbb = self.rowbufs.next()
            for half in range(2):
                bt, bb = self.bank(2 + half + 4 * (gi % 2))
                w_t, w_b = wv[half]
                for kc in range(8):
                    P.op("pe", lambda e, kc=kc, w_t=w_t, bt=bt, c0=c0, nt=nt: e.matmul(bt[0:nt, :], lhsT=self.xn[:, kc, c0:c0 + nt], rhs=w_t[:, kc, :],
                                                                                       start=(kc == 0), stop=(kc == 7)),
                         reads=[w_b, self.xnb[kc]], writes=[bb])
                P.op("act", lambda e, half=half, bt=bt, vsb=vsb, nt=nt: e.copy(out=vsb[0:nt, half * 512:(half + 1) * 512], in_=bt[0:nt, :]), reads=[bb], writes=[vsbb])
            P.dma("sp", lambda e, vsb=vsb, v_dst=v_dst, nt=nt: e.dma_start(out=v_dst, in_=vsb[0:nt, :]), reads=[vsbb])
            vb_ap, vb_bufs = c.vb_dst[gi]
            P.op("dve", lambda e, vb_ap=vb_ap, vsb=vsb, nt=nt: e.tensor_copy(out=vb_ap, in_=vsb[0:nt, :]), reads=[vsbb], writes=vb_bufs)
        self.done_w(2)
        for gi, (c0, nt, _src) in enumerate(c.tgs):
            k_dst, v_dst, kt_dst = c.kv_dst[gi]
            kb, kbl = kbs[gi]
            tb_t, tb_b = self.bankb
            for h in range(NH):
                P.op("pe", lambda e, h=h, kb=kb, tb_t=tb_t, nt=nt: e.transpose(out=tb_t[:, h * 128:h * 128 + nt], in_=kb[0:nt, h * HD:(h + 1) * HD],
                                                                               identity=self.ident_b[0:nt, 0:nt]),
                     reads=list(kbl) + [self.ident_bb], writes=[tb_b])
            kt_ap, kt_bufs = kt_dst
            P.op("act", lambda e, kt_ap=kt_ap, tb_t=tb_t, nt=nt: e.copy(out=kt_ap, in_=tb_t[:, :].rearrange("p (h t) -> p h t", h=NH)[:, :, 0:nt]),
                 reads=[tb_b], writes=kt_bufs)

    def q_proj(self, c):
        P = self.P
        N = c.N
        self.rmsnorm_to_xn(c, 3)
        st1 = {}
        wcur = {}

        def s1(h):
            hh = h % 4
            if hh == 0:
                wcur["w"] = self.next_w()
            w_t, w_b = wcur["w"]
            bt, bb = self.bank(3 + (h % 2))
            for kc in range(8):
                P.op("pe", lambda e, kc=kc, hh=hh, w_t=w_t, bt=bt: e.matmul(bt[:, 0:N], lhsT=w_t[:, kc, hh * 128:(hh + 1) * 128], rhs=self.xn[:, kc, 0:N],
                                                                            start=(kc == 0), stop=(kc == 7)),
                     reads=[w_b, self.xnb[kc]], writes=[bb])
            if hh == 3:
                self.done_w(1)
            qf, qfb = self.tf.next()
            sq, sqb = self.tb.next()
            P.op("act", lambda e, qf=qf, bt=bt: e.copy(out=qf[:, 0:N], in_=bt[:, 0:N]), reads=[bb], writes=[qfb])
            P.op("act", lambda e, sq=sq, bt=bt: e.activation(out=sq[:, 0:N], in_=bt[:, 0:N], func=AF.Square), reads=[bb], writes=[sqb])
            st1[h] = (qf, qfb, sq, sqb)

        def s2(h):
            qf, qfb, sq, sqb = st1.pop(h)
            st, stb = self.bank(6)
            P.op("pe", lambda e, sq=sq, st=st: e.matmul(st[:, 0:N], lhsT=self.ones_hd[:], rhs=sq[:, 0:N], start=True, stop=True),
                 reads=[self.ones_hdb, sqb], writes=[stb])
            rs, rsb = self.tf.next()
            P.op("act", lambda e, rs=rs, st=st: e.activation(out=rs[:, 0:N], in_=st[:, 0:N], func=AF.Ln, bias=EPS), reads=[stb], writes=[rsb])
            P.op("act", lambda e, rs=rs: e.activation(out=rs[:, 0:N], in_=rs[:, 0:N], func=AF.Exp, scale=-0.5), reads=[rsb], writes=[rsb])
            P.op("dve", lambda e, h=h, qf=qf, rs=rs: e.scalar_tensor_tensor(out=self.bufA[:, h, 0:N], in0=qf[:, 0:N], scalar=self.qg[:, 1:2], in1=rs[:, 0:N],
                                                                            op0=ALU.mult, op1=ALU.mult),
                 reads=[qfb, rsb, self.qgb], writes=[self.bufAb[h]])

        s1(0)
        for h in range(NH):
            if h + 1 < NH:
                s1(h + 1)
            s2(h)

    def zero_fill_r(self, ap, wbufs):
        pat = [[0, int(n_)] for n_ in ap.shape[1:]]
        self.P.op("pool", lambda e: e.affine_select(out=ap, in_=ap, pattern=pat, compare_op=ALU.is_gt,
                                                    fill=0.0, base=0, channel_multiplier=0),
                  reads=wbufs, writes=wbufs)

    def attn_prompt(self, c):
        P = self.P
        N = c.N
        qt = c.qt
        nchunk = 4 * qt + 4
        items = []
        for hp in range(NH // 2):
            for idx, ci in enumerate(range(nchunk - 1, -1, -1)):
                i = ci - 4 * qt
                items.append(dict(hp=hp, ci=ci, first=(idx == 0), last=(ci == 0), i=i, qlo=max(0, i) * 128))
        n = len(items)

        def v3(ap):
            return ap.rearrange("p (a n) -> p a n", a=2)

        def qk_sp(k):
            it = items[k]
            hp, ci, qlo = it["hp"], it["ci"], it["qlo"]
            z = v3(self.pairs[k % 2][:, :])
            zb = [self.bank_bufs[2 * (k % 2)], self.bank_bufs[2 * (k % 2) + 1]]
            it["z"] = (z, zb)
            for a in range(2):
                h = 2 * hp + a
                P.op("pe", lambda e, a=a, h=h: e.matmul(z[:, a, qlo:N], lhsT=self.KT[:, h, ci * 128:(ci + 1) * 128], rhs=self.bufA[:, h, qlo:N],
                                                        start=True, stop=True),
                     reads=[self.KTb[ci], self.bufAb[h]], writes=[zb[a]])
            e2f, e2b = self.hq(2 + (k % 2))
            e2 = v3(e2f)
            spf, spb = self.tr.next()
            sp = v3(spf[:, :])
            it["sp"] = (sp, spb)
            P.op("act", lambda e: e.activation(out=e2[:, :, qlo:N], in_=z[:, :, qlo:N], func=AF.Exp), reads=zb, writes=e2b)
            P.op("act", lambda e: e.activation(out=sp[:, :, qlo:N], in_=e2[:, :, qlo:N], func=AF.Ln, bias=1.0), reads=e2b, writes=[spb])
            if it["i"] >= 0:
                for a in range(2):
                    P.op("dve", lambda e, a=a: e.tensor_tensor(out=sp[:, a, qlo:qlo + 128], in0=sp[:, a, qlo:qlo + 128], in1=self.M01[:], op=ALU.mult),
                         reads=[spb, self.M01b], writes=[spb])
            if not it["last"]:
                if it["first"]:
                    if qlo >= 128:
                        self.zero_fill_r(sp[:, :, qlo - 128:qlo], [spb])
                    it["rs"] = (sp, spb)
                else:
                    prs, prsb = items[k - 1]["rs"]
                    rsf, rsb = self.tr.next()
                    rs = v3(rsf[:, :])
                    if qlo >= 128:
                        self.zero_fill_r(rs[:, :, qlo - 128:qlo], [rsb])
                    P.op("dve", lambda e: e.tensor_tensor(out=rs[:, :, qlo:N], in0=prs[:, :, qlo:N], in1=sp[:, :, qlo:N], op=ALU.add),
                         reads=[prsb, spb], writes=[rsb])
                    it["rs"] = (rs, rsb)

        def cum_w(k):
            it = items[k]
            qlo = it["qlo"]
            z, zb = it["z"]
            sp, spb = it["sp"]
            first = it["first"]
            for a in range(2):
                if it["i"] >= 0:
                    P.op("pe", lambda e, a=a: e.matmul(z[:, a, qlo:qlo + 128], lhsT=self.ident_b[:], rhs=self.MB[:], start=False, stop=False, skip_group_check=True),
                         reads=[self.ident_bb, self.MBb], writes=[zb[a]])
                P.op("pe", lambda e, a=a: e.matmul(z[:, a, qlo:N], lhsT=self.negtri[:], rhs=sp[:, a, qlo:N], start=False, stop=first, skip_group_check=True),
                     reads=[self.negtrib, spb], writes=[zb[a]])
                if not first:
                    prs, prsb = items[k - 1]["rs"]
                    P.op("pe", lambda e, a=a, prs=prs: e.matmul(z[:, a, qlo:N], lhsT=self.negones[:], rhs=prs[:, a, qlo:N], start=False, stop=True, skip_group_check=True),
                         reads=[self.negonesb, prsb], writes=[zb[a]])
            wf, wb = self.hb(8 + (k % 3))
            w = v3(wf)
            it["w"] = (w, wb)
            if qlo > 0:
                P.op("pool", lambda e: e.memset(w[:, :, 0:qlo], 0.0), writes=wb)
            P.op("act", lambda e: e.activation(out=w[:, :, qlo:N], in_=z[:, :, qlo:N], func=AF.Exp), reads=zb, writes=wb)

        def pv(k):
            it = items[k]
            hp, ci = it["hp"], it["ci"]
            w, wb = it["w"]
            for a in range(2):
                h = 2 * hp + a
                ot, otb = self.bank(4 + 2 * (hp % 2) + a)
                P.op("pe", lambda e, a=a, h=h, ot=ot: e.matmul(ot[:, 0:N], lhsT=self.Vb[:, ci, h * HD:(h + 1) * HD], rhs=w[:, a, 0:N], start=it["first"], stop=it["last"]),
                     reads=[self.Vbb[ci]] + wb, writes=[otb])
                if it["last"]:
                    P.op("dve", lambda e, h=h, ot=ot: e.tensor_copy(out=self.hmid[:, h, 0:N], in_=ot[:, 0:N]), reads=[otb], writes=[self.hmidb[h]])

        for step in range(n + 2):
            if 0 <= step - 1 < n:
                cum_w(step - 1)
            if step < n:
                qk_sp(step)
            if 0 <= step - 2 < n:
                pv(step - 2)

    def attn_sample(self, c):
        P = self.P
        d = self.d
        NQ = NH * DSEQ
        hbi = 0
        for s in range(SB_):
            ot, otb = self.bank(5 + s)
            qcol = s * DSEQ
            nchunk = PAST // 128 + 1
            first = True
            for ci in range(nchunk - 1, -1, -1):
                new = (ci == nchunk - 1)
                if new:
                    rows = DSEQ
                    kt_h = lambda h, qcol=qcol: self.KT[:, h, qcol:qcol + DSEQ]
                    kt_bufs = [self.KTb[0]]
                    v_h = lambda h, s=s: self.Vb[0:DSEQ, s, h * HD:(h + 1) * HD]
                    v_bufs = [self.Vbb[s]]
                else:
                    rows = 128
                    kf, kfb = self.rowbufs.next()
                    P.dma("sp", lambda e, kf=kf, s=s, ci=ci: e.dma_start(out=kf[:], in_=d["ck"][s, ci * 128:(ci + 1) * 128, :]), writes=[kfb])
                    vf, vfb = self.rowbufs.next()
                    P.dma("sp", lambda e, vf=vf, s=s, ci=ci: e.dma_start(out=vf[:], in_=d["cv"][s, ci * 128:(ci + 1) * 128, :]), writes=[vfb])
                    kb, kbl = self.hb(4 + (hbi % 8))
                    hbi += 1
                    P.op("dve", lambda e, kb=kb, kf=kf: e.tensor_copy(out=kb, in_=kf[:]), reads=[kfb], writes=kbl)
                    tb_t, tb_b = self.bankb
                    for h in range(NH):
                        P.op("pe", lambda e, h=h, kb=kb, tb_t=tb_t: e.transpose(out=tb_t[:, h * 128:(h + 1) * 128], in_=kb[:, h * HD:(h + 1) * HD], identity=self.ident_b[:]),
                             reads=list(kbl) + [self.ident_bb], writes=[tb_b])
                    kc_r, kc_bl = self.hb(4 + (hbi % 8))
                    hbi += 1
                    kc_t = kc_r.rearrange("p (h t) -> p h t", h=NH)
                    P.op("dve", lambda e, kc_r=kc_r, tb_t=tb_t: e.tensor_copy(out=kc_r, in_=tb_t[:, :]),
                         reads=[tb_b], writes=kc_bl)
                    vb, vbl = self.hb(4 + (hbi % 8))
                    hbi += 1
                    P.op("pool", lambda e, vb=vb, vf=vf: e.tensor_copy(out=vb, in_=vf[:]), reads=[vfb], writes=vbl)
                    kt_h = lambda h, kc_t=kc_t: kc_t[:, h, :]
                    kt_bufs = list(kc_bl)
                    v_h = lambda h, vb=vb: vb[:, h * HD:(h + 1) * HD]
                    v_bufs = list(vbl)
                zt, zb = self.bank(ci % 3)
                for h in range(NH):
                    P.op("pe", lambda e, h=h, zt=zt, kt_h=kt_h, rows=rows, qcol=qcol: e.matmul(zt[0:rows, h * DSEQ:(h + 1) * DSEQ], lhsT=kt_h(h), rhs=self.bufA[:, h, qcol:qcol + DSEQ],
                                                                                    start=(h == 0), stop=(h == NH - 1), skip_group_check=True),
                         reads=kt_bufs + [self.bufAb[h]], writes=[zb])
                et, etb = self.tf.next()
                sp, spb = self.tr.next()
                P.op("act", lambda e, et=et, zt=zt, rows=rows: e.activation(out=et[0:rows, 0:NQ], in_=zt[0:rows, 0:NQ], func=AF.Exp), reads=[zb], writes=[etb])
                P.op("act", lambda e, et=et, sp=sp, rows=rows: e.activation(out=sp[0:rows, 0:NQ], in_=et[0:rows, 0:NQ], func=AF.Ln, bias=1.0), reads=[etb], writes=[spb])
                if new:
                    P.op("dve", lambda e, sp=sp: e.tensor_tensor(out=sp[0:DSEQ, 0:NQ], in0=sp[0:DSEQ, 0:NQ], in1=self.M01s[0:DSEQ, :], op=ALU.mult),
                         reads=[spb, self.M01sb], writes=[spb])
                    P.op("pe", lambda e, zt=zt: e.matmul(zt[0:DSEQ, 0:NQ], lhsT=self.ident_b[0:DSEQ, 0:DSEQ], rhs=self.MBs[0:DSEQ, :], start=False, stop=False,
                                                         skip_group_check=True),
                         reads=[self.ident_bb, self.MBsb], writes=[zb])
                P.op("pe", lambda e, zt=zt, sp=sp, rows=rows, first=first: e.matmul(zt[0:rows, 0:NQ], lhsT=self.negtri[0:rows, 0:rows], rhs=sp[0:rows, 0:NQ], start=False, stop=first,
                                                                                    skip_group_check=True),
                     reads=[self.negtrib, spb], writes=[zb])
                if not first:
                    P.op("pe", lambda e, zt=zt: e.matmul(zt[:, 0:NQ], lhsT=self.negones[:], rhs=self.RS[:, 0:NQ], start=False, stop=True, skip_group_check=True),
                         reads=[self.negonesb, self.RSb], writes=[zb])
                wt, wtb = self.tb.next()
                P.op("act", lambda e, wt=wt, zt=zt, rows=rows: e.activation(out=wt[0:rows, 0:NQ], in_=zt[0:rows, 0:NQ], func=AF.Exp), reads=[zb], writes=[wtb])
                if first:
                    self.zero_fill_r(self.RS[:, 0:NQ], [self.RSb])
                    P.op("pool", lambda e, sp=sp: e.tensor_copy(out=self.RS[0:DSEQ, 0:NQ], in_=sp[0:DSEQ, 0:NQ]), reads=[spb], writes=[self.RSb])
                elif ci > 0:
                    P.op("dve", lambda e, sp=sp: e.tensor_tensor(out=self.RS[:, 0:NQ], in0=self.RS[:, 0:NQ], in1=sp[:, 0:NQ], op=ALU.add),
                         reads=[spb, self.RSb], writes=[self.RSb])
                for h in range(NH):
                    P.op("pe", lambda e, h=h, ot=ot, wt=wt, v_h=v_h, rows=rows, first=first: e.matmul(ot[:, h * DSEQ:(h + 1) * DSEQ], lhsT=v_h(h), rhs=wt[0:rows, h * DSEQ:(h + 1) * DSEQ],
                                                                                                      start=(first and h == 0), stop=(ci == 0 and h == NH - 1), skip_group_check=True),
                         reads=v_bufs + [wtb], writes=[otb])
                first = False
            P.op("act", lambda e, ot=ot, qcol=qcol: e.copy(out=self.hmid[:, 0:NH, qcol:qcol + DSEQ], in_=ot[:, 0:NQ].rearrange("p (h q) -> p h q", h=NH)),
                 reads=[otb], writes=[self.hmidb[h] for h in range(NH)])

    def final_out(self, c):
        P = self.P
        N = c.N
        self.rmsnorm_to_xn(c, 5, inplace=True)
        for gi, (c0, nt, _src) in enumerate(c.tgs):
            ysb, ysbb = self.rowbufs.next()
            for half in range(2):
                bt, bb = self.bank(half)
                for q in range(4):
                    fc = half * 4 + q
                    P.op("pe", lambda e, q=q, fc=fc, bt=bt, c0=c0, nt=nt: e.transpose(out=bt[0:nt, q * 128:(q + 1) * 128], in_=self.x[:, fc, c0:c0 + nt], identity=self.ident_f[:]),
                         reads=[self.xb[fc], self.ident_fb], writes=[bb])
                if half == 0:
                    P.op("act", lambda e, bt=bt, ysb=ysb, nt=nt: e.copy(out=ysb[0:nt, 0:512], in_=bt[0:nt, :]), reads=[bb], writes=[ysbb])
                else:
                    P.op("dve", lambda e, bt=bt, ysb=ysb, nt=nt: e.tensor_copy(out=ysb[0:nt, 512:1024], in_=bt[0:nt, :]), reads=[bb], writes=[ysbb])
            dst = c.y_dst[gi]
            P.dma("sp", lambda e, ysb=ysb, dst=dst, nt=nt: e.dma_start(out=dst, in_=ysb[0:nt, :]), reads=[ysbb])

    def init_states_zero(self):
        P = self.P
        P.op("pool", lambda e: e.memset(self.rhist[:], 0.0), writes=self.rhistb)
        P.op("pool", lambda e: e.memset(self.hstate[:], 0.0), writes=self.hstateb)
        for l in range(2):
            t, b = self.fhist[l]
            P.op("pool", lambda e, t=t: e.memset(t[:], 0.0), writes=b)

    def init_states_sample(self):
        P, d = self.P, self.d
        for s in range(SB_):
            for k in range(3):
                P.dma("sp", lambda e, s=s, k=k: e.dma_start(out=self.rhist[:, :, s, k], in_=d["s_conv"][s, k].rearrange("(j p) -> p j", p=128),
                                                            allow_slow_non_contiguous=True), writes=self.rhistb)
            P.dma("sp", lambda e, s=s: e.dma_start(out=self.hstate[:, :, s], in_=d["s_h"][s].rearrange("(j p) -> p j", p=128),
                                                   allow_slow_non_contiguous=True), writes=self.hstateb)
            for l in range(2):
                t, b = self.fhist[l]
                for k in range(2):
                    P.dma("sp", lambda e, s=s, l=l, t=t, k=k: e.dma_start(out=t[:, :, s, k], in_=d["s_fconv"][l, s, k].rearrange("(j p) -> p j", p=128),
                                                                          allow_slow_non_contiguous=True), writes=b)

    def write_states(self, seg, dst_h, dst_conv, dst_fconv):
        P = self.P
        bt, bb = self.bank(0)
        stg, stgb = self.rowbufs.next()
        P.op("pe", lambda e, bt=bt: e.transpose(out=bt[0:8, 0:128], in_=self.hstate[:, :, seg], identity=self.ident_f[:]),
             reads=self.hstateb + [self.ident_fb], writes=[bb])
        P.op("act", lambda e, bt=bt, stg=stg: e.copy(out=stg[0:8, 0:128], in_=bt[0:8, 0:128]), reads=[bb], writes=[stgb])
        P.dma("sp", lambda e, stg=stg: e.dma_start(out=dst_h.rearrange("(j p) -> j p", p=128), in_=stg[0:8, 0:128]), reads=[stgb])
        stg, stgb = self.rowbufs.next()
        for half in range(2):
            bt, bb = self.bank(1 + half)
            for q in range(4):
                j = half * 4 + q
                P.op("pe", lambda e, j=j, q=q, bt=bt: e.transpose(out=bt[0:3, q * 128:(q + 1) * 128], in_=self.rhist[:, j, seg, :], identity=self.ident_f[:]),
                     reads=[self.rhistb[j], self.ident_fb], writes=[bb])
            P.op("act", lambda e, half=half, bt=bt, stg=stg: e.copy(out=stg[0:3, half * 512:(half + 1) * 512], in_=bt[0:3, :]), reads=[bb], writes=[stgb])
        P.dma("sp", lambda e, stg=stg: e.dma_start(out=dst_conv, in_=stg[0:3, :]), reads=[stgb])
        for l in range(2):
            t, b = self.fhist[l]
            for part in range(3):
                stg, stgb = self.rowbufs.next()
                for half in range(2):
                    bt, bb = self.bank(3 + half)
                    for q in range(4):
                        j = part * 8 + half * 4 + q
                        P.op("pe", lambda e, j=j, q=q, bt=bt, t=t: e.transpose(out=bt[0:2, q * 128:(q + 1) * 128], in_=t[:, j, seg, :], identity=self.ident_f[:]),
                             reads=[b[j], self.ident_fb], writes=[bb])
                    P.op("act", lambda e, half=half, bt=bt, stg=stg: e.copy(out=stg[0:2, half * 512:(half + 1) * 512], in_=bt[0:2, :]), reads=[bb], writes=[stgb])
                P.dma("sp", lambda e, stg=stg, l=l, part=part: e.dma_start(out=dst_fconv[l][:, part * 1024:(part + 1) * 1024], in_=stg[0:2, :]), reads=[stgb])

    def dbg(self, c, name, ap, bufs, shape, dt=F32):
        if not getattr(c, "dbg", False):
            return
        o = self.nc.dram_tensor("dbg_" + name, list(shape), dt, kind="ExternalOutput").ap()
        self.P.dma("sp", lambda e: e.dma_start(out=o, in_=ap), reads=bufs)

    def do_casts(self, stage):
        for (dst, src, name, sem) in self.cast_plan.pop(stage, []):
            self._cast(dst, src, name, sem)

    def run_tile(self, c, next_c=None):
        N = c.N
        if getattr(c, "dbg", False):
            o = self.nc.dram_tensor("dbg_win", [1024, 2048], BF16, kind="ExternalOutput").ap()
            self.P.dma("sp", lambda e: e.dma_start(out=o, in_=self.d["w_in_b"]), reads=[self.wsb["w_in"]])
        self.x_to_fm(c)
        self.dbg(c, "x0", self.x[:, :, 0:N], self.xb, [128, 8, N])
        self.rmsnorm_to_xn(c, 0)
        self.dbg(c, "xn0", self.xn[:, :, 0:N], self.xnb, [128, 8, N], BF16)
        self.do_casts("lru")
        self.lru_mixer(c)
        self.dbg(c, "hg", self.bufA[:, :, 0:N], self.bufAb, [128, 8, N], BF16)
        self.dbg(c, "x1", self.x[:, :, 0:N], self.xb, [128, 8, N])
        self.do_casts("ffn0")
        self.conv_ffn(c, 0)
        self.dbg(c, "x2", self.x[:, :, 0:N], self.xb, [128, 8, N])
        self.do_casts("kv")
        self.shared_kv(c)
        self.do_casts("q")
        self.q_proj(c)
        if c.sample:
            self.attn_sample(c)
        else:
            self.attn_prompt(c)
        if next_c is not None:
            c0n, ntn, srcn = next_c.tgs[0]
            self.P.dma("sp", lambda e: e.dma_start(out=self.xpre[0:ntn, :], in_=srcn), writes=[self.xpreb])
            next_c.prefetched = True
        self.proj_residual(c, self.hmid, self.hmidb)
        self.conv_ffn(c, 1)
        self.final_out(c)


def make_prompt_ctx(B, b, qt):
    d = B.d
    c = TileCtx()
    c.sample = False
    c.N, c.nseg, c.L = TN, 1, TN
    c.qt = qt
    t0 = qt * TN
    c.tgs = [(g * 128, 128, d["xp"][b, t0 + g * 128:t0 + (g + 1) * 128, :]) for g in range(4)]
    c.kv_dst = []
    c.vb_dst = []
    c.y_dst = []
    for g in range(4):
        ch = qt * 4 + g
        rows = slice(t0 + g * 128, t0 + (g + 1) * 128)
        c.kv_dst.append((d["p_k"][b, rows, :], d["p_v"][b, rows, :], (B.KT[:, :, ch * 128:(ch + 1) * 128], [B.KTb[ch]])))
        c.vb_dst.append((B.Vb[:, ch, :], [B.Vbb[ch]]))
        c.y_dst.append(d["y_p"][b, rows, :])
    return c


def make_sample_ctx(B):
    d = B.d
    c = TileCtx()
    c.sample = True
    c.N, c.nseg, c.L = SB_ * DSEQ, SB_, DSEQ
    c.qt = 0
    c.tgs = []
    c.kv_dst = []
    c.vb_dst = []
    c.y_dst = []
    for s in range(SB_):
        rows = slice(s * DSEQ, (s + 1) * DSEQ)
        c.tgs.append((s * DSEQ, DSEQ, d["xs"][rows, :]))
        c.kv_dst.append((d["o_k"][rows, :], d["o_v"][rows, :], (B.KT[:, :, s * DSEQ:(s + 1) * DSEQ], [B.KTb[0]])))
        c.vb_dst.append((B.Vb[0:DSEQ, s, :], [B.Vbb[s]]))
        c.y_dst.append(d["y_s"][rows, :])
    return c


def build_nc():
    nc = bass.Bass("TRN2", target_bir_lowering=False)
    B = Builder(nc)
    B.declare()
    with B.st:
        B.setup()
        B.plan_weights(PB * (SEQ // TN) + 1)
        P, d = B.P, B.d
        ctxs = []
        for b in range(PB):
            for qt in range(SEQ // TN):
                c = make_prompt_ctx(B, b, qt)
                c.dbg = DEBUG and b == 0 and qt == 0
                ctxs.append((b, qt, c))
        cs = make_sample_ctx(B)
        for i, (b, qt, c) in enumerate(ctxs):
            if qt == 0:
                B.init_states_zero()
            nxt = ctxs[i + 1][2] if i + 1 < len(ctxs) else cs
            B.run_tile(c, nxt)
            if qt == SEQ // TN - 1:
                B.write_states(0, d["p_h"][b], d["p_conv"][b], [d["p_fconv"][l, b] for l in range(2)])
        B.init_states_sample()
        B.run_tile(cs, None)
        for s_ in range(SB_):
            B.write_states(s_, d["o_h"][s_], d["o_conv"][s_], [d["o_fconv"][l, s_] for l in range(2)])
        P.emit()
    return nc


_NC_CACHE = {}


def kernel(**inputs):
    f32 = lambda a: np.ascontiguousarray(np.asarray(a, dtype=np.float32))
    x_prompt = f32(inputs["x_prompt"])
    x_sample = f32(inputs["x_sample"])
    state_lru_h = f32(inputs["state_lru_h"])
    state_lru_conv = f32(inputs["state_lru_conv"])
    state_ffn_conv = f32(inputs["state_ffn_conv"])
    cache_k = f32(inputs["cache_k"])
    cache_v = f32(inputs["cache_v"])
    shared = {
        "a_norm": f32(inputs["a_norm"])[0], "a_w_in": f32(inputs["a_w_in"])[0], "a_conv_w": f32(inputs["a_conv_w"])[0],
        "a_conv_b": f32(inputs["a_conv_b"])[0], "a_wr": f32(inputs["a_wr"])[0], "a_br": f32(inputs["a_br"])[0],
        "a_wi": f32(inputs["a_wi"])[0], "a_bi": f32(inputs["a_bi"])[0], "a_lambda": f32(inputs["a_lambda"])[0],
        "a_w_out": f32(inputs["a_w_out"])[0], "kv_norm": f32(inputs["kv_norm"]), "w_kv": f32(inputs["w_kv"]),
        "k_norm": f32(inputs["k_norm"]), "b_norm": f32(inputs["b_norm"])[0], "b_wq": f32(inputs["b_wq"])[0],
        "q_norm": f32(inputs["q_norm"])[0], "b_wo": f32(inputs["b_wo"])[0], "f_norm": f32(inputs["f_norm"]),
        "f_w_up": f32(inputs["f_w_up"]), "f_conv_w": f32(inputs["f_conv_w"]), "f_conv_b": f32(inputs["f_conv_b"]),
        "f_w_down": f32(inputs["f_w_down"]), "out_norm": f32(inputs["out_norm"]),
    }
    shared = {k: np.ascontiguousarray(v) for k, v in shared.items()}
    in_maps = []
    for cidx in range(NCORES):
        ps = slice(cidx * PB, (cidx + 1) * PB)
        ss = slice(cidx * SB_, (cidx + 1) * SB_)
        m = dict(shared)
        m["xp"] = np.ascontiguousarray(x_prompt[ps])
        m["xs"] = np.ascontiguousarray(x_sample[ss].reshape(SB_ * DSEQ, D))
        m["s_h"] = np.ascontiguousarray(state_lru_h[0, ss])
        m["s_conv"] = np.ascontiguousarray(state_lru_conv[0, ss])
        m["s_fconv"] = np.ascontiguousarray(state_ffn_conv[:, ss])
        m["ck"] = np.ascontiguousarray(cache_k[ss].reshape(SB_, PAST, D))
        m["cv"] = np.ascontiguousarray(cache_v[ss].reshape(SB_, PAST, D))
        in_maps.append(m)
    if "nc" not in _NC_CACHE:
        _NC_CACHE["nc"] = build_nc()
    nc = _NC_CACHE["nc"]
    res = run_bass_kernel_spmd(nc, in_maps, core_ids=list(range(NCORES)))
    R = res.results
    if DEBUG:
        _NC_CACHE["dbg"] = {k: np.asarray(v) for k, v in R[0].items() if k.startswith("dbg_")}
    cat = lambda k, ax=0: np.concatenate([np.asarray(r[k], dtype=np.float32) for r in R], axis=ax)
    B_ = NCORES * PB
    DB = NCORES * SB_
    y_prompt = cat("y_p")
    y_sample = cat("y_s").reshape(DB, DSEQ, D)
    p_lru_h = cat("p_h")[None]
    p_lru_conv = cat("p_conv")[None]
    p_ffn_conv = cat("p_fconv", 1)
    p_k = cat("p_k").reshape(B_, SEQ, NH, HD)
    p_v = cat("p_v").reshape(B_, SEQ, NH, HD)
    s_lru_h = cat("o_h")[None]
    s_lru_conv = cat("o_conv")[None]
    s_ffn_conv = cat("o_fconv", 1)
    s_k = cat("o_k").reshape(DB, DSEQ, NH, HD)
    s_v = cat("o_v").reshape(DB, DSEQ, NH, HD)
    return (y_prompt, y_sample, p_lru_h, p_lru_conv, p_ffn_conv, p_k, p_v,
            s_lru_h, s_lru_conv, s_ffn_conv, s_k, s_v)
```

```python
import contextlib
import numpy as np
import concourse.bass as bass
import concourse.mybir as mybir
from concourse.bass_utils import run_bass_kernel_spmd

F32 = mybir.dt.float32
BF16 = mybir.dt.bfloat16
F32R = mybir.dt.float32r
AF = mybir.ActivationFunctionType
ALU = mybir.AluOpType

NCORES = 8
D = 1024
DFF = 3072
SEQ = 2048
PB = 4
SB_ = 2
DSEQ = 16
PAST = 4096
NH = 8
HD = 128
TN = 512
EPS = 1e-6
NSLOT = 4
NEG = -30000.0
DEBUG = False


class Buf:
    __slots__ = ("name", "last_w", "readers", "excl")

    def __init__(self, name, excl=False):
        self.name = name
        self.last_w = None
        self.readers = {}
        self.excl = excl


class Op:
    __slots__ = ("eng", "fn", "deps", "signal", "ev", "idx", "is_dma", "dsem")


class Prog:
    ENGS = ("pe", "act", "dve", "pool", "sp")

    def __init__(self, nc, n_generic_dma_sems=24):
        self.nc = nc
        self.ops = []
        self.dma_sem_state = {}
        self.n_generic = n_generic_dma_sems
        self.generic_i = 0
        self.n_unique = 0
        self.chains = {}

    def buf(self, name, excl=False):
        return Buf(name, excl)

    def _add(self, eng, fn, reads, writes, is_dma=False, dsem=None, after=None):
        op = Op()
        op.eng = eng
        op.fn = fn
        op.signal = False
        op.ev = None
        op.idx = len(self.ops)
        op.is_dma = is_dma
        op.dsem = dsem
        rd = [b for b in reads if not b.excl]
        wr = list(writes) + [b for b in reads if b.excl]
        deps = {}
        for b in rd:
            if b.last_w is not None:
                deps[b.last_w.idx] = b.last_w
        for b in wr:
            if b.last_w is not None:
                deps[b.last_w.idx] = b.last_w
            for r in b.readers.values():
                deps[r.idx] = r
        if after is not None:
            deps[after.idx] = after
        if is_dma:
            st = self.dma_sem_state.setdefault(dsem, [0, None])
            if st[1] is not None:
                deps[st[1].idx] = st[1]
            st[0] += 1
            st[1] = op
            op.ev = (dsem, 16 * st[0])
        dl = []
        for d in deps.values():
            if d is op:
                continue
            if (not d.is_dma) and (not is_dma) and d.eng == "pe" and eng == "pe":
                continue
            if not d.is_dma:
                d.signal = True
            dl.append(d)
        op.deps = dl
        key = ("dma", op.idx) if is_dma else eng
        for b in rd:
            b.readers[key] = op
        for b in wr:
            b.last_w = op
            b.readers = {}
        self.ops.append(op)
        return op

    def op(self, eng, fn, reads=(), writes=()):
        return self._add(eng, fn, reads, writes)

    def dma(self, queue, fn, reads=(), writes=(), sem=None, chain=None):
        after = None
        if chain is not None:
            after = self.chains.get(chain)
        if queue == "pool":
            sem = "u%d" % self.n_unique
            self.n_unique += 1
        elif sem is None:
            sem = "g%d" % self.generic_i
            self.generic_i = (self.generic_i + 1) % self.n_generic
        op = self._add(queue, fn, reads, writes, is_dma=True, dsem=sem, after=after)
        if chain is not None:
            self.chains[chain] = op
        return op

    def emit(self):
        nc = self.nc
        cnt = {e: 0 for e in self.ENGS}
        for op in self.ops:
            if op.is_dma:
                continue
            if op.signal:
                cnt[op.eng] += 1
                op.ev = ("eng_" + op.eng, cnt[op.eng])
        sem_names = ["eng_" + e for e in self.ENGS] + list(self.dma_sem_state.keys())
        with contextlib.ExitStack() as st:
            sems = {}
            for n in sem_names:
                sems[n] = st.enter_context(nc.semaphore("s_" + n))
            block = st.enter_context(nc.Block())
            per_eng = {e: [] for e in self.ENGS}
            for op in self.ops:
                per_eng[op.eng].append(op)
            finals = [(n, 16 * s[0]) for n, s in self.dma_sem_state.items()]

            def run(e, ename):
                known = {}
                for op in per_eng[ename]:
                    for d in op.deps:
                        sn, val = d.ev
                        if known.get(sn, 0) >= val:
                            continue
                        known[sn] = val
                        e.wait_ge(sems[sn], val)
                    ins = op.fn(e)
                    if op.is_dma:
                        ins.then_inc(sems[op.dsem], 16)
                    elif op.signal:
                        ins.then_inc(sems["eng_" + ename], 1)
                if ename == "sp":
                    for n, v in finals:
                        if v > 0:
                            e.wait_ge(sems[n], v)

            @block.tensor
            def _(e):
                run(e, "pe")

            @block.scalar
            def _(e):
                run(e, "act")

            @block.vector
            def _(e):
                run(e, "dve")

            @block.gpsimd
            def _(e):
                run(e, "pool")

            @block.sync
            def _(e):
                run(e, "sp")


class Ring:
    def __init__(self, items):
        self.items = items
        self.i = 0

    def next(self):
        it = self.items[self.i]
        self.i = (self.i + 1) % len(self.items)
        return it


class TileCtx:
    pass


class Builder:
    def __init__(self, nc):
        self.nc = nc
        self.P = Prog(nc)
        self.st = contextlib.ExitStack()
        self.wq = []
        self.w_issued = 0
        self.w_used = 0

    def sb(self, name, shape, dt):
        return self.st.enter_context(self.nc.sbuf_tensor(name, shape, dt))

    def dram_in(self, name, shape):
        return self.nc.dram_tensor(name, list(shape), F32, kind="ExternalInput").ap()

    def dram_out(self, name, shape):
        return self.nc.dram_tensor(name, list(shape), F32, kind="ExternalOutput").ap()

    def tile(self, name, shape, dt, nbuf=None):
        t = self.sb(name, shape, dt)
        if nbuf is None:
            return t, self.P.buf(name)
        return t, [self.P.buf("%s_%d" % (name, i)) for i in range(nbuf)]

    def ring(self, name, shape, dt, n):
        return Ring([(self.sb("%s%d" % (name, i), shape, dt), self.P.buf("%s%d" % (name, i))) for i in range(n)])

    def declare(self):
        nc = self.nc
        d = {}
        d["xp"] = self.dram_in("xp", [PB, SEQ, D])
        d["xs"] = self.dram_in("xs", [SB_ * DSEQ, D])
        d["s_h"] = self.dram_in("s_h", [SB_, D])
        d["s_conv"] = self.dram_in("s_conv", [SB_, 3, D])
        d["s_fconv"] = self.dram_in("s_fconv", [2, SB_, 2, DFF])
        d["ck"] = self.dram_in("ck", [SB_, PAST, D])
        d["cv"] = self.dram_in("cv", [SB_, PAST, D])
        d["a_norm"] = self.dram_in("a_norm", [D])
        d["a_w_in"] = self.dram_in("a_w_in", [D, 2 * D])
        d["a_conv_w"] = self.dram_in("a_conv_w", [4, D])
        d["a_conv_b"] = self.dram_in("a_conv_b", [D])
        d["a_wr"] = self.dram_in("a_wr", [16, 64, 64])
        d["a_br"] = self.dram_in("a_br", [D])
        d["a_wi"] = self.dram_in("a_wi", [16, 64, 64])
        d["a_bi"] = self.dram_in("a_bi", [D])
        d["a_lambda"] = self.dram_in("a_lambda", [D])
        d["a_w_out"] = self.dram_in("a_w_out", [D, D])
        d["kv_norm"] = self.dram_in("kv_norm", [D])
        d["w_kv"] = self.dram_in("w_kv", [D, 2 * D])
        d["k_norm"] = self.dram_in("k_norm", [HD])
        d["b_norm"] = self.dram_in("b_norm", [D])
        d["b_wq"] = self.dram_in("b_wq", [D, D])
        d["q_norm"] = self.dram_in("q_norm", [HD])
        d["b_wo"] = self.dram_in("b_wo", [D, D])
        d["f_norm"] = self.dram_in("f_norm", [2, D])
        d["f_w_up"] = self.dram_in("f_w_up", [2, D, 2 * DFF])
        d["f_conv_w"] = self.dram_in("f_conv_w", [2, 3, DFF])
        d["f_conv_b"] = self.dram_in("f_conv_b", [2, DFF])
        d["f_w_down"] = self.dram_in("f_w_down", [2, DFF, D])
        d["out_norm"] = self.dram_in("out_norm", [D])
        d["y_p"] = self.dram_out("y_p", [PB, SEQ, D])
        d["y_s"] = self.dram_out("y_s", [SB_ * DSEQ, D])
        d["p_h"] = self.dram_out("p_h", [PB, D])
        d["p_conv"] = self.dram_out("p_conv", [PB, 3, D])
        d["p_fconv"] = self.dram_out("p_fconv", [2, PB, 2, DFF])
        d["p_k"] = self.dram_out("p_k", [PB, SEQ, D])
        d["p_v"] = self.dram_out("p_v", [PB, SEQ, D])
        d["o_h"] = self.dram_out("o_h", [SB_, D])
        d["o_conv"] = self.dram_out("o_conv", [SB_, 3, D])
        d["o_fconv"] = self.dram_out("o_fconv", [2, SB_, 2, DFF])
        d["o_k"] = self.dram_out("o_k", [SB_ * DSEQ, D])
        d["o_v"] = self.dram_out("o_v", [SB_ * DSEQ, D])
        def scratch(name, shape):
            return nc.dram_tensor(name, list(shape), BF16, kind="Internal").ap()
        d["w_in_b"] = scratch("w_in_b", [D, 2 * D])
        d["w_out_b"] = scratch("w_out_b", [D, D])
        d["w_kv_b"] = scratch("w_kv_b", [D, 2 * D])
        d["wq_b"] = scratch("wq_b", [D, D])
        d["wo_b"] = scratch("wo_b", [D, D])
        d["w_up_b"] = [scratch("w_up_b%d" % l, [D, 2 * DFF]) for l in range(2)]
        d["w_dn_b"] = [scratch("w_dn_b%d" % l, [DFF, D]) for l in range(2)]
        self.d = d

    def setup(self):
        P, d = self.P, self.d
        sb = self.sb
        self.banks = []
        for i in range(7):
            t = self.st.enter_context(self.nc.psum_tensor("bank%d" % i, [128, 512], F32))
            self.banks.append((t, P.buf("bank%d" % i, excl=True)))
        t = self.st.enter_context(self.nc.psum_tensor("bankb", [128, 1024], BF16))
        self.bankb = (t, P.buf("bankb", excl=True))

        self.x, self.xb = self.tile("x", [128, 8, TN], F32, 8)
        self.xn, self.xnb = self.tile("xn", [128, 8, TN], BF16, 8)
        self.bufA, self.bufAb = self.tile("bufA", [128, 8, TN], BF16, 8)
        self.hmid, self.hmidb = self.tile("hmid", [128, 24, TN], BF16, 24)
        self.KT, self.KTb = self.tile("KT", [128, NH, SEQ], BF16, SEQ // 128)
        self.Vb, self.Vbb = self.tile("Vb", [128, SEQ // 128, D], BF16, SEQ // 128)
        self.RS, self.RSb = self.tile("RS", [128, 128], F32R)
        self.wslots = [(sb("wslot%d" % i, [128, 8, 512], BF16), P.buf("wslot%d" % i)) for i in range(NSLOT)]
        self.rowbufs = self.ring("rowbuf", [128, D], F32, 3)
        self.xin = self.rowbufs
        self.tf = self.ring("tf", [128, TN + 4], F32, 6)
        self.tr = self.ring("tr", [128, TN], F32R, 6)
        self.tb = self.ring("tb", [128, TN], BF16, 3)
        self.xpre, self.xpreb = self.tile("xpre", [128, D], F32)
        self.small = self.ring("small", [128, 64], F32, 4)
        self.rhist, self.rhistb = self.tile("rhist", [128, 8, 2, 3], F32, 8)
        self.hstate, self.hstateb = self.tile("hstate", [128, 8, 2], F32, 8)
        self.fhist = []
        for l in range(2):
            self.fhist.append(self.tile("fhist%d" % l, [128, 24, 2, 2], F32, 24))

        self.ident_f, self.ident_fb = self.tile("ident_f", [128, 128], F32)
        self.ident_b, self.ident_bb = self.tile("ident_b", [128, 128], BF16)
        self.ones_dm, self.ones_dmb = self.tile("ones_dm", [128, 128], BF16)
        self.ones_hd, self.ones_hdb = self.tile("ones_hd", [128, 128], BF16)
        self.negtri, self.negtrib = self.tile("negtri", [128, 128], F32R)
        self.negones, self.negonesb = self.tile("negones", [128, 128], F32R)
        self.M01, self.M01b = self.tile("M01", [128, 128], F32)
        self.MB, self.MBb = self.tile("MB", [128, 128], BF16)
        self.M01s, self.M01sb = self.tile("M01s", [128, 128], F32)
        self.MBs, self.MBsb = self.tile("MBs", [128, 128], BF16)
        self.gcol, self.gcolb = self.tile("gcol", [128, 6, 8], F32)
        self.acw, self.acwb = self.tile("acw", [128, 8, 4], F32)
        self.acols, self.acolsb = self.tile("acols", [128, 8, 8], F32)
        self.fcw, self.fcwb = self.tile("fcw", [128, 2, 24, 3], F32)
        self.fcb, self.fcbb = self.tile("fcb", [128, 2, 24], F32)
        self.knbc, self.knbcb = self.tile("knbc", [128, 128], F32)
        self.qg, self.qgb = self.tile("qg", [128, 2], F32)
        self.WRI, self.WRIb = self.tile("WRI", [128, 16, 128], BF16)
        self.lnhalf, self.lnhalfb = self.tile("lnhalf", [128, 1], F32)
        self.mhalf, self.mhalfb = self.tile("mhalf", [128, 8], F32)

        def pool(fn, reads=(), writes=()):
            P.op("pool", fn, reads, writes)

        idf, idb = self.ident_f, self.ident_b
        pool(lambda e: e.memset(idf[:], 1.0), writes=[self.ident_fb])
        pool(lambda e: e.affine_select(out=idf[:], in_=idf[:], pattern=[[-1, 128]], compare_op=ALU.is_equal,
                                       fill=0.0, base=0, channel_multiplier=1),
             reads=[self.ident_fb], writes=[self.ident_fb])
        pool(lambda e: e.tensor_copy(out=idb[:], in_=idf[:]), reads=[self.ident_fb], writes=[self.ident_bb])
        pool(lambda e: e.memset(self.ones_dm[:], 1.0 / D), writes=[self.ones_dmb])
        pool(lambda e: e.memset(self.ones_hd[:], 1.0 / HD), writes=[self.ones_hdb])
        self.tmpc, self.tmpcb = self.tile("tmpc", [128, 128], F32)
        tmpc = self.tmpc
        nt = self.negtri
        pool(lambda e: e.memset(tmpc[:], -1.0), writes=[self.tmpcb])
        pool(lambda e: e.affine_select(out=self.negones[:], in_=tmpc[:], pattern=[[0, 128]], compare_op=ALU.is_ge,
                                       fill=0.0, base=1, channel_multiplier=0),
             reads=[self.tmpcb], writes=[self.negonesb])
        pool(lambda e: e.affine_select(out=nt[:], in_=tmpc[:], pattern=[[-1, 128]], compare_op=ALU.is_ge,
                                       fill=0.0, base=0, channel_multiplier=1),
             reads=[self.tmpcb], writes=[self.negtrib])
        m01, mb = self.M01, self.MB
        pool(lambda e: e.memset(m01[:], 1.0), writes=[self.M01b])
        pool(lambda e: e.affine_select(out=m01[:], in_=m01[:], pattern=[[1, 128]], compare_op=ALU.is_gt,
                                       fill=0.0, base=0, channel_multiplier=-1),
             reads=[self.M01b], writes=[self.M01b])
        pool(lambda e: e.memset(mb[:], 0.0), writes=[self.MBb])
        pool(lambda e: e.affine_select(out=mb[:], in_=mb[:], pattern=[[1, 128]], compare_op=ALU.is_gt,
                                       fill=NEG, base=0, channel_multiplier=-1),
             reads=[self.MBb], writes=[self.MBb])
        m01s, mbs = self.M01s, self.MBs
        pool(lambda e: e.memset(m01s[:], 1.0), writes=[self.M01sb])
        pool(lambda e: e.affine_select(out=m01s[:], in_=m01s[:], pattern=[[0, NH], [1, DSEQ]], compare_op=ALU.is_gt,
                                       fill=0.0, base=0, channel_multiplier=-1),
             reads=[self.M01sb], writes=[self.M01sb])
        pool(lambda e: e.memset(mbs[:], 0.0), writes=[self.MBsb])
        pool(lambda e: e.affine_select(out=mbs[:], in_=mbs[:], pattern=[[0, NH], [1, DSEQ]], compare_op=ALU.is_gt,
                                       fill=NEG, base=0, channel_multiplier=-1),
             reads=[self.MBsb], writes=[self.MBsb])
        pool(lambda e: e.memset(self.WRI[:], 0.0), writes=[self.WRIb])
        pool(lambda e: e.memset(self.lnhalf[:], float(np.log(0.5))), writes=[self.lnhalfb])
        pool(lambda e: e.memset(self.mhalf[:], -0.5), writes=[self.mhalfb])

        self.wsb = {}

        def cast(dst, src, name, sem):
            b = P.buf("scr_" + name)
            dv = dst.rearrange("r (a b) -> r a b", b=1024)
            sv = src.rearrange("r (a b) -> r a b", b=1024)
            P.dma("pool", lambda e: e.dma_start(out=dv, in_=sv), writes=[b], chain=sem)
            self.wsb[name] = b

        self._cast = cast
        self.cast_plan = {
            "lru": [(d["w_dn_b"][0], d["f_w_down"][0], "w_dn0", "castA"), (d["w_kv_b"], d["w_kv"], "w_kv", "castB")],
            "ffn0": [(d["wq_b"], d["b_wq"], "wq", "castA"), (d["wo_b"], d["b_wo"], "wo", "castB")],
            "kv": [(d["w_up_b"][1], d["f_w_up"][1], "w_up1", "castC")],
            "q": [(d["w_dn_b"][1], d["f_w_down"][1], "w_dn1", "castA")],
        }

        cast(d["w_in_b"], d["a_w_in"], "w_in", "castA")
        def small_load(dst_ap, src_ap, wbuf):
            P.dma("sp", lambda e: e.dma_start(out=dst_ap, in_=src_ap, allow_slow_non_contiguous=True), writes=[wbuf])

        norms = [d["a_norm"], d["f_norm"][0], d["kv_norm"], d["b_norm"], d["f_norm"][1], d["out_norm"]]
        for i, nv in enumerate(norms):
            small_load(self.gcol[:, i, :], nv.rearrange("(j p) -> p j", p=128), self.gcolb)
        for k in range(4):
            small_load(self.acw[:, :, k], d["a_conv_w"][k].rearrange("(j p) -> p j", p=128), self.acwb)
        for i, nm in enumerate(["a_conv_b", "a_br", "a_bi", "a_lambda"]):
            small_load(self.acols[:, :, i], d[nm].rearrange("(j p) -> p j", p=128), self.acolsb)
        for l in range(2):
            for k in range(3):
                small_load(self.fcw[:, l, :, k], d["f_conv_w"][l, k].rearrange("(j p) -> p j", p=128), self.fcwb)
            small_load(self.fcb[:, l, :], d["f_conv_b"][l].rearrange("(j p) -> p j", p=128), self.fcbb)
        P.dma("sp", lambda e: e.dma_start(out=self.knbc[:], in_=d["k_norm"].partition_broadcast(128)), writes=[self.knbcb])
        small_load(self.qg[:, 0:1], d["q_norm"].rearrange("(p o) -> p o", o=1), self.qgb)
        for j in range(8):
            for half in range(2):
                n = 2 * j + half
                lo = 64 * half
                P.dma("pool", lambda e, j=j, n=n, lo=lo: e.dma_start(out=self.WRI[lo:lo + 64, j, lo:lo + 64], in_=d["a_wr"][n]),
                      writes=[self.WRIb], chain="casts%d" % (n % 4))
                P.dma("pool", lambda e, j=j, n=n, lo=lo: e.dma_start(out=self.WRI[lo:lo + 64, 8 + j, lo:lo + 64], in_=d["a_wi"][n]),
                      writes=[self.WRIb], chain="casts%d" % (n % 4))
        cast(d["w_out_b"], d["a_w_out"], "w_out", "castB")
        cast(d["w_up_b"][0], d["f_w_up"][0], "w_up0", "castC")

        ac = self.acols
        act = lambda fn, reads=(), writes=(): P.op("act", fn, reads, writes)
        act(lambda e: e.mul(out=ac[:, :, 4], in_=ac[:, :, 1], mul=0.5), reads=[self.acolsb], writes=[self.acolsb])
        act(lambda e: e.mul(out=ac[:, :, 5], in_=ac[:, :, 2], mul=0.5), reads=[self.acolsb], writes=[self.acolsb])
        act(lambda e: e.activation(out=ac[:, :, 6], in_=ac[:, :, 3], func=AF.Exp, scale=-1.0), reads=[self.acolsb], writes=[self.acolsb])
        act(lambda e: e.activation(out=ac[:, :, 7], in_=ac[:, :, 6], func=AF.Ln, bias=1.0), reads=[self.acolsb], writes=[self.acolsb])
        act(lambda e: e.mul(out=ac[:, :, 6], in_=ac[:, :, 7], mul=-4.0), reads=[self.acolsb], writes=[self.acolsb])
        act(lambda e: e.mul(out=ac[:, :, 7], in_=ac[:, :, 7], mul=-8.0), reads=[self.acolsb], writes=[self.acolsb])
        act(lambda e: e.mul(out=self.qg[:, 1:2], in_=self.qg[:, 0:1], mul=float(HD) ** -0.5), reads=[self.qgb], writes=[self.qgb])

    def plan_weights(self, ntiles):
        d = self.d
        reqs = []

        def add(src, rows0, col0, key):
            reqs.append((src, rows0, col0, key))

        for _ in range(ntiles):
            for g in range(2):
                add(d["w_in_b"], 0, D + 512 * g, "w_in")
                add(d["w_in_b"], 0, 512 * g, "w_in")
            for mg in range(2):
                add(d["w_out_b"], 0, 512 * mg, "w_out")
            self._plan_ffn(add, 0)
            for g in range(2):
                add(d["w_kv_b"], 0, 512 * g, "w_kv")
            for g in range(2):
                add(d["w_kv_b"], 0, D + 512 * g, "w_kv")
            for g in range(2):
                add(d["wq_b"], 0, 512 * g, "wq")
            for g in range(2):
                add(d["wo_b"], 0, 512 * g, "wo")
            self._plan_ffn(add, 1)
        self.wq = reqs

    def _plan_ffn(self, add, l):
        d = self.d
        for g in range(6):
            add(d["w_up_b"][l], 0, 512 * g, "w_up%d" % l)
            add(d["w_up_b"][l], 0, DFF + 512 * g, "w_up%d" % l)
        for mg in range(2):
            for ks in range(3):
                add(d["w_dn_b"][l], 1024 * ks, 512 * mg, "w_dn%d" % l)

    def _issue_w(self, upto):
        P = self.P
        while self.w_issued < min(upto, len(self.wq)):
            i = self.w_issued
            src, r0, c0, key = self.wq[i]
            assert key in self.wsb, key
            slot, sbuf_ = self.wslots[i % NSLOT]
            view = src[r0:r0 + 1024, c0:c0 + 512].rearrange("(kc p) c -> p kc c", p=128)
            P.dma("sp", lambda e, slot=slot, view=view: e.dma_start(out=slot[:], in_=view),
                  reads=[self.wsb[key]], writes=[sbuf_], sem="wslot%d" % (i % NSLOT))
            self.w_issued += 1

    def next_w(self):
        i = self.w_used
        if self.w_issued == 0:
            self._issue_w(NSLOT)
        assert i < self.w_issued or i >= len(self.wq), (i, self.w_issued)
        self.w_used += 1
        return self.wslots[i % NSLOT]

    def done_w(self, n=1):
        self.w_released = getattr(self, "w_released", 0) + n
        self._issue_w(self.w_released + NSLOT)

    def hf(self, k):
        ap = self.hmid[:, 2 * k:2 * k + 2, :].rearrange("p a b -> p (a b)").bitcast(F32)
        return ap, [self.hmidb[2 * k], self.hmidb[2 * k + 1]]

    def hb(self, k):
        ap = self.hmid[:, 2 * k:2 * k + 2, :].rearrange("p a b -> p (a b)")
        return ap, [self.hmidb[2 * k], self.hmidb[2 * k + 1]]

    def bank(self, i):
        if i == 7:
            t, b = self.bankb
            return t[:, :].bitcast(F32), b
        return self.banks[i]

    def rmsnorm_to_xn(self, c, gi, inplace=False):
        P = self.P
        N = c.N
        st_bank, st_b = self.bank(6)
        for fc in range(8):
            sq, sqb = self.tb.next()
            P.op("act", lambda e, fc=fc, sq=sq: e.activation(out=sq[:, 0:N], in_=self.x[:, fc, 0:N], func=AF.Square),
                 reads=[self.xb[fc]], writes=[sqb])
            P.op("pe", lambda e, fc=fc, sq=sq: e.matmul(st_bank[:, 0:N], lhsT=self.ones_dm[:], rhs=sq[:, 0:N],
                                                        start=(fc == 0), stop=(fc == 7)),
                 reads=[self.ones_dmb, sqb], writes=[st_b])
        lnv, lnvb = self.tf.next()
        rstd, rstdb = self.tf.next()
        P.op("act", lambda e: e.activation(out=lnv[:, 0:N], in_=st_bank[:, 0:N], func=AF.Ln, bias=EPS), reads=[st_b], writes=[lnvb])
        P.op("act", lambda e: e.activation(out=rstd[:, 0:N], in_=lnv[:, 0:N], func=AF.Exp, scale=-0.5), reads=[lnvb], writes=[rstdb])
        for fc in range(8):
            if not inplace:
                P.op("dve", lambda e, fc=fc: e.scalar_tensor_tensor(out=self.xn[:, fc, 0:N], in0=self.x[:, fc, 0:N],
                                                                    scalar=self.gcol[:, gi, fc:fc + 1], in1=rstd[:, 0:N],
                                                                    op0=ALU.mult, op1=ALU.mult),
                     reads=[self.xb[fc], self.gcolb, rstdb], writes=[self.xnb[fc]])
            else:
                P.op("dve", lambda e, fc=fc: e.scalar_tensor_tensor(out=self.x[:, fc, 0:N], in0=self.x[:, fc, 0:N],
                                                                    scalar=self.gcol[:, gi, fc:fc + 1], in1=rstd[:, 0:N],
                                                                    op0=ALU.mult, op1=ALU.mult),
                     reads=[self.xb[fc], self.gcolb, rstdb], writes=[self.xb[fc]])

    def x_to_fm(self, c):
        P = self.P
        for gi, (c0, nt, src) in enumerate(c.tgs):
            if gi == 0 and getattr(c, "prefetched", False):
                xi, xib = self.xpre, self.xpreb
            else:
                xi, xib = self.rowbufs.next()
                P.dma("act", lambda e, xi=xi, src=src, nt=nt: e.dma_start(out=xi[0:nt, :], in_=src), writes=[xib])
            for half in range(2):
                bt, bb = self.bank((2 * gi + half) % 4)
                for q in range(4):
                    fc = 4 * half + q
                    P.op("pe", lambda e, xi=xi, nt=nt, fc=fc, q=q, bt=bt: e.transpose(out=bt[:, q * 128:q * 128 + nt], in_=xi[0:nt, fc * 128:(fc + 1) * 128],
                                                                                        identity=self.ident_f[0:nt, 0:nt]),
                         reads=[xib, self.ident_fb], writes=[bb])
                src3 = bt[:, :].rearrange("p (q t) -> p q t", q=4)[:, :, 0:nt]
                dst3 = self.x[:, 4 * half:4 * half + 4, c0:c0 + nt]
                wb = [self.xb[4 * half + q] for q in range(4)]
                if half == 0:
                    P.op("act", lambda e, src3=src3, dst3=dst3: e.copy(out=dst3, in_=src3), reads=[bb], writes=wb)
                else:
                    P.op("dve", lambda e, src3=src3, dst3=dst3: e.tensor_copy(out=dst3, in_=src3), reads=[bb], writes=wb)

    def lru_mixer(self, c):
        P = self.P
        N, S, L = c.N, c.nseg, c.L
        ac = self.acols
        wslot = {}
        stA = {}
        stB = {}

        def phaseA(j):
            g, jj = divmod(j, 4)
            if jj == 0:
                wslot["rec"] = self.next_w()
                wslot["gate"] = self.next_w()
            (wr_t, wr_b), (wg_t, wg_b) = wslot["rec"], wslot["gate"]
            Rt, Rb = self.bank((0, 2)[j % 2])
            Gt, Gb = self.bank((1, 3)[j % 2])
            for kc in range(8):
                P.op("pe", lambda e, kc=kc, jj=jj, wr_t=wr_t, Rt=Rt: e.matmul(Rt[:, 0:N], lhsT=wr_t[:, kc, jj * 128:(jj + 1) * 128], rhs=self.xn[:, kc, 0:N],
                                                                              start=(kc == 0), stop=(kc == 7)),
                     reads=[wr_b, self.xnb[kc]], writes=[Rb])
            for kc in range(8):
                P.op("pe", lambda e, kc=kc, jj=jj, wg_t=wg_t, Gt=Gt: e.matmul(Gt[:, 0:N], lhsT=wg_t[:, kc, jj * 128:(jj + 1) * 128], rhs=self.xn[:, kc, 0:N],
                                                                              start=(kc == 0), stop=(kc == 7)),
                     reads=[wg_b, self.xnb[kc]], writes=[Gb])
            if jj == 3:
                self.done_w(2)
            rb_t, rb_b = self.rowbufs.next()
            rb3 = rb_t[:, 0:S * (L + 3)].rearrange("p (s l) -> p s l", s=S)
            P.op("pool", lambda e, j=j, rb3=rb3: e.tensor_copy(out=rb3[:, :, 0:3], in_=self.rhist[:, j, 0:S, :]),
                 reads=[self.rhistb[j]], writes=[rb_b])
            R3 = Rt[:, 0:N].rearrange("p (s l) -> p s l", s=S)
            P.op("act", lambda e, rb3=rb3, R3=R3: e.copy(out=rb3[:, :, 3:3 + L], in_=R3), reads=[Rb], writes=[rb_b])
            c0_t, c0_b = self.hf(2 * (j % 4))
            gg_t, gg_b = self.hf(2 * (j % 4) + 1)
            c03 = c0_t[:, 0:N].rearrange("p (s l) -> p s l", s=S)
            P.op("pool", lambda e, j=j, rb3=rb3, c03=c03: e.tensor_scalar(out=c03, in0=rb3[:, :, 3:3 + L], scalar1=self.acw[:, j, 3:4], scalar2=ac[:, j, 0:1],
                                                                          op0=ALU.mult, op1=ALU.add),
                 reads=[rb_b, self.acwb, self.acolsb], writes=c0_b)
            for k in (2, 1, 0):
                P.op("dve", lambda e, j=j, k=k, rb3=rb3, c03=c03: e.scalar_tensor_tensor(out=c03, in0=rb3[:, :, k:k + L], scalar=self.acw[:, j, k:k + 1],
                                                                                          in1=c03, op0=ALU.mult, op1=ALU.add),
                     reads=[rb_b, self.acwb] + c0_b, writes=c0_b)
            P.op("pool", lambda e, j=j, rb3=rb3: e.tensor_copy(out=self.rhist[:, j, 0:S, :], in_=rb3[:, :, L:L + 3]),
                 reads=[rb_b], writes=[self.rhistb[j]])
            cb_t, cb_b = self.tb.next()
            P.op("pool", lambda e, cb_t=cb_t, c0_t=c0_t: e.tensor_copy(out=cb_t[:, 0:N], in_=c0_t[:, 0:N]), reads=c0_b, writes=[cb_b])
            P.op("act", lambda e, gg_t=gg_t, Gt=Gt: e.activation(out=gg_t[:, 0:N], in_=Gt[:, 0:N], func=AF.Gelu_apprx_tanh), reads=[Gb], writes=gg_b)
            stA[j] = (c0_t, c0_b, gg_t, gg_b, cb_t, cb_b)

        def phaseB1(j):
            c0_t, c0_b, gg_t, gg_b, cb_t, cb_b = stA[j]
            rp_t, rp_b = self.bank((4, 6)[j % 2])
            ip_t, ip_b = self.bank((5, 7)[j % 2])
            P.op("pe", lambda e, j=j, cb_t=cb_t, rp_t=rp_t: e.matmul(rp_t[:, 0:N], lhsT=self.WRI[:, j, :], rhs=cb_t[:, 0:N], start=True, stop=True),
                 reads=[self.WRIb, cb_b], writes=[rp_b])
            P.op("pe", lambda e, j=j, cb_t=cb_t, ip_t=ip_t: e.matmul(ip_t[:, 0:N], lhsT=self.WRI[:, 8 + j, :], rhs=cb_t[:, 0:N], start=True, stop=True),
                 reads=[self.WRIb, cb_b], writes=[ip_b])
            r_t, r_b = self.hf(8 + 2 * (j % 2))
            i_t, i_b = self.hf(9 + 2 * (j % 2))
            P.op("act", lambda e, j=j, r_t=r_t, rp_t=rp_t: e.activation(out=r_t[:, 0:N], in_=rp_t[:, 0:N], func=AF.Tanh, scale=0.5, bias=ac[:, j, 4:5]),
                 reads=[rp_b, self.acolsb], writes=r_b)
            P.op("act", lambda e, j=j, i_t=i_t, ip_t=ip_t: e.activation(out=i_t[:, 0:N], in_=ip_t[:, 0:N], func=AF.Tanh, scale=0.5, bias=ac[:, j, 5:6]),
                 reads=[ip_b, self.acolsb], writes=i_b)
            stB[j] = (r_t, r_b, i_t, i_b)

        def phaseB2(j):
            c0_t, c0_b, gg_t, gg_b, cb_t, cb_b = stA.pop(j)
            r_t, r_b, i_t, i_b = stB.pop(j)
            a_t, a_b = self.tf.next()
            s_t, s_b = self.tf.next()
            P.op("act", lambda e, j=j, a_t=a_t, r_t=r_t: e.activation(out=a_t[:, 0:N], in_=r_t[:, 0:N], func=AF.Exp, scale=ac[:, j, 6:7], bias=ac[:, j, 6:7]),
                 reads=r_b + [self.acolsb], writes=[a_b])
            P.op("pool", lambda e, s_t=s_t, a_t=a_t: e.tensor_tensor(out=s_t[:, 0:N], in0=a_t[:, 0:N], in1=a_t[:, 0:N], op=ALU.mult),
                 reads=[a_b], writes=[s_b])
            P.op("act", lambda e, s_t=s_t: e.activation(out=s_t[:, 0:N], in_=s_t[:, 0:N], func=AF.Ln, scale=-1.0, bias=1.0), reads=[s_b], writes=[s_b])
            P.op("act", lambda e, s_t=s_t: e.activation(out=s_t[:, 0:N], in_=s_t[:, 0:N], func=AF.Exp, scale=0.5, bias=self.lnhalf[:, 0:1]), reads=[s_b, self.lnhalfb], writes=[s_b])
            P.op("dve", lambda e, i_t=i_t, c0_t=c0_t: e.scalar_tensor_tensor(out=i_t[:, 0:N], in0=i_t[:, 0:N], scalar=1.0, in1=c0_t[:, 0:N], op0=ALU.add, op1=ALU.mult),
                 reads=i_b + c0_b, writes=i_b)
            P.op("dve", lambda e, i_t=i_t, s_t=s_t: e.tensor_tensor(out=i_t[:, 0:N], in0=i_t[:, 0:N], in1=s_t[:, 0:N], op=ALU.mult),
                 reads=i_b + [s_b], writes=i_b)
            h_t, h_b = self.tf.next()
            for s in range(S):
                P.op("dve", lambda e, j=j, s=s, h_t=h_t, a_t=a_t, i_t=i_t: e.tensor_tensor_scan(out=h_t[:, s * L:(s + 1) * L], data0=a_t[:, s * L:(s + 1) * L],
                                                                                                data1=i_t[:, s * L:(s + 1) * L], initial=self.hstate[:, j, s:s + 1],
                                                                                                op0=ALU.mult, op1=ALU.add),
                     reads=[a_b, self.hstateb[j]] + i_b, writes=[h_b])
            h3 = h_t[:, 0:N].rearrange("p (s l) -> p s l", s=S)
            P.op("dve", lambda e, j=j, h3=h3: e.tensor_copy(out=self.hstate[:, j, 0:S], in_=h3[:, :, L - 1]),
                 reads=[h_b], writes=[self.hstateb[j]])
            P.op("dve", lambda e, j=j, h_t=h_t, gg_t=gg_t: e.tensor_tensor(out=self.bufA[:, j, 0:N], in0=h_t[:, 0:N], in1=gg_t[:, 0:N], op=ALU.mult),
                 reads=[h_b] + gg_b, writes=[self.bufAb[j]])

        phaseA(0)
        phaseA(1)
        for j in range(0, 8, 2):
            if j + 2 < 8:
                phaseA(j + 2)
            phaseB1(j)
            if j + 3 < 8:
                phaseA(j + 3)
            phaseB1(j + 1)
            phaseB2(j)
            phaseB2(j + 1)
        self.proj_residual(c, self.bufA, self.bufAb)

    def proj_residual(self, c, src, srcb):
        P = self.P
        N = c.N
        for mg in range(2):
            w_t, w_b = self.next_w()
            bks = [self.bank(mm) for mm in range(4)]
            for kc in range(8):
                for mm in range(4):
                    bt, bb = bks[mm]
                    P.op("pe", lambda e, kc=kc, mm=mm, w_t=w_t, bt=bt: e.matmul(bt[:, 0:N], lhsT=w_t[:, kc, mm * 128:(mm + 1) * 128], rhs=src[:, kc, 0:N],
                                                                                start=(kc == 0), stop=(kc == 7)),
                         reads=[w_b, srcb[kc]], writes=[bb])
            for mm in range(4):
                m = 4 * mg + mm
                bt, bb = bks[mm]
                P.op("dve", lambda e, m=m, bt=bt: e.tensor_tensor(out=self.x[:, m, 0:N], in0=bt[:, 0:N], in1=self.x[:, m, 0:N], op=ALU.add),
                     reads=[bb, self.xb[m]], writes=[self.xb[m]])
            self.done_w(1)

    def conv_ffn(self, c, l):
        P = self.P
        N, S, L = c.N, c.nseg, c.L
        self.rmsnorm_to_xn(c, 1 if l == 0 else 4)
        fh_t, fh_b = self.fhist[l]
        bi = 0
        for g in range(6):
            wg_t, wg_b = self.next_w()
            wu_t, wu_b = self.next_w()
            for jj in range(4):
                j = 4 * g + jj
                Gt, Gb = self.bank(bi % 6)
                Ut, Ub = self.bank((bi + 1) % 6)
                bi += 2
                for kc in range(8):
                    P.op("pe", lambda e, kc=kc, jj=jj, wg_t=wg_t, Gt=Gt: e.matmul(Gt[:, 0:N], lhsT=wg_t[:, kc, jj * 128:(jj + 1) * 128], rhs=self.xn[:, kc, 0:N],
                                                                                  start=(kc == 0), stop=(kc == 7)),
                         reads=[wg_b, self.xnb[kc]], writes=[Gb])
                for kc in range(8):
                    P.op("pe", lambda e, kc=kc, jj=jj, wu_t=wu_t, Ut=Ut: e.matmul(Ut[:, 0:N], lhsT=wu_t[:, kc, jj * 128:(jj + 1) * 128], rhs=self.xn[:, kc, 0:N],
                                                                                  start=(kc == 0), stop=(kc == 7)),
                         reads=[wu_b, self.xnb[kc]], writes=[Ub])
                gb_t, gb_b = self.tf.next()
                gb3 = gb_t[:, 0:S * (L + 2)].rearrange("p (s l) -> p s l", s=S)
                P.op("pool", lambda e, j=j, gb3=gb3: e.tensor_copy(out=gb3[:, :, 0:2], in_=fh_t[:, j, 0:S, :]), reads=[fh_b[j]], writes=[gb_b])
                G3 = Gt[:, 0:N].rearrange("p (s l) -> p s l", s=S)
                P.op("act", lambda e, gb3=gb3, G3=G3: e.copy(out=gb3[:, :, 2:2 + L], in_=G3), reads=[Gb], writes=[gb_b])
                c0_t, c0_b = self.tf.next()
                P.op("act", lambda e, j=j, c0_t=c0_t, Gt=Gt: e.activation(out=c0_t[:, 0:N], in_=Gt[:, 0:N], func=AF.Identity,
                                                                          scale=self.fcw[:, l, j, 2:3], bias=self.fcb[:, l, j:j + 1]),
                     reads=[Gb, self.fcwb, self.fcbb], writes=[c0_b])
                c03 = c0_t[:, 0:N].rearrange("p (s l) -> p s l", s=S)
                for k, eng in ((1, "dve"), (0, "dve")):
                    P.op(eng, lambda e, j=j, k=k, gb3=gb3, c03=c03: e.scalar_tensor_tensor(out=c03, in0=gb3[:, :, k:k + L], scalar=self.fcw[:, l, j, k:k + 1],
                                                                                              in1=c03, op0=ALU.mult, op1=ALU.add),
                         reads=[gb_b, self.fcwb, c0_b], writes=[c0_b])
                P.op("pool", lambda e, j=j, gb3=gb3: e.tensor_copy(out=fh_t[:, j, 0:S, :], in_=gb3[:, :, L:L + 2]), reads=[gb_b], writes=[fh_b[j]])
                P.op("act", lambda e, c0_t=c0_t: e.activation(out=c0_t[:, 0:N], in_=c0_t[:, 0:N], func=AF.Gelu_apprx_tanh), reads=[c0_b], writes=[c0_b])
                P.op("dve", lambda e, j=j, c0_t=c0_t, Ut=Ut: e.tensor_tensor(out=self.hmid[:, j, 0:N], in0=Ut[:, 0:N], in1=c0_t[:, 0:N], op=ALU.mult),
                     reads=[Ub, c0_b], writes=[self.hmidb[j]])
            self.done_w(2)
        for mg in range(2):
            bks = [self.bank(i) for i in range(4)]
            for ks in range(3):
                w_t, w_b = self.next_w()
                for mm in range(4):
                    bt, bb = bks[mm]
                    for kc in range(8):
                        P.op("pe", lambda e, kc=kc, mm=mm, ks=ks, w_t=w_t, bt=bt: e.matmul(bt[:, 0:N], lhsT=w_t[:, kc, mm * 128:(mm + 1) * 128],
                                                                                           rhs=self.hmid[:, ks * 8 + kc, 0:N],
                                                                                           start=(ks == 0 and kc == 0), stop=(ks == 2 and kc == 7)),
                             reads=[w_b, self.hmidb[ks * 8 + kc]], writes=[bb])
                self.done_w(1)
            for mm in range(4):
                m = 4 * mg + mm
                bt, bb = bks[mm]
                P.op("dve", lambda e, m=m, bt=bt: e.tensor_tensor(out=self.x[:, m, 0:N], in0=bt[:, 0:N], in1=self.x[:, m, 0:N], op=ALU.add),
                     reads=[bb, self.xb[m]], writes=[self.xb[m]])

    def shared_kv(self, c):
        P = self.P
        N = c.N
        self.rmsnorm_to_xn(c, 2)
        wk = [self.next_w(), self.next_w()]
        kbs = []
        for gi, (c0, nt, _src) in enumerate(c.tgs):
            k_dst, v_dst, kt_dst = c.kv_dst[gi]
            ksb, ksbb = self.rowbufs.next()
            for half in range(2):
                bt, bb = self.bank(half + 4 * (gi % 2))
                w_t, w_b = wk[half]
                for kc in range(8):
                    P.op("pe", lambda e, kc=kc, w_t=w_t, bt=bt, c0=c0, nt=nt: e.matmul(bt[0:nt, :], lhsT=self.xn[:, kc, c0:c0 + nt], rhs=w_t[:, kc, :],
                                                                                       start=(kc == 0), stop=(kc == 7)),
                         reads=[w_b, self.xnb[kc]], writes=[bb])
                P.op("act", lambda e, half=half, bt=bt, ksb=ksb, nt=nt: e.copy(out=ksb[0:nt, half * 512:(half + 1) * 512], in_=bt[0:nt, :]), reads=[bb], writes=[ksbb])
            qk_ = 4 + (gi % 2)
            sq = self.hmid[:, 4 * qk_:4 * qk_ + 4, :].rearrange("p a b -> p (a b)").bitcast(F32)
            sqb = [self.hmidb[4 * qk_ + i_] for i_ in range(4)]
            P.op("act", lambda e, sq=sq, ksb=ksb, nt=nt: e.activation(out=sq[0:nt, :], in_=ksb[0:nt, :], func=AF.Square), reads=[ksbb], writes=sqb)
            ss, ssb = self.small.next()
            P.op("dve", lambda e, ss=ss, sq=sq, nt=nt: e.tensor_reduce(out=ss[0:nt, 0:NH], in_=sq[0:nt, :].rearrange("p (h d) -> p h d", h=NH),
                                                                      axis=mybir.AxisListType.X, op=ALU.add), reads=sqb, writes=[ssb])
            P.op("pool", lambda e, ss=ss, nt=nt: e.tensor_scalar(out=ss[0:nt, 8:16], in0=ss[0:nt, 0:NH], scalar1=1.0 / HD, scalar2=EPS, op0=ALU.mult, op1=ALU.add),
                 reads=[ssb], writes=[ssb])
            P.op("pool", lambda e, ss=ss, nt=nt: e.tensor_tensor(out=ss[0:nt, 16:24], in0=ss[0:nt, 8:16], in1=self.mhalf[0:nt, :], op=ALU.pow),
                 reads=[ssb, self.mhalfb], writes=[ssb])
            for h in range(NH):
                P.op("dve", lambda e, h=h, ksb=ksb, ss=ss, nt=nt: e.scalar_tensor_tensor(out=ksb[0:nt, h * HD:(h + 1) * HD], in0=ksb[0:nt, h * HD:(h + 1) * HD],
                                                                                         scalar=ss[0:nt, 16 + h:17 + h], in1=self.knbc[0:nt, :],
                                                                                         op0=ALU.mult, op1=ALU.mult),
                     reads=[ksbb, ssb, self.knbcb], writes=[ksbb])
            P.dma("sp", lambda e, ksb=ksb, k_dst=k_dst, nt=nt: e.dma_start(out=k_dst, in_=ksb[0:nt, :]), reads=[ksbb])
            kb, kbl = self.hb(4 + gi)
            P.op("dve", lambda e, kb=kb, ksb=ksb, nt=nt: e.tensor_copy(out=kb[0:nt, :], in_=ksb[0:nt, :]), reads=[ksbb], writes=kbl)
            kbs.append((kb, kbl))
        self.done_w(2)
        wv = [self.next_w(), self.next_w()]
        for gi, (c0, nt, _src) in enumerate(c.tgs):
            k_dst, v_dst, kt_dst = c.kv_dst[gi]
            vsb, vsbb = self.rowbufs.next()
            for half in range(2):
                bt, bb = self.bank(2 + half + 4 * (gi % 2))
                w_t, w_b = wv[half]
                for kc in range(8):
                    P.op("pe", lambda e, kc=kc, w_t=w_t, bt=bt, c0=c0, nt=nt: e.matmul(bt[0:nt, :], lhsT=self.xn[:, kc, c0:c0 + nt], rhs=w_t[:, kc, :],
                                                                                       start=(kc == 0), stop=(kc == 7)),
                         reads=[w_b, self.xnb[kc]], writes=[bb])
                P.op("act", lambda e, half=half, bt=bt, vsb=vsb, nt=nt: e.copy(out=vsb[0:nt, half * 512:(half + 1) * 512], in_=bt[0:nt, :]), reads=[bb], writes=[vsbb])
            P.dma("sp", lambda e, vsb=vsb, v_dst=v_dst, nt=nt: e.dma_start(out=v_dst, in_=vsb[0:nt, :]), reads=[vsbb])
            vb_ap, vb_bufs = c.vb_dst[gi]
            P.op("dve", lambda e, vb_ap=vb_ap, vsb=vsb, nt=nt: e.tensor_copy(out=vb_ap, in_=vsb[0:nt, :]), reads=[vsbb], writes=vb_bufs)
        self.done_w(2)
        for gi, (c0, nt, _src) in enumerate(c.tgs):
            k_dst, v_dst, kt_dst = c.kv_dst[gi]
            kb, kbl = kbs[gi]
            if gi % 2 == 0:
                tb_t, tb_b = self.bankb
            else:
                tb_t, tb_b = self.banks[6][0][:, :].bitcast(BF16), self.banks[6][1]
            for h in range(NH):
                P.op("pe", lambda e, h=h, kb=kb, tb_t=tb_t, nt=nt: e.transpose(out=tb_t[:, h * 128:h * 128 + nt], in_=kb[0:nt, h * HD:(h + 1) * HD],
                                                                               identity=self.ident_b[0:nt, 0:nt]),
                     reads=list(kbl) + [self.ident_bb], writes=[tb_b])
            kt_ap, kt_bufs = kt_dst
            P.op("act", lambda e, kt_ap=kt_ap, tb_t=tb_t, nt=nt: e.copy(out=kt_ap, in_=tb_t[:, :].rearrange("p (h t) -> p h t", h=NH)[:, :, 0:nt]),
                 reads=[tb_b], writes=kt_bufs)

    def q_proj(self, c):
        P = self.P
        N = c.N
        self.rmsnorm_to_xn(c, 3)
        st1 = {}
        wcur = {}

        def s1(h):
            hh = h % 4
            if hh == 0:
                wcur["w"] = self.next_w()
            w_t, w_b = wcur["w"]
            bt, bb = self.bank(3 + (h % 2))
            for kc in range(8):
                P.op("pe", lambda e, kc=kc, hh=hh, w_t=w_t, bt=bt: e.matmul(bt[:, 0:N], lhsT=w_t[:, kc, hh * 128:(hh + 1) * 128], rhs=self.xn[:, kc, 0:N],
                                                                            start=(kc == 0), stop=(kc == 7)),
                     reads=[w_b, self.xnb[kc]], writes=[bb])
            if hh == 3:
                self.done_w(1)
            qf, qfb = self.tf.next()
            sq, sqb = self.tb.next()
            P.op("act", lambda e, qf=qf, bt=bt: e.copy(out=qf[:, 0:N], in_=bt[:, 0:N]), reads=[bb], writes=[qfb])
            P.op("act", lambda e, sq=sq, bt=bt: e.activation(out=sq[:, 0:N], in_=bt[:, 0:N], func=AF.Square), reads=[bb], writes=[sqb])
            st1[h] = (qf, qfb, sq, sqb)

        def s2(h):
            qf, qfb, sq, sqb = st1.pop(h)
            st, stb = self.bank(6)
            P.op("pe", lambda e, sq=sq, st=st: e.matmul(st[:, 0:N], lhsT=self.ones_hd[:], rhs=sq[:, 0:N], start=True, stop=True),
                 reads=[self.ones_hdb, sqb], writes=[stb])
            rs, rsb = self.tf.next()
            P.op("act", lambda e, rs=rs, st=st: e.activation(out=rs[:, 0:N], in_=st[:, 0:N], func=AF.Ln, bias=EPS), reads=[stb], writes=[rsb])
            P.op("act", lambda e, rs=rs: e.activation(out=rs[:, 0:N], in_=rs[:, 0:N], func=AF.Exp, scale=-0.5), reads=[rsb], writes=[rsb])
            P.op("dve", lambda e, h=h, qf=qf, rs=rs: e.scalar_tensor_tensor(out=self.bufA[:, h, 0:N], in0=qf[:, 0:N], scalar=self.qg[:, 1:2], in1=rs[:, 0:N],
                                                                            op0=ALU.mult, op1=ALU.mult),
                 reads=[qfb, rsb, self.qgb], writes=[self.bufAb[h]])

        s1(0)
        for h in range(NH):
            if h + 1 < NH:
                s1(h + 1)
            s2(h)

    def zero_fill_r(self, ap, wbufs):
        w = ap.shape[-1]
        self.P.op("pool", lambda e: e.affine_select(out=ap, in_=self.tmpc[:, 0:w], pattern=[[0, w]], compare_op=ALU.is_gt,
                                                    fill=0.0, base=0, channel_multiplier=0),
                  reads=[self.tmpcb], writes=wbufs)

    def attn_prompt(self, c):
        P = self.P
        N = c.N
        qt = c.qt
        nchunk = 4 * qt + 4
        items = []
        for h in range(NH):
            for idx, ci in enumerate(range(nchunk - 1, -1, -1)):
                i = ci - 4 * qt
                items.append(dict(h=h, ci=ci, first=(idx == 0), last=(ci == 0), i=i, qlo=max(0, i) * 128))
        n = len(items)

        def qk_sp(k):
            it = items[k]
            h, ci, qlo = it["h"], it["ci"], it["qlo"]
            zt, zb = self.bank(k % 3)
            it["z"] = (zt, zb)
            P.op("pe", lambda e: e.matmul(zt[:, qlo:N], lhsT=self.KT[:, h, ci * 128:(ci + 1) * 128], rhs=self.bufA[:, h, qlo:N], start=True, stop=True),
                 reads=[self.KTb[ci], self.bufAb[h]], writes=[zb])
            et, etb = self.tf.next()
            sp, spb = self.tr.next()
            it["sp"] = (sp, spb)
            P.op("act", lambda e: e.activation(out=et[:, qlo:N], in_=zt[:, qlo:N], func=AF.Exp), reads=[zb], writes=[etb])
            P.op("act", lambda e: e.activation(out=sp[:, qlo:N], in_=et[:, qlo:N], func=AF.Ln, bias=1.0), reads=[etb], writes=[spb])
            if it["i"] >= 0:
                P.op("dve", lambda e: e.tensor_tensor(out=sp[:, qlo:qlo + 128], in0=sp[:, qlo:qlo + 128], in1=self.M01[:], op=ALU.mult),
                     reads=[spb, self.M01b], writes=[spb])
            if not it["last"]:
                if it["first"]:
                    if qlo >= 128:
                        self.zero_fill_r(sp[:, qlo - 128:qlo], [spb])
                    it["rs"] = (sp, spb)
                else:
                    prs, prsb = items[k - 1]["rs"]
                    rs, rsb = self.tr.next()
                    if qlo >= 128:
                        self.zero_fill_r(rs[:, qlo - 128:qlo], [rsb])
                    P.op("dve", lambda e: e.tensor_tensor(out=rs[:, qlo:N], in0=prs[:, qlo:N], in1=sp[:, qlo:N], op=ALU.add),
                         reads=[prsb, spb], writes=[rsb])
                    it["rs"] = (rs, rsb)

        def cum_w(k):
            it = items[k]
            qlo = it["qlo"]
            zt, zb = it["z"]
            sp, spb = it["sp"]
            first = it["first"]
            if it["i"] >= 0:
                P.op("pe", lambda e: e.matmul(zt[:, qlo:qlo + 128], lhsT=self.ident_b[:], rhs=self.MB[:], start=False, stop=False, skip_group_check=True),
                     reads=[self.ident_bb, self.MBb], writes=[zb])
            P.op("pe", lambda e: e.matmul(zt[:, qlo:N], lhsT=self.negtri[:], rhs=sp[:, qlo:N], start=False, stop=first,
                                          skip_group_check=True),
                 reads=[self.negtrib, spb], writes=[zb])
            if not first:
                prs, prsb = items[k - 1]["rs"]
                P.op("pe", lambda e: e.matmul(zt[:, qlo:N], lhsT=self.negones[:], rhs=prs[:, qlo:N], start=False, stop=True,
                                              skip_group_check=True),
                     reads=[self.negonesb, prsb], writes=[zb])
            wt, wtb = self.tb.next()
            it["w"] = (wt, wtb)
            if qlo > 0:
                P.op("pool", lambda e: e.memset(wt[:, 0:qlo], 0.0), writes=[wtb])
            P.op("act", lambda e: e.activation(out=wt[:, qlo:N], in_=zt[:, qlo:N], func=AF.Exp), reads=[zb], writes=[wtb])

        def pv(k):
            it = items[k]
            h, ci = it["h"], it["ci"]
            wt, wtb = it["w"]
            ot, otb = self.bank(5 if h % 2 == 0 else 6)
            P.op("pe", lambda e: e.matmul(ot[:, 0:N], lhsT=self.Vb[:, ci, h * HD:(h + 1) * HD], rhs=wt[:, 0:N], start=it["first"], stop=it["last"]),
                 reads=[self.Vbb[ci], wtb], writes=[otb])
            if it["last"]:
                P.op("dve", lambda e: e.tensor_copy(out=self.hmid[:, h, 0:N], in_=ot[:, 0:N]), reads=[otb], writes=[self.hmidb[h]])

        for step in range(n + 2):
            if step < n:
                qk_sp(step)
            if 0 <= step - 1 < n:
                cum_w(step - 1)
            if 0 <= step - 2 < n:
                pv(step - 2)

    def attn_sample(self, c):
        P = self.P
        d = self.d
        NQ = NH * DSEQ
        hbi = 0
        for s in range(SB_):
            ot, otb = self.bank(5 + s)
            qcol = s * DSEQ
            nchunk = PAST // 128 + 1
            first = True
            for ci in range(nchunk - 1, -1, -1):
                new = (ci == nchunk - 1)
                if new:
                    rows = DSEQ
                    kt_h = lambda h, qcol=qcol: self.KT[:, h, qcol:qcol + DSEQ]
                    kt_bufs = [self.KTb[0]]
                    v_h = lambda h, s=s: self.Vb[0:DSEQ, s, h * HD:(h + 1) * HD]
                    v_bufs = [self.Vbb[s]]
                else:
                    rows = 128
                    kf, kfb = self.rowbufs.next()
                    P.dma("sp", lambda e, kf=kf, s=s, ci=ci: e.dma_start(out=kf[:], in_=d["ck"][s, ci * 128:(ci + 1) * 128, :]), writes=[kfb])
                    vf, vfb = self.rowbufs.next()
                    P.dma("sp", lambda e, vf=vf, s=s, ci=ci: e.dma_start(out=vf[:], in_=d["cv"][s, ci * 128:(ci + 1) * 128, :]), writes=[vfb])
                    kb, kbl = self.hb(4 + (hbi % 8))
                    hbi += 1
                    P.op("dve", lambda e, kb=kb, kf=kf: e.tensor_copy(out=kb, in_=kf[:]), reads=[kfb], writes=kbl)
                    tb_t, tb_b = self.bankb
                    for h in range(NH):
                        P.op("pe", lambda e, h=h, kb=kb, tb_t=tb_t: e.transpose(out=tb_t[:, h * 128:(h + 1) * 128], in_=kb[:, h * HD:(h + 1) * HD], identity=self.ident_b[:]),
                             reads=list(kbl) + [self.ident_bb], writes=[tb_b])
                    kc_r, kc_bl = self.hb(4 + (hbi % 8))
                    hbi += 1
                    kc_t = kc_r.rearrange("p (h t) -> p h t", h=NH)
                    P.op("dve", lambda e, kc_r=kc_r, tb_t=tb_t: e.tensor_copy(out=kc_r, in_=tb_t[:, :]),
                         reads=[tb_b], writes=kc_bl)
                    vb, vbl = self.hb(4 + (hbi % 8))
                    hbi += 1
                    P.op("pool", lambda e, vb=vb, vf=vf: e.tensor_copy(out=vb, in_=vf[:]), reads=[vfb], writes=vbl)
                    kt_h = lambda h, kc_t=kc_t: kc_t[:, h, :]
                    kt_bufs = list(kc_bl)
                    v_h = lambda h, vb=vb: vb[:, h * HD:(h + 1) * HD]
                    v_bufs = list(vbl)
                zt, zb = self.bank(ci % 3)
                for h in range(NH):
                    P.op("pe", lambda e, h=h, zt=zt, kt_h=kt_h, rows=rows, qcol=qcol: e.matmul(zt[0:rows, h * DSEQ:(h + 1) * DSEQ], lhsT=kt_h(h), rhs=self.bufA[:, h, qcol:qcol + DSEQ],
                                                                                    start=(h == 0), stop=(h == NH - 1), skip_group_check=True),
                         reads=kt_bufs + [self.bufAb[h]], writes=[zb])
                et, etb = self.tf.next()
                sp, spb = self.tr.next()
                P.op("act", lambda e, et=et, zt=zt, rows=rows: e.activation(out=et[0:rows, 0:NQ], in_=zt[0:rows, 0:NQ], func=AF.Exp), reads=[zb], writes=[etb])
                P.op("act", lambda e, et=et, sp=sp, rows=rows: e.activation(out=sp[0:rows, 0:NQ], in_=et[0:rows, 0:NQ], func=AF.Ln, bias=1.0), reads=[etb], writes=[spb])
                if new:
                    P.op("dve", lambda e, sp=sp: e.tensor_tensor(out=sp[0:DSEQ, 0:NQ], in0=sp[0:DSEQ, 0:NQ], in1=self.M01s[0:DSEQ, :], op=ALU.mult),
                         reads=[spb, self.M01sb], writes=[spb])
                    P.op("pe", lambda e, zt=zt: e.matmul(zt[0:DSEQ, 0:NQ], lhsT=self.ident_b[0:DSEQ, 0:DSEQ], rhs=self.MBs[0:DSEQ, :], start=False, stop=False,
                                                         skip_group_check=True),
                         reads=[self.ident_bb, self.MBsb], writes=[zb])
                P.op("pe", lambda e, zt=zt, sp=sp, rows=rows, first=first: e.matmul(zt[0:rows, 0:NQ], lhsT=self.negtri[0:rows, 0:rows], rhs=sp[0:rows, 0:NQ], start=False, stop=first,
                                                                                    skip_group_check=True),
                     reads=[self.negtrib, spb], writes=[zb])
                if not first:
                    P.op("pe", lambda e, zt=zt: e.matmul(zt[:, 0:NQ], lhsT=self.negones[:], rhs=self.RS[:, 0:NQ], start=False, stop=True, skip_group_check=True),
                         reads=[self.negonesb, self.RSb], writes=[zb])
                wt, wtb = self.tb.next()
                P.op("act", lambda e, wt=wt, zt=zt, rows=rows: e.activation(out=wt[0:rows, 0:NQ], in_=zt[0:rows, 0:NQ], func=AF.Exp), reads=[zb], writes=[wtb])
                if first:
                    self.zero_fill_r(self.RS[:, 0:NQ], [self.RSb])
                    P.op("pool", lambda e, sp=sp: e.tensor_copy(out=self.RS[0:DSEQ, 0:NQ], in_=sp[0:DSEQ, 0:NQ]), reads=[spb], writes=[self.RSb])
                elif ci > 0:
                    P.op("dve", lambda e, sp=sp: e.tensor_tensor(out=self.RS[:, 0:NQ], in0=self.RS[:, 0:NQ], in1=sp[:, 0:NQ], op=ALU.add),
                         reads=[spb, self.RSb], writes=[self.RSb])
                for h in range(NH):
                    P.op("pe", lambda e, h=h, ot=ot, wt=wt, v_h=v_h, rows=rows, first=first: e.matmul(ot[:, h * DSEQ:(h + 1) * DSEQ], lhsT=v_h(h), rhs=wt[0:rows, h * DSEQ:(h + 1) * DSEQ],
                                                                                                      start=(first and h == 0), stop=(ci == 0 and h == NH - 1), skip_group_check=True),
                         reads=v_bufs + [wtb], writes=[otb])
                first = False
            P.op("act", lambda e, ot=ot, qcol=qcol: e.copy(out=self.hmid[:, 0:NH, qcol:qcol + DSEQ], in_=ot[:, 0:NQ].rearrange("p (h q) -> p h q", h=NH)),
                 reads=[otb], writes=[self.hmidb[h] for h in range(NH)])

    def final_out(self, c):
        P = self.P
        N = c.N
        self.rmsnorm_to_xn(c, 5, inplace=True)
        for gi, (c0, nt, _src) in enumerate(c.tgs):
            ysb, ysbb = self.rowbufs.next()
            for half in range(2):
                bt, bb = self.bank(half)
                for q in range(4):
                    fc = half * 4 + q
                    P.op("pe", lambda e, q=q, fc=fc, bt=bt, c0=c0, nt=nt: e.transpose(out=bt[0:nt, q * 128:(q + 1) * 128], in_=self.x[:, fc, c0:c0 + nt], identity=self.ident_f[:]),
                         reads=[self.xb[fc], self.ident_fb], writes=[bb])
                if half == 0:
                    P.op("act", lambda e, bt=bt, ysb=ysb, nt=nt: e.copy(out=ysb[0:nt, 0:512], in_=bt[0:nt, :]), reads=[bb], writes=[ysbb])
                else:
                    P.op("dve", lambda e, bt=bt, ysb=ysb, nt=nt: e.tensor_copy(out=ysb[0:nt, 512:1024], in_=bt[0:nt, :]), reads=[bb], writes=[ysbb])
            dst = c.y_dst[gi]
            P.dma("sp", lambda e, ysb=ysb, dst=dst, nt=nt: e.dma_start(out=dst, in_=ysb[0:nt, :]), reads=[ysbb])

    def init_states_zero(self):
        P = self.P
        P.op("pool", lambda e: e.memset(self.rhist[:], 0.0), writes=self.rhistb)
        P.op("pool", lambda e: e.memset(self.hstate[:], 0.0), writes=self.hstateb)
        for l in range(2):
            t, b = self.fhist[l]
            P.op("pool", lambda e, t=t: e.memset(t[:], 0.0), writes=b)

    def init_states_sample(self):
        P, d = self.P, self.d
        for s in range(SB_):
            for k in range(3):
                P.dma("sp", lambda e, s=s, k=k: e.dma_start(out=self.rhist[:, :, s, k], in_=d["s_conv"][s, k].rearrange("(j p) -> p j", p=128),
                                                            allow_slow_non_contiguous=True), writes=self.rhistb)
            P.dma("sp", lambda e, s=s: e.dma_start(out=self.hstate[:, :, s], in_=d["s_h"][s].rearrange("(j p) -> p j", p=128),
                                                   allow_slow_non_contiguous=True), writes=self.hstateb)
            for l in range(2):
                t, b = self.fhist[l]
                for k in range(2):
                    P.dma("sp", lambda e, s=s, l=l, t=t, k=k: e.dma_start(out=t[:, :, s, k], in_=d["s_fconv"][l, s, k].rearrange("(j p) -> p j", p=128),
                                                                          allow_slow_non_contiguous=True), writes=b)

    def write_states(self, seg, dst_h, dst_conv, dst_fconv):
        P = self.P
        bt, bb = self.bank(0)
        stg, stgb = self.rowbufs.next()
        P.op("pe", lambda e, bt=bt: e.transpose(out=bt[0:8, 0:128], in_=self.hstate[:, :, seg], identity=self.ident_f[:]),
             reads=self.hstateb + [self.ident_fb], writes=[bb])
        P.op("act", lambda e, bt=bt, stg=stg: e.copy(out=stg[0:8, 0:128], in_=bt[0:8, 0:128]), reads=[bb], writes=[stgb])
        P.dma("sp", lambda e, stg=stg: e.dma_start(out=dst_h.rearrange("(j p) -> j p", p=128), in_=stg[0:8, 0:128]), reads=[stgb])
        stg, stgb = self.rowbufs.next()
        for half in range(2):
            bt, bb = self.bank(1 + half)
            for q in range(4):
                j = half * 4 + q
                P.op("pe", lambda e, j=j, q=q, bt=bt: e.transpose(out=bt[0:3, q * 128:(q + 1) * 128], in_=self.rhist[:, j, seg, :], identity=self.ident_f[:]),
                     reads=[self.rhistb[j], self.ident_fb], writes=[bb])
            P.op("act", lambda e, half=half, bt=bt, stg=stg: e.copy(out=stg[0:3, half * 512:(half + 1) * 512], in_=bt[0:3, :]), reads=[bb], writes=[stgb])
        P.dma("sp", lambda e, stg=stg: e.dma_start(out=dst_conv, in_=stg[0:3, :]), reads=[stgb])
        for l in range(2):
            t, b = self.fhist[l]
            for part in range(3):
                stg, stgb = self.rowbufs.next()
                for half in range(2):
                    bt, bb = self.bank(3 + half)
                    for q in range(4):
                        j = part * 8 + half * 4 + q
                        P.op("pe", lambda e, j=j, q=q, bt=bt, t=t: e.transpose(out=bt[0:2, q * 128:(q + 1) * 128], in_=t[:, j, seg, :], identity=self.ident_f[:]),
                             reads=[b[j], self.ident_fb], writes=[bb])
                    P.op("act", lambda e, half=half, bt=bt, stg=stg: e.copy(out=stg[0:2, half * 512:(half + 1) * 512], in_=bt[0:2, :]), reads=[bb], writes=[stgb])
                P.dma("sp", lambda e, stg=stg, l=l, part=part: e.dma_start(out=dst_fconv[l][:, part * 1024:(part + 1) * 1024], in_=stg[0:2, :]), reads=[stgb])

    def dbg(self, c, name, ap, bufs, shape, dt=F32):
        if not getattr(c, "dbg", False):
            return
        o = self.nc.dram_tensor("dbg_" + name, list(shape), dt, kind="ExternalOutput").ap()
        self.P.dma("sp", lambda e: e.dma_start(out=o, in_=ap), reads=bufs)

    def do_casts(self, stage):
        for (dst, src, name, sem) in self.cast_plan.pop(stage, []):
            self._cast(dst, src, name, sem)

    def run_tile(self, c, next_c=None):
        N = c.N
        if getattr(c, "dbg", False):
            o = self.nc.dram_tensor("dbg_win", [1024, 2048], BF16, kind="ExternalOutput").ap()
            self.P.dma("sp", lambda e: e.dma_start(out=o, in_=self.d["w_in_b"]), reads=[self.wsb["w_in"]])
        self.x_to_fm(c)
        self.dbg(c, "x0", self.x[:, :, 0:N], self.xb, [128, 8, N])
        self.rmsnorm_to_xn(c, 0)
        self.dbg(c, "xn0", self.xn[:, :, 0:N], self.xnb, [128, 8, N], BF16)
        self.do_casts("lru")
        self.lru_mixer(c)
        self.dbg(c, "hg", self.bufA[:, :, 0:N], self.bufAb, [128, 8, N], BF16)
        self.dbg(c, "x1", self.x[:, :, 0:N], self.xb, [128, 8, N])
        self.do_casts("ffn0")
        self.conv_ffn(c, 0)
        self.dbg(c, "x2", self.x[:, :, 0:N], self.xb, [128, 8, N])
        self.do_casts("kv")
        self.shared_kv(c)
        self.do_casts("q")
        self.q_proj(c)
        if c.sample:
            self.attn_sample(c)
        else:
            self.attn_prompt(c)
        if next_c is not None:
            c0n, ntn, srcn = next_c.tgs[0]
            self.P.dma("sp", lambda e: e.dma_start(out=self.xpre[0:ntn, :], in_=srcn), writes=[self.xpreb])
            next_c.prefetched = True
        self.proj_residual(c, self.hmid, self.hmidb)
        self.conv_ffn(c, 1)
        self.final_out(c)


def make_prompt_ctx(B, b, qt):
    d = B.d
    c = TileCtx()
    c.sample = False
    c.N, c.nseg, c.L = TN, 1, TN
    c.qt = qt
    t0 = qt * TN
    c.tgs = [(g * 128, 128, d["xp"][b, t0 + g * 128:t0 + (g + 1) * 128, :]) for g in range(4)]
    c.kv_dst = []
    c.vb_dst = []
    c.y_dst = []
    for g in range(4):
        ch = qt * 4 + g
        rows = slice(t0 + g * 128, t0 + (g + 1) * 128)
        c.kv_dst.append((d["p_k"][b, rows, :], d["p_v"][b, rows, :], (B.KT[:, :, ch * 128:(ch + 1) * 128], [B.KTb[ch]])))
        c.vb_dst.append((B.Vb[:, ch, :], [B.Vbb[ch]]))
        c.y_dst.append(d["y_p"][b, rows, :])
    return c


def make_sample_ctx(B):
    d = B.d
    c = TileCtx()
    c.sample = True
    c.N, c.nseg, c.L = SB_ * DSEQ, SB_, DSEQ
    c.qt = 0
    c.tgs = []
    c.kv_dst = []
    c.vb_dst = []
    c.y_dst = []
    for s in range(SB_):
        rows = slice(s * DSEQ, (s + 1) * DSEQ)
        c.tgs.append((s * DSEQ, DSEQ, d["xs"][rows, :]))
        c.kv_dst.append((d["o_k"][rows, :], d["o_v"][rows, :], (B.KT[:, :, s * DSEQ:(s + 1) * DSEQ], [B.KTb[0]])))
        c.vb_dst.append((B.Vb[0:DSEQ, s, :], [B.Vbb[s]]))
        c.y_dst.append(d["y_s"][rows, :])
    return c


def build_nc():
    nc = bass.Bass("TRN2", target_bir_lowering=False)
    B = Builder(nc)
    B.declare()
    with B.st:
        B.setup()
        B.plan_weights(PB * (SEQ // TN) + 1)
        P, d = B.P, B.d
        ctxs = []
        for b in range(PB):
            for qt in range(SEQ // TN):
                c = make_prompt_ctx(B, b, qt)
                c.dbg = DEBUG and b == 0 and qt == 0
                ctxs.append((b, qt, c))
        cs = make_sample_ctx(B)
        for i, (b, qt, c) in enumerate(ctxs):
            if qt == 0:
                B.init_states_zero()
            nxt = ctxs[i + 1][2] if i + 1 < len(ctxs) else cs
            B.run_tile(c, nxt)
            if qt == SEQ // TN - 1:
                B.write_states(0, d["p_h"][b], d["p_conv"][b], [d["p_fconv"][l, b] for l in range(2)])
        B.init_states_sample()
        B.run_tile(cs, None)
        for s_ in range(SB_):
            B.write_states(s_, d["o_h"][s_], d["o_conv"][s_], [d["o_fconv"][l, s_] for l in range(2)])
        P.emit()
    return nc


_NC_CACHE = {}


def kernel(**inputs):
    f32 = lambda a: np.ascontiguousarray(np.asarray(a, dtype=np.float32))
    x_prompt = f32(inputs["x_prompt"])
    x_sample = f32(inputs["x_sample"])
    state_lru_h = f32(inputs["state_lru_h"])
    state_lru_conv = f32(inputs["state_lru_conv"])
    state_ffn_conv = f32(inputs["state_ffn_conv"])
    cache_k = f32(inputs["cache_k"])
    cache_v = f32(inputs["cache_v"])
    shared = {
        "a_norm": f32(inputs["a_norm"])[0], "a_w_in": f32(inputs["a_w_in"])[0], "a_conv_w": f32(inputs["a_conv_w"])[0],
        "a_conv_b": f32(inputs["a_conv_b"])[0], "a_wr": f32(inputs["a_wr"])[0], "a_br": f32(inputs["a_br"])[0],
        "a_wi": f32(inputs["a_wi"])[0], "a_bi": f32(inputs["a_bi"])[0], "a_lambda": f32(inputs["a_lambda"])[0],
        "a_w_out": f32(inputs["a_w_out"])[0], "kv_norm": f32(inputs["kv_norm"]), "w_kv": f32(inputs["w_kv"]),
        "k_norm": f32(inputs["k_norm"]), "b_norm": f32(inputs["b_norm"])[0], "b_wq": f32(inputs["b_wq"])[0],
        "q_norm": f32(inputs["q_norm"])[0], "b_wo": f32(inputs["b_wo"])[0], "f_norm": f32(inputs["f_norm"]),
        "f_w_up": f32(inputs["f_w_up"]), "f_conv_w": f32(inputs["f_conv_w"]), "f_conv_b": f32(inputs["f_conv_b"]),
        "f_w_down": f32(inputs["f_w_down"]), "out_norm": f32(inputs["out_norm"]),
    }
    shared = {k: np.ascontiguousarray(v) for k, v in shared.items()}
    in_maps = []
    for cidx in range(NCORES):
        ps = slice(cidx * PB, (cidx + 1) * PB)
        ss = slice(cidx * SB_, (cidx + 1) * SB_)
        m = dict(shared)
        m["xp"] = np.ascontiguousarray(x_prompt[ps])
        m["xs"] = np.ascontiguousarray(x_sample[ss].reshape(SB_ * DSEQ, D))
        m["s_h"] = np.ascontiguousarray(state_lru_h[0, ss])
        m["s_conv"] = np.ascontiguousarray(state_lru_conv[0, ss])
        m["s_fconv"] = np.ascontiguousarray(state_ffn_conv[:, ss])
        m["ck"] = np.ascontiguousarray(cache_k[ss].reshape(SB_, PAST, D))
        m["cv"] = np.ascontiguousarray(cache_v[ss].reshape(SB_, PAST, D))
        in_maps.append(m)
    if "nc" not in _NC_CACHE:
        _NC_CACHE["nc"] = build_nc()
    nc = _NC_CACHE["nc"]
    res = run_bass_kernel_spmd(nc, in_maps, core_ids=list(range(NCORES)))
    R = res.results
    if DEBUG:
        _NC_CACHE["dbg"] = {k: np.asarray(v) for k, v in R[0].items() if k.startswith("dbg_")}
    cat = lambda k, ax=0: np.concatenate([np.asarray(r[k], dtype=np.float32) for r in R], axis=ax)
    B_ = NCORES * PB
    DB = NCORES * SB_
    y_prompt = cat("y_p")
    y_sample = cat("y_s").reshape(DB, DSEQ, D)
    p_lru_h = cat("p_h")[None]
    p_lru_conv = cat("p_conv")[None]
    p_ffn_conv = cat("p_fconv", 1)
    p_k = cat("p_k").reshape(B_, SEQ, NH, HD)
    p_v = cat("p_v").reshape(B_, SEQ, NH, HD)
    s_lru_h = cat("o_h")[None]
    s_lru_conv = cat("o_conv")[None]
    s_ffn_conv = cat("o_fconv", 1)
    s_k = cat("o_k").reshape(DB, DSEQ, NH, HD)
    s_v = cat("o_v").reshape(DB, DSEQ, NH, HD)
    return (y_prompt, y_sample, p_lru_h, p_lru_conv, p_ffn_conv, p_k, p_v,
            s_lru_h, s_lru_conv, s_ffn_conv, s_k, s_v)
```

```python
import contextlib
import numpy as np
import concourse.bass as bass
import concourse.mybir as mybir
from concourse.bass_utils import run_bass_kernel_spmd

F32 = mybir.dt.float32
BF16 = mybir.dt.bfloat16
F32R = mybir.dt.float32r
AF = mybir.ActivationFunctionType
ALU = mybir.AluOpType

NCORES = 8
D = 1024
DFF = 3072
SEQ = 2048
PB = 4
SB_ = 2
DSEQ = 16
PAST = 4096
NH = 8
HD = 128
TN = 512
EPS = 1e-6
NSLOT = 4
NEG = -30000.0
DEBUG = False


class Buf:
    __slots__ = ("name", "last_w", "readers", "excl")

    def __init__(self, name, excl=False):
        self.name = name
        self.last_w = None
        self.readers = {}
        self.excl = excl


class Op:
    __slots__ = ("eng", "fn", "deps", "signal", "ev", "idx", "is_dma", "dsem")


class Prog:
    ENGS = ("pe", "act", "dve", "pool", "sp")

    def __init__(self, nc, n_generic_dma_sems=24):
        self.nc = nc
        self.ops = []
        self.dma_sem_state = {}
        self.n_generic = n_generic_dma_sems
        self.generic_i = 0
        self.n_unique = 0
        self.chains = {}

    def buf(self, name, excl=False):
        return Buf(name, excl)

    def _add(self, eng, fn, reads, writes, is_dma=False, dsem=None, after=None):
        op = Op()
        op.eng = eng
        op.fn = fn
        op.signal = False
        op.ev = None
        op.idx = len(self.ops)
        op.is_dma = is_dma
        op.dsem = dsem
        rd = [b for b in reads if not b.excl]
        wr = list(writes) + [b for b in reads if b.excl]
        deps = {}
        for b in rd:
            if b.last_w is not None:
                deps[b.last_w.idx] = b.last_w
        for b in wr:
            if b.last_w is not None:
                deps[b.last_w.idx] = b.last_w
            for r in b.readers.values():
                deps[r.idx] = r
        if after is not None:
            deps[after.idx] = after
        if is_dma:
            st = self.dma_sem_state.setdefault(dsem, [0, None])
            if st[1] is not None:
                deps[st[1].idx] = st[1]
            st[0] += 1
            st[1] = op
            op.ev = (dsem, 16 * st[0])
        dl = []
        for d in deps.values():
            if d is op:
                continue
            if (not d.is_dma) and (not is_dma) and d.eng == "pe" and eng == "pe":
                continue
            if not d.is_dma:
                d.signal = True
            dl.append(d)
        op.deps = dl
        key = ("dma", op.idx) if is_dma else eng
        for b in rd:
            b.readers[key] = op
        for b in wr:
            b.last_w = op
            b.readers = {}
        self.ops.append(op)
        return op

    def op(self, eng, fn, reads=(), writes=()):
        return self._add(eng, fn, reads, writes)

    def dma(self, queue, fn, reads=(), writes=(), sem=None, chain=None):
        after = None
        if chain is not None:
            after = self.chains.get(chain)
        if queue == "pool":
            sem = "u%d" % self.n_unique
            self.n_unique += 1
        elif sem is None:
            sem = "g%d" % self.generic_i
            self.generic_i = (self.generic_i + 1) % self.n_generic
        op = self._add(queue, fn, reads, writes, is_dma=True, dsem=sem, after=after)
        if chain is not None:
            self.chains[chain] = op
        return op

    def emit(self):
        nc = self.nc
        cnt = {e: 0 for e in self.ENGS}
        for op in self.ops:
            if op.is_dma:
                continue
            if op.signal:
                cnt[op.eng] += 1
                op.ev = ("eng_" + op.eng, cnt[op.eng])
        sem_names = ["eng_" + e for e in self.ENGS] + list(self.dma_sem_state.keys())
        with contextlib.ExitStack() as st:
            sems = {}
            for n in sem_names:
                sems[n] = st.enter_context(nc.semaphore("s_" + n))
            block = st.enter_context(nc.Block())
            per_eng = {e: [] for e in self.ENGS}
            for op in self.ops:
                per_eng[op.eng].append(op)
            finals = [(n, 16 * s[0]) for n, s in self.dma_sem_state.items()]

            def run(e, ename):
                known = {}
                for op in per_eng[ename]:
                    for d in op.deps:
                        sn, val = d.ev
                        if known.get(sn, 0) >= val:
                            continue
                        known[sn] = val
                        e.wait_ge(sems[sn], val)
                    ins = op.fn(e)
                    if op.is_dma:
                        ins.then_inc(sems[op.dsem], 16)
                    elif op.signal:
                        ins.then_inc(sems["eng_" + ename], 1)
                if ename == "sp":
                    for n, v in finals:
                        if v > 0:
                            e.wait_ge(sems[n], v)

            @block.tensor
            def _(e):
                run(e, "pe")

            @block.scalar
            def _(e):
                run(e, "act")

            @block.vector
            def _(e):
                run(e, "dve")

            @block.gpsimd
            def _(e):
                run(e, "pool")

            @block.sync
            def _(e):
                run(e, "sp")


class Ring:
    def __init__(self, items):
        self.items = items
        self.i = 0

    def next(self):
        it = self.items[self.i]
        self.i = (self.i + 1) % len(self.items)
        return it


class TileCtx:
    pass


class Builder:
    def __init__(self, nc):
        self.nc = nc
        self.P = Prog(nc)
        self.st = contextlib.ExitStack()
        self.wq = []
        self.w_issued = 0
        self.w_used = 0

    def sb(self, name, shape, dt):
        return self.st.enter_context(self.nc.sbuf_tensor(name, shape, dt))

    def dram_in(self, name, shape):
        return self.nc.dram_tensor(name, list(shape), F32, kind="ExternalInput").ap()

    def dram_out(self, name, shape):
        return self.nc.dram_tensor(name, list(shape), F32, kind="ExternalOutput").ap()

    def tile(self, name, shape, dt, nbuf=None):
        t = self.sb(name, shape, dt)
        if nbuf is None:
            return t, self.P.buf(name)
        return t, [self.P.buf("%s_%d" % (name, i)) for i in range(nbuf)]

    def ring(self, name, shape, dt, n):
        return Ring([(self.sb("%s%d" % (name, i), shape, dt), self.P.buf("%s%d" % (name, i))) for i in range(n)])

    def declare(self):
        nc = self.nc
        d = {}
        d["xp"] = self.dram_in("xp", [PB, SEQ, D])
        d["xs"] = self.dram_in("xs", [SB_ * DSEQ, D])
        d["s_h"] = self.dram_in("s_h", [SB_, D])
        d["s_conv"] = self.dram_in("s_conv", [SB_, 3, D])
        d["s_fconv"] = self.dram_in("s_fconv", [2, SB_, 2, DFF])
        d["ck"] = self.dram_in("ck", [SB_, PAST, D])
        d["cv"] = self.dram_in("cv", [SB_, PAST, D])
        d["a_norm"] = self.dram_in("a_norm", [D])
        d["a_w_in"] = self.dram_in("a_w_in", [D, 2 * D])
        d["a_conv_w"] = self.dram_in("a_conv_w", [4, D])
        d["a_conv_b"] = self.dram_in("a_conv_b", [D])
        d["a_wr"] = self.dram_in("a_wr", [16, 64, 64])
        d["a_br"] = self.dram_in("a_br", [D])
        d["a_wi"] = self.dram_in("a_wi", [16, 64, 64])
        d["a_bi"] = self.dram_in("a_bi", [D])
        d["a_lambda"] = self.dram_in("a_lambda", [D])
        d["a_w_out"] = self.dram_in("a_w_out", [D, D])
        d["kv_norm"] = self.dram_in("kv_norm", [D])
        d["w_kv"] = self.dram_in("w_kv", [D, 2 * D])
        d["k_norm"] = self.dram_in("k_norm", [HD])
        d["b_norm"] = self.dram_in("b_norm", [D])
        d["b_wq"] = self.dram_in("b_wq", [D, D])
        d["q_norm"] = self.dram_in("q_norm", [HD])
        d["b_wo"] = self.dram_in("b_wo", [D, D])
        d["f_norm"] = self.dram_in("f_norm", [2, D])
        d["f_w_up"] = self.dram_in("f_w_up", [2, D, 2 * DFF])
        d["f_conv_w"] = self.dram_in("f_conv_w", [2, 3, DFF])
        d["f_conv_b"] = self.dram_in("f_conv_b", [2, DFF])
        d["f_w_down"] = self.dram_in("f_w_down", [2, DFF, D])
        d["out_norm"] = self.dram_in("out_norm", [D])
        d["y_p"] = self.dram_out("y_p", [PB, SEQ, D])
        d["y_s"] = self.dram_out("y_s", [SB_ * DSEQ, D])
        d["p_h"] = self.dram_out("p_h", [PB, D])
        d["p_conv"] = self.dram_out("p_conv", [PB, 3, D])
        d["p_fconv"] = self.dram_out("p_fconv", [2, PB, 2, DFF])
        d["p_k"] = self.dram_out("p_k", [PB, SEQ, D])
        d["p_v"] = self.dram_out("p_v", [PB, SEQ, D])
        d["o_h"] = self.dram_out("o_h", [SB_, D])
        d["o_conv"] = self.dram_out("o_conv", [SB_, 3, D])
        d["o_fconv"] = self.dram_out("o_fconv", [2, SB_, 2, DFF])
        d["o_k"] = self.dram_out("o_k", [SB_ * DSEQ, D])
        d["o_v"] = self.dram_out("o_v", [SB_ * DSEQ, D])
        def scratch(name, shape):
            return nc.dram_tensor(name, list(shape), BF16, kind="Internal").ap()
        d["w_in_b"] = scratch("w_in_b", [D, 2 * D])
        d["w_out_b"] = scratch("w_out_b", [D, D])
        d["w_kv_b"] = scratch("w_kv_b", [D, 2 * D])
        d["wq_b"] = scratch("wq_b", [D, D])
        d["wo_b"] = scratch("wo_b", [D, D])
        d["w_up_b"] = [scratch("w_up_b%d" % l, [D, 2 * DFF]) for l in range(2)]
        d["w_dn_b"] = [scratch("w_dn_b%d" % l, [DFF, D]) for l in range(2)]
        self.d = d

    def setup(self):
        P, d = self.P, self.d
        sb = self.sb
        self.banks = []
        for i in range(7):
            t = self.st.enter_context(self.nc.psum_tensor("bank%d" % i, [128, 512], F32))
            self.banks.append((t, P.buf("bank%d" % i, excl=True)))
        t = self.st.enter_context(self.nc.psum_tensor("bankb", [128, 1024], BF16))
        self.bankb = (t, P.buf("bankb", excl=True))

        self.x, self.xb = self.tile("x", [128, 8, TN], F32, 8)
        self.xn, self.xnb = self.tile("xn", [128, 8, TN], BF16, 8)
        self.bufA, self.bufAb = self.tile("bufA", [128, 8, TN], BF16, 8)
        self.hmid, self.hmidb = self.tile("hmid", [128, 24, TN], BF16, 24)
        self.KT, self.KTb = self.tile("KT", [128, NH, SEQ], BF16, SEQ // 128)
        self.Vb, self.Vbb = self.tile("Vb", [128, SEQ // 128, D], BF16, SEQ // 128)
        self.RS, self.RSb = self.tile("RS", [128, 128], F32R)
        self.wslots = [(sb("wslot%d" % i, [128, 8, 512], BF16), P.buf("wslot%d" % i)) for i in range(NSLOT)]
        self.rowbufs = self.ring("rowbuf", [128, D], F32, 3)
        self.xin = self.rowbufs
        self.tf = self.ring("tf", [128, TN + 4], F32, 6)
        self.tr = self.ring("tr", [128, TN], F32R, 6)
        self.tb = self.ring("tb", [128, TN], BF16, 3)
        self.xpre, self.xpreb = self.tile("xpre", [128, D], F32)
        self.small = self.ring("small", [128, 64], F32, 4)
        self.rhist, self.rhistb = self.tile("rhist", [128, 8, 2, 3], F32, 8)
        self.hstate, self.hstateb = self.tile("hstate", [128, 8, 2], F32, 8)
        self.fhist = []
        for l in range(2):
            self.fhist.append(self.tile("fhist%d" % l, [128, 24, 2, 2], F32, 24))

        self.ident_f, self.ident_fb = self.tile("ident_f", [128, 128], F32)
        self.ident_b, self.ident_bb = self.tile("ident_b", [128, 128], BF16)
        self.ones_dm, self.ones_dmb = self.tile("ones_dm", [128, 128], BF16)
        self.ones_hd, self.ones_hdb = self.tile("ones_hd", [128, 128], BF16)
        self.negtri, self.negtrib = self.tile("negtri", [128, 128], F32R)
        self.negones, self.negonesb = self.tile("negones", [128, 128], F32R)
        self.M01, self.M01b = self.tile("M01", [128, 128], F32)
        self.MB, self.MBb = self.tile("MB", [128, 128], BF16)
        self.M01s, self.M01sb = self.tile("M01s", [128, 128], F32)
        self.MBs, self.MBsb = self.tile("MBs", [128, 128], BF16)
        self.gcol, self.gcolb = self.tile("gcol", [128, 6, 8], F32)
        self.acw, self.acwb = self.tile("acw", [128, 8, 4], F32)
        self.acols, self.acolsb = self.tile("acols", [128, 8, 8], F32)
        self.fcw, self.fcwb = self.tile("fcw", [128, 2, 24, 3], F32)
        self.fcb, self.fcbb = self.tile("fcb", [128, 2, 24], F32)
        self.knbc, self.knbcb = self.tile("knbc", [128, 128], F32)
        self.qg, self.qgb = self.tile("qg", [128, 2], F32)
        self.WRI, self.WRIb = self.tile("WRI", [128, 16, 128], BF16)
        self.lnhalf, self.lnhalfb = self.tile("lnhalf", [128, 1], F32)
        self.mhalf, self.mhalfb = self.tile("mhalf", [128, 8], F32)

        def pool(fn, reads=(), writes=()):
            P.op("pool", fn, reads, writes)

        idf, idb = self.ident_f, self.ident_b
        pool(lambda e: e.memset(idf[:], 1.0), writes=[self.ident_fb])
        pool(lambda e: e.affine_select(out=idf[:], in_=idf[:], pattern=[[-1, 128]], compare_op=ALU.is_equal,
                                       fill=0.0, base=0, channel_multiplier=1),
             reads=[self.ident_fb], writes=[self.ident_fb])
        pool(lambda e: e.tensor_copy(out=idb[:], in_=idf[:]), reads=[self.ident_fb], writes=[self.ident_bb])
        pool(lambda e: e.memset(self.ones_dm[:], 1.0 / D), writes=[self.ones_dmb])
        pool(lambda e: e.memset(self.ones_hd[:], 1.0 / HD), writes=[self.ones_hdb])
        self.tmpc, self.tmpcb = self.tile("tmpc", [128, 128], F32)
        tmpc = self.tmpc
        nt = self.negtri
        pool(lambda e: e.memset(tmpc[:], -1.0), writes=[self.tmpcb])
        pool(lambda e: e.affine_select(out=self.negones[:], in_=tmpc[:], pattern=[[0, 128]], compare_op=ALU.is_ge,
                                       fill=0.0, base=1, channel_multiplier=0),
             reads=[self.tmpcb], writes=[self.negonesb])
        pool(lambda e: e.affine_select(out=nt[:], in_=tmpc[:], pattern=[[-1, 128]], compare_op=ALU.is_ge,
                                       fill=0.0, base=0, channel_multiplier=1),
             reads=[self.tmpcb], writes=[self.negtrib])
        m01, mb = self.M01, self.MB
        pool(lambda e: e.memset(m01[:], 1.0), writes=[self.M01b])
        pool(lambda e: e.affine_select(out=m01[:], in_=m01[:], pattern=[[1, 128]], compare_op=ALU.is_gt,
                                       fill=0.0, base=0, channel_multiplier=-1),
             reads=[self.M01b], writes=[self.M01b])
        pool(lambda e: e.memset(mb[:], 0.0), writes=[self.MBb])
        pool(lambda e: e.affine_select(out=mb[:], in_=mb[:], pattern=[[1, 128]], compare_op=ALU.is_gt,
                                       fill=NEG, base=0, channel_multiplier=-1),
             reads=[self.MBb], writes=[self.MBb])
        m01s, mbs = self.M01s, self.MBs
        pool(lambda e: e.memset(m01s[:], 1.0), writes=[self.M01sb])
        pool(lambda e: e.affine_select(out=m01s[:], in_=m01s[:], pattern=[[0, NH], [1, DSEQ]], compare_op=ALU.is_gt,
                                       fill=0.0, base=0, channel_multiplier=-1),
             reads=[self.M01sb], writes=[self.M01sb])
        pool(lambda e: e.memset(mbs[:], 0.0), writes=[self.MBsb])
        pool(lambda e: e.affine_select(out=mbs[:], in_=mbs[:], pattern=[[0, NH], [1, DSEQ]], compare_op=ALU.is_gt,
                                       fill=NEG, base=0, channel_multiplier=-1),
             reads=[self.MBsb], writes=[self.MBsb])
        pool(lambda e: e.memset(self.WRI[:], 0.0), writes=[self.WRIb])
        pool(lambda e: e.memset(self.lnhalf[:], float(np.log(0.5))), writes=[self.lnhalfb])
        pool(lambda e: e.memset(self.mhalf[:], -0.5), writes=[self.mhalfb])

        self.wsb = {}

        def cast(dst, src, name, sem):
            b = P.buf("scr_" + name)
            dv = dst.rearrange("r (a b) -> r a b", b=1024)
            sv = src.rearrange("r (a b) -> r a b", b=1024)
            P.dma("pool", lambda e: e.dma_start(out=dv, in_=sv), writes=[b], chain=sem)
            self.wsb[name] = b

        self._cast = cast
        self.cast_plan = {
            "lru": [(d["w_dn_b"][0], d["f_w_down"][0], "w_dn0", "castA"), (d["w_kv_b"], d["w_kv"], "w_kv", "castB")],
            "ffn0": [(d["wq_b"], d["b_wq"], "wq", "castA"), (d["wo_b"], d["b_wo"], "wo", "castB")],
            "kv": [(d["w_up_b"][1], d["f_w_up"][1], "w_up1", "castC")],
            "q": [(d["w_dn_b"][1], d["f_w_down"][1], "w_dn1", "castA")],
        }

        cast(d["w_in_b"], d["a_w_in"], "w_in", "castA")
        def small_load(dst_ap, src_ap, wbuf):
            P.dma("sp", lambda e: e.dma_start(out=dst_ap, in_=src_ap, allow_slow_non_contiguous=True), writes=[wbuf])

        norms = [d["a_norm"], d["f_norm"][0], d["kv_norm"], d["b_norm"], d["f_norm"][1], d["out_norm"]]
        for i, nv in enumerate(norms):
            small_load(self.gcol[:, i, :], nv.rearrange("(j p) -> p j", p=128), self.gcolb)
        for k in range(4):
            small_load(self.acw[:, :, k], d["a_conv_w"][k].rearrange("(j p) -> p j", p=128), self.acwb)
        for i, nm in enumerate(["a_conv_b", "a_br", "a_bi", "a_lambda"]):
            small_load(self.acols[:, :, i], d[nm].rearrange("(j p) -> p j", p=128), self.acolsb)
        for l in range(2):
            for k in range(3):
                small_load(self.fcw[:, l, :, k], d["f_conv_w"][l, k].rearrange("(j p) -> p j", p=128), self.fcwb)
            small_load(self.fcb[:, l, :], d["f_conv_b"][l].rearrange("(j p) -> p j", p=128), self.fcbb)
        P.dma("sp", lambda e: e.dma_start(out=self.knbc[:], in_=d["k_norm"].partition_broadcast(128)), writes=[self.knbcb])
        small_load(self.qg[:, 0:1], d["q_norm"].rearrange("(p o) -> p o", o=1), self.qgb)
        for j in range(8):
            for half in range(2):
                n = 2 * j + half
                lo = 64 * half
                P.dma("pool", lambda e, j=j, n=n, lo=lo: e.dma_start(out=self.WRI[lo:lo + 64, j, lo:lo + 64], in_=d["a_wr"][n]),
                      writes=[self.WRIb], chain="casts%d" % (n % 4))
                P.dma("pool", lambda e, j=j, n=n, lo=lo: e.dma_start(out=self.WRI[lo:lo + 64, 8 + j, lo:lo + 64], in_=d["a_wi"][n]),
                      writes=[self.WRIb], chain="casts%d" % (n % 4))
        cast(d["w_out_b"], d["a_w_out"], "w_out", "castB")
        cast(d["w_up_b"][0], d["f_w_up"][0], "w_up0", "castC")

        ac = self.acols
        act = lambda fn, reads=(), writes=(): P.op("act", fn, reads, writes)
        act(lambda e: e.mul(out=ac[:, :, 4], in_=ac[:, :, 1], mul=0.5), reads=[self.acolsb], writes=[self.acolsb])
        act(lambda e: e.mul(out=ac[:, :, 5], in_=ac[:, :, 2], mul=0.5), reads=[self.acolsb], writes=[self.acolsb])
        act(lambda e: e.activation(out=ac[:, :, 6], in_=ac[:, :, 3], func=AF.Exp, scale=-1.0), reads=[self.acolsb], writes=[self.acolsb])
        act(lambda e: e.activation(out=ac[:, :, 7], in_=ac[:, :, 6], func=AF.Ln, bias=1.0), reads=[self.acolsb], writes=[self.acolsb])
        act(lambda e: e.mul(out=ac[:, :, 6], in_=ac[:, :, 7], mul=-4.0), reads=[self.acolsb], writes=[self.acolsb])
        act(lambda e: e.mul(out=ac[:, :, 7], in_=ac[:, :, 7], mul=-8.0), reads=[self.acolsb], writes=[self.acolsb])
        act(lambda e: e.mul(out=self.qg[:, 1:2], in_=self.qg[:, 0:1], mul=float(HD) ** -0.5), reads=[self.qgb], writes=[self.qgb])

    def plan_weights(self, ntiles):
        d = self.d
        reqs = []

        def add(src, rows0, col0, key):
            reqs.append((src, rows0, col0, key))

        for _ in range(ntiles):
            for g in range(2):
                add(d["w_in_b"], 0, D + 512 * g, "w_in")
                add(d["w_in_b"], 0, 512 * g, "w_in")
            for mg in range(2):
                add(d["w_out_b"], 0, 512 * mg, "w_out")
            self._plan_ffn(add, 0)
            for g in range(2):
                add(d["w_kv_b"], 0, 512 * g, "w_kv")
            for g in range(2):
                add(d["w_kv_b"], 0, D + 512 * g, "w_kv")
            for g in range(2):
                add(d["wq_b"], 0, 512 * g, "wq")
            for g in range(2):
                add(d["wo_b"], 0, 512 * g, "wo")
            self._plan_ffn(add, 1)
        self.wq = reqs

    def _plan_ffn(self, add, l):
        d = self.d
        for g in range(6):
            add(d["w_up_b"][l], 0, 512 * g, "w_up%d" % l)
            add(d["w_up_b"][l], 0, DFF + 512 * g, "w_up%d" % l)
        for mg in range(2):
            for ks in range(3):
                add(d["w_dn_b"][l], 1024 * ks, 512 * mg, "w_dn%d" % l)

    def _issue_w(self, upto):
        P = self.P
        while self.w_issued < min(upto, len(self.wq)):
            i = self.w_issued
            src, r0, c0, key = self.wq[i]
            assert key in self.wsb, key
            slot, sbuf_ = self.wslots[i % NSLOT]
            view = src[r0:r0 + 1024, c0:c0 + 512].rearrange("(kc p) c -> p kc c", p=128)
            P.dma("sp", lambda e, slot=slot, view=view: e.dma_start(out=slot[:], in_=view),
                  reads=[self.wsb[key]], writes=[sbuf_], sem="wslot%d" % (i % NSLOT))
            self.w_issued += 1

    def next_w(self):
        i = self.w_used
        if self.w_issued == 0:
            self._issue_w(NSLOT)
        assert i < self.w_issued or i >= len(self.wq), (i, self.w_issued)
        self.w_used += 1
        return self.wslots[i % NSLOT]

    def done_w(self, n=1):
        self.w_released = getattr(self, "w_released", 0) + n
        self._issue_w(self.w_released + NSLOT)

    def hf(self, k):
        ap = self.hmid[:, 2 * k:2 * k + 2, :].rearrange("p a b -> p (a b)").bitcast(F32)
        return ap, [self.hmidb[2 * k], self.hmidb[2 * k + 1]]

    def hb(self, k):
        ap = self.hmid[:, 2 * k:2 * k + 2, :].rearrange("p a b -> p (a b)")
        return ap, [self.hmidb[2 * k], self.hmidb[2 * k + 1]]

    def bank(self, i):
        if i == 7:
            t, b = self.bankb
            return t[:, :].bitcast(F32), b
        return self.banks[i]

    def rmsnorm_to_xn(self, c, gi, inplace=False):
        P = self.P
        N = c.N
        st_bank, st_b = self.bank(6)
        for fc in range(8):
            sq, sqb = self.tb.next()
            P.op("act", lambda e, fc=fc, sq=sq: e.activation(out=sq[:, 0:N], in_=self.x[:, fc, 0:N], func=AF.Square),
                 reads=[self.xb[fc]], writes=[sqb])
            P.op("pe", lambda e, fc=fc, sq=sq: e.matmul(st_bank[:, 0:N], lhsT=self.ones_dm[:], rhs=sq[:, 0:N],
                                                        start=(fc == 0), stop=(fc == 7)),
                 reads=[self.ones_dmb, sqb], writes=[st_b])
        lnv, lnvb = self.tf.next()
        rstd, rstdb = self.tf.next()
        P.op("act", lambda e: e.activation(out=lnv[:, 0:N], in_=st_bank[:, 0:N], func=AF.Ln, bias=EPS), reads=[st_b], writes=[lnvb])
        P.op("act", lambda e: e.activation(out=rstd[:, 0:N], in_=lnv[:, 0:N], func=AF.Exp, scale=-0.5), reads=[lnvb], writes=[rstdb])
        for fc in range(8):
            if not inplace:
                P.op("dve", lambda e, fc=fc: e.scalar_tensor_tensor(out=self.xn[:, fc, 0:N], in0=self.x[:, fc, 0:N],
                                                                    scalar=self.gcol[:, gi, fc:fc + 1], in1=rstd[:, 0:N],
                                                                    op0=ALU.mult, op1=ALU.mult),
                     reads=[self.xb[fc], self.gcolb, rstdb], writes=[self.xnb[fc]])
            else:
                P.op("dve", lambda e, fc=fc: e.scalar_tensor_tensor(out=self.x[:, fc, 0:N], in0=self.x[:, fc, 0:N],
                                                                    scalar=self.gcol[:, gi, fc:fc + 1], in1=rstd[:, 0:N],
                                                                    op0=ALU.mult, op1=ALU.mult),
                     reads=[self.xb[fc], self.gcolb, rstdb], writes=[self.xb[fc]])

    def x_to_fm(self, c):
        P = self.P
        for gi, (c0, nt, src) in enumerate(c.tgs):
            if gi == 0 and getattr(c, "prefetched", False):
                xi, xib = self.xpre, self.xpreb
            else:
                xi, xib = self.rowbufs.next()
                P.dma("act", lambda e, xi=xi, src=src, nt=nt: e.dma_start(out=xi[0:nt, :], in_=src), writes=[xib])
            for half in range(2):
                bt, bb = self.bank((2 * gi + half) % 4)
                for q in range(4):
                    fc = 4 * half + q
                    P.op("pe", lambda e, xi=xi, nt=nt, fc=fc, q=q, bt=bt: e.transpose(out=bt[:, q * 128:q * 128 + nt], in_=xi[0:nt, fc * 128:(fc + 1) * 128],
                                                                                        identity=self.ident_f[0:nt, 0:nt]),
                         reads=[xib, self.ident_fb], writes=[bb])
                src3 = bt[:, :].rearrange("p (q t) -> p q t", q=4)[:, :, 0:nt]
                dst3 = self.x[:, 4 * half:4 * half + 4, c0:c0 + nt]
                wb = [self.xb[4 * half + q] for q in range(4)]
                if half == 0:
                    P.op("act", lambda e, src3=src3, dst3=dst3: e.copy(out=dst3, in_=src3), reads=[bb], writes=wb)
                else:
                    P.op("dve", lambda e, src3=src3, dst3=dst3: e.tensor_copy(out=dst3, in_=src3), reads=[bb], writes=wb)

    def lru_mixer(self, c):
        P = self.P
        N, S, L = c.N, c.nseg, c.L
        ac = self.acols
        wslot = {}
        stA = {}
        stB = {}

        def phaseA(j):
            g, jj = divmod(j, 4)
            if jj == 0:
                wslot["rec"] = self.next_w()
                wslot["gate"] = self.next_w()
            (wr_t, wr_b), (wg_t, wg_b) = wslot["rec"], wslot["gate"]
            Rt, Rb = self.bank((0, 2)[j % 2])
            Gt, Gb = self.bank((1, 3)[j % 2])
            for kc in range(8):
                P.op("pe", lambda e, kc=kc, jj=jj, wr_t=wr_t, Rt=Rt: e.matmul(Rt[:, 0:N], lhsT=wr_t[:, kc, jj * 128:(jj + 1) * 128], rhs=self.xn[:, kc, 0:N],
                                                                              start=(kc == 0), stop=(kc == 7)),
                     reads=[wr_b, self.xnb[kc]], writes=[Rb])
            for kc in range(8):
                P.op("pe", lambda e, kc=kc, jj=jj, wg_t=wg_t, Gt=Gt: e.matmul(Gt[:, 0:N], lhsT=wg_t[:, kc, jj * 128:(jj + 1) * 128], rhs=self.xn[:, kc, 0:N],
                                                                              start=(kc == 0), stop=(kc == 7)),
                     reads=[wg_b, self.xnb[kc]], writes=[Gb])
            if jj == 3:
                self.done_w(2)
            rb_t, rb_b = self.rowbufs.next()
            rb3 = rb_t[:, 0:S * (L + 3)].rearrange("p (s l) -> p s l", s=S)
            P.op("pool", lambda e, j=j, rb3=rb3: e.tensor_copy(out=rb3[:, :, 0:3], in_=self.rhist[:, j, 0:S, :]),
                 reads=[self.rhistb[j]], writes=[rb_b])
            R3 = Rt[:, 0:N].rearrange("p (s l) -> p s l", s=S)
            P.op("act", lambda e, rb3=rb3, R3=R3: e.copy(out=rb3[:, :, 3:3 + L], in_=R3), reads=[Rb], writes=[rb_b])
            c0_t, c0_b = self.hf(2 * (j % 4))
            gg_t, gg_b = self.hf(2 * (j % 4) + 1)
            c03 = c0_t[:, 0:N].rearrange("p (s l) -> p s l", s=S)
            P.op("pool", lambda e, j=j, rb3=rb3, c03=c03: e.tensor_scalar(out=c03, in0=rb3[:, :, 3:3 + L], scalar1=self.acw[:, j, 3:4], scalar2=ac[:, j, 0:1],
                                                                          op0=ALU.mult, op1=ALU.add),
                 reads=[rb_b, self.acwb, self.acolsb], writes=c0_b)
            for k in (2, 1, 0):
                P.op("dve", lambda e, j=j, k=k, rb3=rb3, c03=c03: e.scalar_tensor_tensor(out=c03, in0=rb3[:, :, k:k + L], scalar=self.acw[:, j, k:k + 1],
                                                                                          in1=c03, op0=ALU.mult, op1=ALU.add),
                     reads=[rb_b, self.acwb] + c0_b, writes=c0_b)
            P.op("pool", lambda e, j=j, rb3=rb3: e.tensor_copy(out=self.rhist[:, j, 0:S, :], in_=rb3[:, :, L:L + 3]),
                 reads=[rb_b], writes=[self.rhistb[j]])
            cb_t, cb_b = self.tb.next()
            P.op("pool", lambda e, cb_t=cb_t, c0_t=c0_t: e.tensor_copy(out=cb_t[:, 0:N], in_=c0_t[:, 0:N]), reads=c0_b, writes=[cb_b])
            P.op("act", lambda e, gg_t=gg_t, Gt=Gt: e.activation(out=gg_t[:, 0:N], in_=Gt[:, 0:N], func=AF.Gelu_apprx_tanh), reads=[Gb], writes=gg_b)
            stA[j] = (c0_t, c0_b, gg_t, gg_b, cb_t, cb_b)

        def phaseB1(j):
            c0_t, c0_b, gg_t, gg_b, cb_t, cb_b = stA[j]
            rp_t, rp_b = self.bank((4, 6)[j % 2])
            ip_t, ip_b = self.bank((5, 7)[j % 2])
            P.op("pe", lambda e, j=j, cb_t=cb_t, rp_t=rp_t: e.matmul(rp_t[:, 0:N], lhsT=self.WRI[:, j, :], rhs=cb_t[:, 0:N], start=True, stop=True),
                 reads=[self.WRIb, cb_b], writes=[rp_b])
            P.op("pe", lambda e, j=j, cb_t=cb_t, ip_t=ip_t: e.matmul(ip_t[:, 0:N], lhsT=self.WRI[:, 8 + j, :], rhs=cb_t[:, 0:N], start=True, stop=True),
                 reads=[self.WRIb, cb_b], writes=[ip_b])
            r_t, r_b = self.hf(8 + 2 * (j % 2))
            i_t, i_b = self.hf(9 + 2 * (j % 2))
            P.op("act", lambda e, j=j, r_t=r_t, rp_t=rp_t: e.activation(out=r_t[:, 0:N], in_=rp_t[:, 0:N], func=AF.Tanh, scale=0.5, bias=ac[:, j, 4:5]),
                 reads=[rp_b, self.acolsb], writes=r_b)
            P.op("act", lambda e, j=j, i_t=i_t, ip_t=ip_t: e.activation(out=i_t[:, 0:N], in_=ip_t[:, 0:N], func=AF.Tanh, scale=0.5, bias=ac[:, j, 5:6]),
                 reads=[ip_b, self.acolsb], writes=i_b)
            stB[j] = (r_t, r_b, i_t, i_b)

        def phaseB2(j):
            c0_t, c0_b, gg_t, gg_b, cb_t, cb_b = stA.pop(j)
            r_t, r_b, i_t, i_b = stB.pop(j)
            a_t, a_b = self.tf.next()
            s_t, s_b = self.tf.next()
            P.op("act", lambda e, j=j, a_t=a_t, r_t=r_t: e.activation(out=a_t[:, 0:N], in_=r_t[:, 0:N], func=AF.Exp, scale=ac[:, j, 6:7], bias=ac[:, j, 6:7]),
                 reads=r_b + [self.acolsb], writes=[a_b])
            P.op("act", lambda e, j=j, s_t=s_t, r_t=r_t: e.activation(out=s_t[:, 0:N], in_=r_t[:, 0:N], func=AF.Exp, scale=ac[:, j, 7:8], bias=ac[:, j, 7:8]),
                 reads=r_b + [self.acolsb], writes=[s_b])
            P.op("act", lambda e, s_t=s_t: e.activation(out=s_t[:, 0:N], in_=s_t[:, 0:N], func=AF.Ln, scale=-1.0, bias=1.0), reads=[s_b], writes=[s_b])
            P.op("act", lambda e, s_t=s_t: e.activation(out=s_t[:, 0:N], in_=s_t[:, 0:N], func=AF.Exp, scale=0.5, bias=self.lnhalf[:, 0:1]), reads=[s_b, self.lnhalfb], writes=[s_b])
            P.op("dve", lambda e, i_t=i_t, c0_t=c0_t: e.scalar_tensor_tensor(out=i_t[:, 0:N], in0=i_t[:, 0:N], scalar=1.0, in1=c0_t[:, 0:N], op0=ALU.add, op1=ALU.mult),
                 reads=i_b + c0_b, writes=i_b)
            P.op("dve", lambda e, i_t=i_t, s_t=s_t: e.tensor_tensor(out=i_t[:, 0:N], in0=i_t[:, 0:N], in1=s_t[:, 0:N], op=ALU.mult),
                 reads=i_b + [s_b], writes=i_b)
            h_t, h_b = self.tf.next()
            for s in range(S):
                P.op("dve", lambda e, j=j, s=s, h_t=h_t, a_t=a_t, i_t=i_t: e.tensor_tensor_scan(out=h_t[:, s * L:(s + 1) * L], data0=a_t[:, s * L:(s + 1) * L],
                                                                                                data1=i_t[:, s * L:(s + 1) * L], initial=self.hstate[:, j, s:s + 1],
                                                                                                op0=ALU.mult, op1=ALU.add),
                     reads=[a_b, self.hstateb[j]] + i_b, writes=[h_b])
            h3 = h_t[:, 0:N].rearrange("p (s l) -> p s l", s=S)
            P.op("dve", lambda e, j=j, h3=h3: e.tensor_copy(out=self.hstate[:, j, 0:S], in_=h3[:, :, L - 1]),
                 reads=[h_b], writes=[self.hstateb[j]])
            P.op("dve", lambda e, j=j, h_t=h_t, gg_t=gg_t: e.tensor_tensor(out=self.bufA[:, j, 0:N], in0=h_t[:, 0:N], in1=gg_t[:, 0:N], op=ALU.mult),
                 reads=[h_b] + gg_b, writes=[self.bufAb[j]])

        phaseA(0)
        phaseA(1)
        for j in range(0, 8, 2):
            if j + 2 < 8:
                phaseA(j + 2)
            phaseB1(j)
            if j + 3 < 8:
                phaseA(j + 3)
            phaseB1(j + 1)
            phaseB2(j)
            phaseB2(j + 1)
        self.proj_residual(c, self.bufA, self.bufAb)

    def proj_residual(self, c, src, srcb):
        P = self.P
        N = c.N
        w = [self.next_w(), self.next_w()]
        bks = [self.bank(i) for i in range(8)]
        for kc in range(8):
            for mg in range(2):
                w_t, w_b = w[mg]
                for mm in range(4):
                    bt, bb = bks[4 * mg + mm]
                    P.op("pe", lambda e, kc=kc, mm=mm, w_t=w_t, bt=bt: e.matmul(bt[:, 0:N], lhsT=w_t[:, kc, mm * 128:(mm + 1) * 128], rhs=src[:, kc, 0:N],
                                                                                start=(kc == 0), stop=(kc == 7)),
                         reads=[w_b, srcb[kc]], writes=[bb])
        for m in range(8):
            bt, bb = bks[m]
            P.op("dve", lambda e, m=m, bt=bt: e.tensor_tensor(out=self.x[:, m, 0:N], in0=bt[:, 0:N], in1=self.x[:, m, 0:N], op=ALU.add),
                 reads=[bb, self.xb[m]], writes=[self.xb[m]])
        self.done_w(2)

    def conv_ffn(self, c, l):
        P = self.P
        N, S, L = c.N, c.nseg, c.L
        self.rmsnorm_to_xn(c, 1 if l == 0 else 4)
        fh_t, fh_b = self.fhist[l]
        bi = 0
        for g in range(6):
            wg_t, wg_b = self.next_w()
            wu_t, wu_b = self.next_w()
            for jj in range(4):
                j = 4 * g + jj
                Gt, Gb = self.bank(bi % 6)
                Ut, Ub = self.bank((bi + 1) % 6)
                bi += 2
                for kc in range(8):
                    P.op("pe", lambda e, kc=kc, jj=jj, wg_t=wg_t, Gt=Gt: e.matmul(Gt[:, 0:N], lhsT=wg_t[:, kc, jj * 128:(jj + 1) * 128], rhs=self.xn[:, kc, 0:N],
                                                                                  start=(kc == 0), stop=(kc == 7)),
                         reads=[wg_b, self.xnb[kc]], writes=[Gb])
                for kc in range(8):
                    P.op("pe", lambda e, kc=kc, jj=jj, wu_t=wu_t, Ut=Ut: e.matmul(Ut[:, 0:N], lhsT=wu_t[:, kc, jj * 128:(jj + 1) * 128], rhs=self.xn[:, kc, 0:N],
                                                                                  start=(kc == 0), stop=(kc == 7)),
                         reads=[wu_b, self.xnb[kc]], writes=[Ub])
                gb_t, gb_b = self.tf.next()
                gb3 = gb_t[:, 0:S * (L + 2)].rearrange("p (s l) -> p s l", s=S)
                P.op("pool", lambda e, j=j, gb3=gb3: e.tensor_copy(out=gb3[:, :, 0:2], in_=fh_t[:, j, 0:S, :]), reads=[fh_b[j]], writes=[gb_b])
                G3 = Gt[:, 0:N].rearrange("p (s l) -> p s l", s=S)
                P.op("act", lambda e, gb3=gb3, G3=G3: e.copy(out=gb3[:, :, 2:2 + L], in_=G3), reads=[Gb], writes=[gb_b])
                c0_t, c0_b = self.tf.next()
                P.op("act", lambda e, j=j, c0_t=c0_t, Gt=Gt: e.activation(out=c0_t[:, 0:N], in_=Gt[:, 0:N], func=AF.Identity,
                                                                          scale=self.fcw[:, l, j, 2:3], bias=self.fcb[:, l, j:j + 1]),
                     reads=[Gb, self.fcwb, self.fcbb], writes=[c0_b])
                c03 = c0_t[:, 0:N].rearrange("p (s l) -> p s l", s=S)
                for k, eng in ((1, "dve"), (0, "dve")):
                    P.op(eng, lambda e, j=j, k=k, gb3=gb3, c03=c03: e.scalar_tensor_tensor(out=c03, in0=gb3[:, :, k:k + L], scalar=self.fcw[:, l, j, k:k + 1],
                                                                                              in1=c03, op0=ALU.mult, op1=ALU.add),
                         reads=[gb_b, self.fcwb, c0_b], writes=[c0_b])
                P.op("pool", lambda e, j=j, gb3=gb3: e.tensor_copy(out=fh_t[:, j, 0:S, :], in_=gb3[:, :, L:L + 2]), reads=[gb_b], writes=[fh_b[j]])
                P.op("act", lambda e, c0_t=c0_t: e.activation(out=c0_t[:, 0:N], in_=c0_t[:, 0:N], func=AF.Gelu_apprx_tanh), reads=[c0_b], writes=[c0_b])
                P.op("dve", lambda e, j=j, c0_t=c0_t, Ut=Ut: e.tensor_tensor(out=self.hmid[:, j, 0:N], in0=Ut[:, 0:N], in1=c0_t[:, 0:N], op=ALU.mult),
                     reads=[Ub, c0_b], writes=[self.hmidb[j]])
            self.done_w(2)
        for mg in range(2):
            bks = [self.bank(i) for i in range(4)]
            for ks in range(3):
                w_t, w_b = self.next_w()
                for mm in range(4):
                    bt, bb = bks[mm]
                    for kc in range(8):
                        P.op("pe", lambda e, kc=kc, mm=mm, ks=ks, w_t=w_t, bt=bt: e.matmul(bt[:, 0:N], lhsT=w_t[:, kc, mm * 128:(mm + 1) * 128],
                                                                                           rhs=self.hmid[:, ks * 8 + kc, 0:N],
                                                                                           start=(ks == 0 and kc == 0), stop=(ks == 2 and kc == 7)),
                             reads=[w_b, self.hmidb[ks * 8 + kc]], writes=[bb])
                self.done_w(1)
            for mm in range(4):
                m = 4 * mg + mm
                bt, bb = bks[mm]
                P.op("dve", lambda e, m=m, bt=bt: e.tensor_tensor(out=self.x[:, m, 0:N], in0=bt[:, 0:N], in1=self.x[:, m, 0:N], op=ALU.add),
                     reads=[bb, self.xb[m]], writes=[self.xb[m]])

    def shared_kv(self, c):
        P = self.P
        N = c.N
        self.rmsnorm_to_xn(c, 2)
        wk = [self.next_w(), self.next_w()]
        kbs = []
        for gi, (c0, nt, _src) in enumerate(c.tgs):
            k_dst, v_dst, kt_dst = c.kv_dst[gi]
            ksb, ksbb = self.rowbufs.next()
            for half in range(2):
                bt, bb = self.bank(half + 4 * (gi % 2))
                w_t, w_b = wk[half]
                for kc in range(8):
                    P.op("pe", lambda e, kc=kc, w_t=w_t, bt=bt, c0=c0, nt=nt: e.matmul(bt[0:nt, :], lhsT=self.xn[:, kc, c0:c0 + nt], rhs=w_t[:, kc, :],
                                                                                       start=(kc == 0), stop=(kc == 7)),
                         reads=[w_b, self.xnb[kc]], writes=[bb])
                P.op("act", lambda e, half=half, bt=bt, ksb=ksb, nt=nt: e.copy(out=ksb[0:nt, half * 512:(half + 1) * 512], in_=bt[0:nt, :]), reads=[bb], writes=[ksbb])
            qk_ = 4 + (gi % 2)
            sq = self.hmid[:, 4 * qk_:4 * qk_ + 4, :].rearrange("p a b -> p (a b)").bitcast(F32)
            sqb = [self.hmidb[4 * qk_ + i_] for i_ in range(4)]
            P.op("act", lambda e, sq=sq, ksb=ksb, nt=nt: e.activation(out=sq[0:nt, :], in_=ksb[0:nt, :], func=AF.Square), reads=[ksbb], writes=sqb)
            ss, ssb = self.small.next()
            P.op("dve", lambda e, ss=ss, sq=sq, nt=nt: e.tensor_reduce(out=ss[0:nt, 0:NH], in_=sq[0:nt, :].rearrange("p (h d) -> p h d", h=NH),
                                                                      axis=mybir.AxisListType.X, op=ALU.add), reads=sqb, writes=[ssb])
            P.op("pool", lambda e, ss=ss, nt=nt: e.tensor_scalar(out=ss[0:nt, 8:16], in0=ss[0:nt, 0:NH], scalar1=1.0 / HD, scalar2=EPS, op0=ALU.mult, op1=ALU.add),
                 reads=[ssb], writes=[ssb])
            P.op("pool", lambda e, ss=ss, nt=nt: e.tensor_tensor(out=ss[0:nt, 16:24], in0=ss[0:nt, 8:16], in1=self.mhalf[0:nt, :], op=ALU.pow),
                 reads=[ssb, self.mhalfb], writes=[ssb])
            for h in range(NH):
                P.op("dve", lambda e, h=h, ksb=ksb, ss=ss, nt=nt: e.scalar_tensor_tensor(out=ksb[0:nt, h * HD:(h + 1) * HD], in0=ksb[0:nt, h * HD:(h + 1) * HD],
                                                                                         scalar=ss[0:nt, 16 + h:17 + h], in1=self.knbc[0:nt, :],
                                                                                         op0=ALU.mult, op1=ALU.mult),
                     reads=[ksbb, ssb, self.knbcb], writes=[ksbb])
            P.dma("sp", lambda e, ksb=ksb, k_dst=k_dst, nt=nt: e.dma_start(out=k_dst, in_=ksb[0:nt, :]), reads=[ksbb])
            kb, kbl = self.hb(4 + gi)
            P.op("dve", lambda e, kb=kb, ksb=ksb, nt=nt: e.tensor_copy(out=kb[0:nt, :], in_=ksb[0:nt, :]), reads=[ksbb], writes=kbl)
            kbs.append((kb, kbl))
        self.done_w(2)
        wv = [self.next_w(), self.next_w()]
        for gi, (c0, nt, _src) in enumerate(c.tgs):
            k_dst, v_dst, kt_dst = c.kv_dst[gi]
            vsb, vsbb = self.rowbufs.next()
            for half in range(2):
                bt, bb = self.bank(2 + half + 4 * (gi % 2))
                w_t, w_b = wv[half]
                for kc in range(8):
                    P.op("pe", lambda e, kc=kc, w_t=w_t, bt=bt, c0=c0, nt=nt: e.matmul(bt[0:nt, :], lhsT=self.xn[:, kc, c0:c0 + nt], rhs=w_t[:, kc, :],
                                                                                       start=(kc == 0), stop=(kc == 7)),
                         reads=[w_b, self.xnb[kc]], writes=[bb])
                P.op("act", lambda e, half=half, bt=bt, vsb=vsb, nt=nt: e.copy(out=vsb[0:nt, half * 512:(half + 1) * 512], in_=bt[0:nt, :]), reads=[bb], writes=[vsbb])
            P.dma("sp", lambda e, vsb=vsb, v_dst=v_dst, nt=nt: e.dma_start(out=v_dst, in_=vsb[0:nt, :]), reads=[vsbb])
            vb_ap, vb_bufs = c.vb_dst[gi]
            P.op("dve", lambda e, vb_ap=vb_ap, vsb=vsb, nt=nt: e.tensor_copy(out=vb_ap, in_=vsb[0:nt, :]), reads=[vsbb], writes=vb_bufs)
        self.done_w(2)
        for gi, (c0, nt, _src) in enumerate(c.tgs):
            k_dst, v_dst, kt_dst = c.kv_dst[gi]
            kb, kbl = kbs[gi]
            tb_t, tb_b = self.bankb
            for h in range(NH):
                P.op("pe", lambda e, h=h, kb=kb, tb_t=tb_t, nt=nt: e.transpose(out=tb_t[:, h * 128:h * 128 + nt], in_=kb[0:nt, h * HD:(h + 1) * HD],
                                                                               identity=self.ident_b[0:nt, 0:nt]),
                     reads=list(kbl) + [self.ident_bb], writes=[tb_b])
            kt_ap, kt_bufs = kt_dst
            P.op("act", lambda e, kt_ap=kt_ap, tb_t=tb_t, nt=nt: e.copy(out=kt_ap, in_=tb_t[:, :].rearrange("p (h t) -> p h t", h=NH)[:, :, 0:nt]),
                 reads=[tb_b], writes=kt_bufs)

    def q_proj(self, c):
        P = self.P
        N = c.N
        self.rmsnorm_to_xn(c, 3)
        st1 = {}
        wcur = {}

        def s1(h):
            hh = h % 4
            if hh == 0:
                wcur["w"] = self.next_w()
            w_t, w_b = wcur["w"]
            bt, bb = self.bank(3 + (h % 2))
            for kc in range(8):
                P.op("pe", lambda e, kc=kc, hh=hh, w_t=w_t, bt=bt: e.matmul(bt[:, 0:N], lhsT=w_t[:, kc, hh * 128:(hh + 1) * 128], rhs=self.xn[:, kc, 0:N],
                                                                            start=(kc == 0), stop=(kc == 7)),
                     reads=[w_b, self.xnb[kc]], writes=[bb])
            if hh == 3:
                self.done_w(1)
            qf, qfb = self.tf.next()
            sq, sqb = self.tb.next()
            P.op("act", lambda e, qf=qf, bt=bt: e.copy(out=qf[:, 0:N], in_=bt[:, 0:N]), reads=[bb], writes=[qfb])
            P.op("act", lambda e, sq=sq, bt=bt: e.activation(out=sq[:, 0:N], in_=bt[:, 0:N], func=AF.Square), reads=[bb], writes=[sqb])
            st1[h] = (qf, qfb, sq, sqb)

        def s2(h):
            qf, qfb, sq, sqb = st1.pop(h)
            st, stb = self.bank(6)
            P.op("pe", lambda e, sq=sq, st=st: e.matmul(st[:, 0:N], lhsT=self.ones_hd[:], rhs=sq[:, 0:N], start=True, stop=True),
                 reads=[self.ones_hdb, sqb], writes=[stb])
            rs, rsb = self.tf.next()
            P.op("act", lambda e, rs=rs, st=st: e.activation(out=rs[:, 0:N], in_=st[:, 0:N], func=AF.Ln, bias=EPS), reads=[stb], writes=[rsb])
            P.op("act", lambda e, rs=rs: e.activation(out=rs[:, 0:N], in_=rs[:, 0:N], func=AF.Exp, scale=-0.5), reads=[rsb], writes=[rsb])
            P.op("dve", lambda e, h=h, qf=qf, rs=rs: e.scalar_tensor_tensor(out=self.bufA[:, h, 0:N], in0=qf[:, 0:N], scalar=self.qg[:, 1:2], in1=rs[:, 0:N],
                                                                            op0=ALU.mult, op1=ALU.mult),
                 reads=[qfb, rsb, self.qgb], writes=[self.bufAb[h]])

        s1(0)
        for h in range(NH):
            if h + 1 < NH:
                s1(h + 1)
            s2(h)

    def zero_fill_r(self, ap, wbufs):
        w = ap.shape[-1]
        self.P.op("pool", lambda e: e.affine_select(out=ap, in_=self.tmpc[:, 0:w], pattern=[[0, w]], compare_op=ALU.is_gt,
                                                    fill=0.0, base=0, channel_multiplier=0),
                  reads=[self.tmpcb], writes=wbufs)

    def attn_prompt(self, c):
        P = self.P
        N = c.N
        qt = c.qt
        nchunk = 4 * qt + 4
        items = []
        for h in range(NH):
            for idx, ci in enumerate(range(nchunk - 1, -1, -1)):
                i = ci - 4 * qt
                items.append(dict(h=h, ci=ci, first=(idx == 0), last=(ci == 0), i=i, qlo=max(0, i) * 128))
        n = len(items)

        def qk_sp(k):
            it = items[k]
            h, ci, qlo = it["h"], it["ci"], it["qlo"]
            zt, zb = self.bank(k % 3)
            it["z"] = (zt, zb)
            P.op("pe", lambda e: e.matmul(zt[:, qlo:N], lhsT=self.KT[:, h, ci * 128:(ci + 1) * 128], rhs=self.bufA[:, h, qlo:N], start=True, stop=True),
                 reads=[self.KTb[ci], self.bufAb[h]], writes=[zb])
            et, etb = self.tf.next()
            sp, spb = self.tr.next()
            it["sp"] = (sp, spb)
            P.op("act", lambda e: e.activation(out=et[:, qlo:N], in_=zt[:, qlo:N], func=AF.Exp), reads=[zb], writes=[etb])
            P.op("act", lambda e: e.activation(out=sp[:, qlo:N], in_=et[:, qlo:N], func=AF.Ln, bias=1.0), reads=[etb], writes=[spb])
            if it["i"] >= 0:
                P.op("dve", lambda e: e.tensor_tensor(out=sp[:, qlo:qlo + 128], in0=sp[:, qlo:qlo + 128], in1=self.M01[:], op=ALU.mult),
                     reads=[spb, self.M01b], writes=[spb])
            if not it["last"]:
                if it["first"]:
                    if qlo >= 128:
                        self.zero_fill_r(sp[:, qlo - 128:qlo], [spb])
                    it["rs"] = (sp, spb)
                else:
                    prs, prsb = items[k - 1]["rs"]
                    rs, rsb = self.tr.next()
                    if qlo >= 128:
                        self.zero_fill_r(rs[:, qlo - 128:qlo], [rsb])
                    P.op("dve", lambda e: e.tensor_tensor(out=rs[:, qlo:N], in0=prs[:, qlo:N], in1=sp[:, qlo:N], op=ALU.add),
                         reads=[prsb, spb], writes=[rsb])
                    it["rs"] = (rs, rsb)

        def cum_w(k):
            it = items[k]
            qlo = it["qlo"]
            zt, zb = it["z"]
            sp, spb = it["sp"]
            first = it["first"]
            if it["i"] >= 0:
                P.op("pe", lambda e: e.matmul(zt[:, qlo:qlo + 128], lhsT=self.ident_b[:], rhs=self.MB[:], start=False, stop=False, skip_group_check=True),
                     reads=[self.ident_bb, self.MBb], writes=[zb])
            P.op("pe", lambda e: e.matmul(zt[:, qlo:N], lhsT=self.negtri[:], rhs=sp[:, qlo:N], start=False, stop=first,
                                          skip_group_check=True),
                 reads=[self.negtrib, spb], writes=[zb])
            if not first:
                prs, prsb = items[k - 1]["rs"]
                P.op("pe", lambda e: e.matmul(zt[:, qlo:N], lhsT=self.negones[:], rhs=prs[:, qlo:N], start=False, stop=True,
                                              skip_group_check=True),
                     reads=[self.negonesb, prsb], writes=[zb])
            wt, wtb = self.tb.next()
            it["w"] = (wt, wtb)
            if qlo > 0:
                P.op("pool", lambda e: e.memset(wt[:, 0:qlo], 0.0), writes=[wtb])
            P.op("act", lambda e: e.activation(out=wt[:, qlo:N], in_=zt[:, qlo:N], func=AF.Exp), reads=[zb], writes=[wtb])

        def pv(k):
            it = items[k]
            h, ci = it["h"], it["ci"]
            wt, wtb = it["w"]
            ot, otb = self.bank(5 if h % 2 == 0 else 6)
            P.op("pe", lambda e: e.matmul(ot[:, 0:N], lhsT=self.Vb[:, ci, h * HD:(h + 1) * HD], rhs=wt[:, 0:N], start=it["first"], stop=it["last"]),
                 reads=[self.Vbb[ci], wtb], writes=[otb])
            if it["last"]:
                P.op("dve", lambda e: e.tensor_copy(out=self.hmid[:, h, 0:N], in_=ot[:, 0:N]), reads=[otb], writes=[self.hmidb[h]])

        for step in range(n + 2):
            if step < n:
                qk_sp(step)
            if 0 <= step - 1 < n:
                cum_w(step - 1)
            if 0 <= step - 2 < n:
                pv(step - 2)

    def attn_sample(self, c):
        P = self.P
        d = self.d
        NQ = NH * DSEQ
        hbi = 0
        for s in range(SB_):
            ot, otb = self.bank(5 + s)
            qcol = s * DSEQ
            nchunk = PAST // 128 + 1
            first = True
            for ci in range(nchunk - 1, -1, -1):
                new = (ci == nchunk - 1)
                if new:
                    rows = DSEQ
                    kt_h = lambda h, qcol=qcol: self.KT[:, h, qcol:qcol + DSEQ]
                    kt_bufs = [self.KTb[0]]
                    v_h = lambda h, s=s: self.Vb[0:DSEQ, s, h * HD:(h + 1) * HD]
                    v_bufs = [self.Vbb[s]]
                else:
                    rows = 128
                    kf, kfb = self.rowbufs.next()
                    P.dma("sp", lambda e, kf=kf, s=s, ci=ci: e.dma_start(out=kf[:], in_=d["ck"][s, ci * 128:(ci + 1) * 128, :]), writes=[kfb])
                    vf, vfb = self.rowbufs.next()
                    P.dma("sp", lambda e, vf=vf, s=s, ci=ci: e.dma_start(out=vf[:], in_=d["cv"][s, ci * 128:(ci + 1) * 128, :]), writes=[vfb])
                    kb, kbl = self.hb(4 + (hbi % 8))
                    hbi += 1
                    P.op("dve", lambda e, kb=kb, kf=kf: e.tensor_copy(out=kb, in_=kf[:]), reads=[kfb], writes=kbl)
                    tb_t, tb_b = self.bankb
                    for h in range(NH):
                        P.op("pe", lambda e, h=h, kb=kb, tb_t=tb_t: e.transpose(out=tb_t[:, h * 128:(h + 1) * 128], in_=kb[:, h * HD:(h + 1) * HD], identity=self.ident_b[:]),
                             reads=list(kbl) + [self.ident_bb], writes=[tb_b])
                    kc_r, kc_bl = self.hb(4 + (hbi % 8))
                    hbi += 1
                    kc_t = kc_r.rearrange("p (h t) -> p h t", h=NH)
                    P.op("dve", lambda e, kc_r=kc_r, tb_t=tb_t: e.tensor_copy(out=kc_r, in_=tb_t[:, :]),
                         reads=[tb_b], writes=kc_bl)
                    vb, vbl = self.hb(4 + (hbi % 8))
                    hbi += 1
                    P.op("pool", lambda e, vb=vb, vf=vf: e.tensor_copy(out=vb, in_=vf[:]), reads=[vfb], writes=vbl)
                    kt_h = lambda h, kc_t=kc_t: kc_t[:, h, :]
                    kt_bufs = list(kc_bl)
                    v_h = lambda h, vb=vb: vb[:, h * HD:(h + 1) * HD]
                    v_bufs = list(vbl)
                zt, zb = self.bank(ci % 3)
                for h in range(NH):
                    P.op("pe", lambda e, h=h, zt=zt, kt_h=kt_h, rows=rows, qcol=qcol: e.matmul(zt[0:rows, h * DSEQ:(h + 1) * DSEQ], lhsT=kt_h(h), rhs=self.bufA[:, h, qcol:qcol + DSEQ],
                                                                                    start=(h == 0), stop=(h == NH - 1), skip_group_check=True),
                         reads=kt_bufs + [self.bufAb[h]], writes=[zb])
                et, etb = self.tf.next()
                sp, spb = self.tr.next()
                P.op("act", lambda e, et=et, zt=zt, rows=rows: e.activation(out=et[0:rows, 0:NQ], in_=zt[0:rows, 0:NQ], func=AF.Exp), reads=[zb], writes=[etb])
                P.op("act", lambda e, et=et, sp=sp, rows=rows: e.activation(out=sp[0:rows, 0:NQ], in_=et[0:rows, 0:NQ], func=AF.Ln, bias=1.0), reads=[etb], writes=[spb])
                if new:
                    P.op("dve", lambda e, sp=sp: e.tensor_tensor(out=sp[0:DSEQ, 0:NQ], in0=sp[0:DSEQ, 0:NQ], in1=self.M01s[0:DSEQ, :], op=ALU.mult),
                         reads=[spb, self.M01sb], writes=[spb])
                    P.op("pe", lambda e, zt=zt: e.matmul(zt[0:DSEQ, 0:NQ], lhsT=self.ident_b[0:DSEQ, 0:DSEQ], rhs=self.MBs[0:DSEQ, :], start=False, stop=False,
                                                         skip_group_check=True),
                         reads=[self.ident_bb, self.MBsb], writes=[zb])
                P.op("pe", lambda e, zt=zt, sp=sp, rows=rows, first=first: e.matmul(zt[0:rows, 0:NQ], lhsT=self.negtri[0:rows, 0:rows], rhs=sp[0:rows, 0:NQ], start=False, stop=first,
                                                                                    skip_group_check=True),
                     reads=[self.negtrib, spb], writes=[zb])
                if not first:
                    P.op("pe", lambda e, zt=zt: e.matmul(zt[:, 0:NQ], lhsT=self.negones[:], rhs=self.RS[:, 0:NQ], start=False, stop=True, skip_group_check=True),
                         reads=[self.negonesb, self.RSb], writes=[zb])
                wt, wtb = self.tb.next()
                P.op("act", lambda e, wt=wt, zt=zt, rows=rows: e.activation(out=wt[0:rows, 0:NQ], in_=zt[0:rows, 0:NQ], func=AF.Exp), reads=[zb], writes=[wtb])
                if first:
                    self.zero_fill_r(self.RS[:, 0:NQ], [self.RSb])
                    P.op("pool", lambda e, sp=sp: e.tensor_copy(out=self.RS[0:DSEQ, 0:NQ], in_=sp[0:DSEQ, 0:NQ]), reads=[spb], writes=[self.RSb])
                elif ci > 0:
                    P.op("dve", lambda e, sp=sp: e.tensor_tensor(out=self.RS[:, 0:NQ], in0=self.RS[:, 0:NQ], in1=sp[:, 0:NQ], op=ALU.add),
                         reads=[spb, self.RSb], writes=[self.RSb])
                for h in range(NH):
                    P.op("pe", lambda e, h=h, ot=ot, wt=wt, v_h=v_h, rows=rows, first=first: e.matmul(ot[:, h * DSEQ:(h + 1) * DSEQ], lhsT=v_h(h), rhs=wt[0:rows, h * DSEQ:(h + 1) * DSEQ],
                                                                                                      start=(first and h == 0), stop=(ci == 0 and h == NH - 1), skip_group_check=True),
                         reads=v_bufs + [wtb], writes=[otb])
                first = False
            P.op("act", lambda e, ot=ot, qcol=qcol: e.copy(out=self.hmid[:, 0:NH, qcol:qcol + DSEQ], in_=ot[:, 0:NQ].rearrange("p (h q) -> p h q", h=NH)),
                 reads=[otb], writes=[self.hmidb[h] for h in range(NH)])

    def final_out(self, c):
        P = self.P
        N = c.N
        self.rmsnorm_to_xn(c, 5, inplace=True)
        for gi, (c0, nt, _src) in enumerate(c.tgs):
            ysb, ysbb = self.rowbufs.next()
            for half in range(2):
                bt, bb = self.bank(half)
                for q in range(4):
                    fc = half * 4 + q
                    P.op("pe", lambda e, q=q, fc=fc, bt=bt, c0=c0, nt=nt: e.transpose(out=bt[0:nt, q * 128:(q + 1) * 128], in_=self.x[:, fc, c0:c0 + nt], identity=self.ident_f[:]),
                         reads=[self.xb[fc], self.ident_fb], writes=[bb])
                if half == 0:
                    P.op("act", lambda e, bt=bt, ysb=ysb, nt=nt: e.copy(out=ysb[0:nt, 0:512], in_=bt[0:nt, :]), reads=[bb], writes=[ysbb])
                else:
                    P.op("dve", lambda e, bt=bt, ysb=ysb, nt=nt: e.tensor_copy(out=ysb[0:nt, 512:1024], in_=bt[0:nt, :]), reads=[bb], writes=[ysbb])
            dst = c.y_dst[gi]
            P.dma("sp", lambda e, ysb=ysb, dst=dst, nt=nt: e.dma_start(out=dst, in_=ysb[0:nt, :]), reads=[ysbb])

    def init_states_zero(self):
        P = self.P
        P.op("pool", lambda e: e.memset(self.rhist[:], 0.0), writes=self.rhistb)
        P.op("pool", lambda e: e.memset(self.hstate[:], 0.0), writes=self.hstateb)
        for l in range(2):
            t, b = self.fhist[l]
            P.op("pool", lambda e, t=t: e.memset(t[:], 0.0), writes=b)

    def init_states_sample(self):
        P, d = self.P, self.d
        for s in range(SB_):
            for k in range(3):
                P.dma("sp", lambda e, s=s, k=k: e.dma_start(out=self.rhist[:, :, s, k], in_=d["s_conv"][s, k].rearrange("(j p) -> p j", p=128),
                                                            allow_slow_non_contiguous=True), writes=self.rhistb)
            P.dma("sp", lambda e, s=s: e.dma_start(out=self.hstate[:, :, s], in_=d["s_h"][s].rearrange("(j p) -> p j", p=128),
                                                   allow_slow_non_contiguous=True), writes=self.hstateb)
            for l in range(2):
                t, b = self.fhist[l]
                for k in range(2):
                    P.dma("sp", lambda e, s=s, l=l, t=t, k=k: e.dma_start(out=t[:, :, s, k], in_=d["s_fconv"][l, s, k].rearrange("(j p) -> p j", p=128),
                                                                          allow_slow_non_contiguous=True), writes=b)

    def write_states(self, seg, dst_h, dst_conv, dst_fconv):
        P = self.P
        bt, bb = self.bank(0)
        stg, stgb = self.rowbufs.next()
        P.op("pe", lambda e, bt=bt: e.transpose(out=bt[0:8, 0:128], in_=self.hstate[:, :, seg], identity=self.ident_f[:]),
             reads=self.hstateb + [self.ident_fb], writes=[bb])
        P.op("act", lambda e, bt=bt, stg=stg: e.copy(out=stg[0:8, 0:128], in_=bt[0:8, 0:128]), reads=[bb], writes=[stgb])
        P.dma("sp", lambda e, stg=stg: e.dma_start(out=dst_h.rearrange("(j p) -> j p", p=128), in_=stg[0:8, 0:128]), reads=[stgb])
        stg, stgb = self.rowbufs.next()
        for half in range(2):
            bt, bb = self.bank(1 + half)
            for q in range(4):
                j = half * 4 + q
                P.op("pe", lambda e, j=j, q=q, bt=bt: e.transpose(out=bt[0:3, q * 128:(q + 1) * 128], in_=self.rhist[:, j, seg, :], identity=self.ident_f[:]),
                     reads=[self.rhistb[j], self.ident_fb], writes=[bb])
            P.op("act", lambda e, half=half, bt=bt, stg=stg: e.copy(out=stg[0:3, half * 512:(half + 1) * 512], in_=bt[0:3, :]), reads=[bb], writes=[stgb])
        P.dma("sp", lambda e, stg=stg: e.dma_start(out=dst_conv, in_=stg[0:3, :]), reads=[stgb])
        for l in range(2):
            t, b = self.fhist[l]
            for part in range(3):
                stg, stgb = self.rowbufs.next()
                for half in range(2):
                    bt, bb = self.bank(3 + half)
                    for q in range(4):
                        j = part * 8 + half * 4 + q
                        P.op("pe", lambda e, j=j, q=q, bt=bt, t=t: e.transpose(out=bt[0:2, q * 128:(q + 1) * 128], in_=t[:, j, seg, :], identity=self.ident_f[:]),
                             reads=[b[j], self.ident_fb], writes=[bb])
                    P.op("act", lambda e, half=half, bt=bt, stg=stg: e.copy(out=stg[0:2, half * 512:(half + 1) * 512], in_=bt[0:2, :]), reads=[bb], writes=[stgb])
                P.dma("sp", lambda e, stg=stg, l=l, part=part: e.dma_start(out=dst_fconv[l][:, part * 1024:(part + 1) * 1024], in_=stg[0:2, :]), reads=[stgb])

    def dbg(self, c, name, ap, bufs, shape, dt=F32):
        if not getattr(c, "dbg", False):
            return
        o = self.nc.dram_tensor("dbg_" + name, list(shape), dt, kind="ExternalOutput").ap()
        self.P.dma("sp", lambda e: e.dma_start(out=o, in_=ap), reads=bufs)

    def do_casts(self, stage):
        for (dst, src, name, sem) in self.cast_plan.pop(stage, []):
            self._cast(dst, src, name, sem)

    def run_tile(self, c, next_c=None):
        N = c.N
        if getattr(c, "dbg", False):
            o = self.nc.dram_tensor("dbg_win", [1024, 2048], BF16, kind="ExternalOutput").ap()
            self.P.dma("sp", lambda e: e.dma_start(out=o, in_=self.d["w_in_b"]), reads=[self.wsb["w_in"]])
        self.x_to_fm(c)
        self.dbg(c, "x0", self.x[:, :, 0:N], self.xb, [128, 8, N])
        self.rmsnorm_to_xn(c, 0)
        self.dbg(c, "xn0", self.xn[:, :, 0:N], self.xnb, [128, 8, N], BF16)
        self.do_casts("lru")
        self.lru_mixer(c)
        self.dbg(c, "hg", self.bufA[:, :, 0:N], self.bufAb, [128, 8, N], BF16)
        self.dbg(c, "x1", self.x[:, :, 0:N], self.xb, [128, 8, N])
        self.do_casts("ffn0")
        self.conv_ffn(c, 0)
        self.dbg(c, "x2", self.x[:, :, 0:N], self.xb, [128, 8, N])
        self.do_casts("kv")
        self.shared_kv(c)
        self.do_casts("q")
        self.q_proj(c)
        if c.sample:
            self.attn_sample(c)
        else:
            self.attn_prompt(c)
        if next_c is not None:
            c0n, ntn, srcn = next_c.tgs[0]
            self.P.dma("sp", lambda e: e.dma_start(out=self.xpre[0:ntn, :], in_=srcn), writes=[self.xpreb])
            next_c.prefetched = True
        self.proj_residual(c, self.hmid, self.hmidb)
        self.conv_ffn(c, 1)
        self.final_out(c)


def make_prompt_ctx(B, b, qt):
    d = B.d
    c = TileCtx()
    c.sample = False
    c.N, c.nseg, c.L = TN, 1, TN
    c.qt = qt
    t0 = qt * TN
    c.tgs = [(g * 128, 128, d["xp"][b, t0 + g * 128:t0 + (g + 1) * 128, :]) for g in range(4)]
    c.kv_dst = []
    c.vb_dst = []
    c.y_dst = []
    for g in range(4):
        ch = qt * 4 + g
        rows = slice(t0 + g * 128, t0 + (g + 1) * 128)
        c.kv_dst.append((d["p_k"][b, rows, :], d["p_v"][b, rows, :], (B.KT[:, :, ch * 128:(ch + 1) * 128], [B.KTb[ch]])))
        c.vb_dst.append((B.Vb[:, ch, :], [B.Vbb[ch]]))
        c.y_dst.append(d["y_p"][b, rows, :])
    return c


def make_sample_ctx(B):
    d = B.d
    c = TileCtx()
    c.sample = True
    c.N, c.nseg, c.L = SB_ * DSEQ, SB_, DSEQ
    c.qt = 0
    c.tgs = []
    c.kv_dst = []
    c.vb_dst = []
    c.y_dst = []
    for s in range(SB_):
        rows = slice(s * DSEQ, (s + 1) * DSEQ)
        c.tgs.append((s * DSEQ, DSEQ, d["xs"][rows, :]))
        c.kv_dst.append((d["o_k"][rows, :], d["o_v"][rows, :], (B.KT[:, :, s * DSEQ:(s + 1) * DSEQ], [B.KTb[0]])))
        c.vb_dst.append((B.Vb[0:DSEQ, s, :], [B.Vbb[s]]))
        c.y_dst.append(d["y_s"][rows, :])
    return c


def build_nc():
    nc = bass.Bass("TRN2", target_bir_lowering=False)
    B = Builder(nc)
    B.declare()
    with B.st:
        B.setup()
        B.plan_weights(PB * (SEQ // TN) + 1)
        P, d = B.P, B.d
        ctxs = []
        for b in range(PB):
            for qt in range(SEQ // TN):
                c = make_prompt_ctx(B, b, qt)
                c.dbg = DEBUG and b == 0 and qt == 0
                ctxs.append((b, qt, c))
        cs = make_sample_ctx(B)
        for i, (b, qt, c) in enumerate(ctxs):
            if qt == 0:
                B.init_states_zero()
            nxt = ctxs[i + 1][2] if i + 1 < len(ctxs) else cs
            B.run_tile(c, nxt)
            if qt == SEQ // TN - 1:
                B.write_states(0, d["p_h"][b], d["p_conv"][b], [d["p_fconv"][l, b] for l in range(2)])
        B.init_states_sample()
        B.run_tile(cs, None)
        for s_ in range(SB_):
            B.write_states(s_, d["o_h"][s_], d["o_conv"][s_], [d["o_fconv"][l, s_] for l in range(2)])
        P.emit()
    return nc


_NC_CACHE = {}


def kernel(**inputs):
    f32 = lambda a: np.ascontiguousarray(np.asarray(a, dtype=np.float32))
    x_prompt = f32(inputs["x_prompt"])
    x_sample = f32(inputs["x_sample"])
    state_lru_h = f32(inputs["state_lru_h"])
    state_lru_conv = f32(inputs["state_lru_conv"])
    state_ffn_conv = f32(inputs["state_ffn_conv"])
    cache_k = f32(inputs["cache_k"])
    cache_v = f32(inputs["cache_v"])
    shared = {
        "a_norm": f32(inputs["a_norm"])[0], "a_w_in": f32(inputs["a_w_in"])[0], "a_conv_w": f32(inputs["a_conv_w"])[0],
        "a_conv_b": f32(inputs["a_conv_b"])[0], "a_wr": f32(inputs["a_wr"])[0], "a_br": f32(inputs["a_br"])[0],
        "a_wi": f32(inputs["a_wi"])[0], "a_bi": f32(inputs["a_bi"])[0], "a_lambda": f32(inputs["a_lambda"])[0],
        "a_w_out": f32(inputs["a_w_out"])[0], "kv_norm": f32(inputs["kv_norm"]), "w_kv": f32(inputs["w_kv"]),
        "k_norm": f32(inputs["k_norm"]), "b_norm": f32(inputs["b_norm"])[0], "b_wq": f32(inputs["b_wq"])[0],
        "q_norm": f32(inputs["q_norm"])[0], "b_wo": f32(inputs["b_wo"])[0], "f_norm": f32(inputs["f_norm"]),
        "f_w_up": f32(inputs["f_w_up"]), "f_conv_w": f32(inputs["f_conv_w"]), "f_conv_b": f32(inputs["f_conv_b"]),
        "f_w_down": f32(inputs["f_w_down"]), "out_norm": f32(inputs["out_norm"]),
    }
    shared = {k: np.ascontiguousarray(v) for k, v in shared.items()}
    in_maps = []
    for cidx in range(NCORES):
        ps = slice(cidx * PB, (cidx + 1) * PB)
        ss = slice(cidx * SB_, (cidx + 1) * SB_)
        m = dict(shared)
        m["xp"] = np.ascontiguousarray(x_prompt[ps])
        m["xs"] = np.ascontiguousarray(x_sample[ss].reshape(SB_ * DSEQ, D))
        m["s_h"] = np.ascontiguousarray(state_lru_h[0, ss])
        m["s_conv"] = np.ascontiguousarray(state_lru_conv[0, ss])
        m["s_fconv"] = np.ascontiguousarray(state_ffn_conv[:, ss])
        m["ck"] = np.ascontiguousarray(cache_k[ss].reshape(SB_, PAST, D))
        m["cv"] = np.ascontiguousarray(cache_v[ss].reshape(SB_, PAST, D))
        in_maps.append(m)
    if "nc" not in _NC_CACHE:
        _NC_CACHE["nc"] = build_nc()
    nc = _NC_CACHE["nc"]
    res = run_bass_kernel_spmd(nc, in_maps, core_ids=list(range(NCORES)))
    R = res.results
    if DEBUG:
        _NC_CACHE["dbg"] = {k: np.asarray(v) for k, v in R[0].items() if k.startswith("dbg_")}
    cat = lambda k, ax=0: np.concatenate([np.asarray(r[k], dtype=np.float32) for r in R], axis=ax)
    B_ = NCORES * PB
    DB = NCORES * SB_
    y_prompt = cat("y_p")
    y_sample = cat("y_s").reshape(DB, DSEQ, D)
    p_lru_h = cat("p_h")[None]
    p_lru_conv = cat("p_conv")[None]
    p_ffn_conv = cat("p_fconv", 1)
    p_k = cat("p_k").reshape(B_, SEQ, NH, HD)
    p_v = cat("p_v").reshape(B_, SEQ, NH, HD)
    s_lru_h = cat("o_h")[None]
    s_lru_conv = cat("o_conv")[None]
    s_ffn_conv = cat("o_fconv", 1)
    s_k = cat("o_k").reshape(DB, DSEQ, NH, HD)
    s_v = cat("o_v").reshape(DB, DSEQ, NH, HD)
    return (y_prompt, y_sample, p_lru_h, p_lru_conv, p_ffn_conv, p_k, p_v,
            s_lru_h, s_lru_conv, s_ffn_conv, s_k, s_v)
```
